# Optimizing a Trainium2 kernel written in Bass

```python
import jax, jax.numpy as jnp
from jax import lax
import numpy as np

D_MODEL = 4096
BATCH = 4
SEQ = 4096
DEPTH = 1

CONV_W = 4
LRU_WIDTH = D_MODEL // 2
LRU_BLOCKS = 16
LRU_BLOCK = LRU_WIDTH // LRU_BLOCKS
LRU_C = 8.0
GDN_DK = 128
GDN_DV = 128
GDN_HEADS = (D_MODEL // 2) // GDN_DK
GDN_KEY = GDN_HEADS * GDN_DK
GDN_VAL = GDN_HEADS * GDN_DV
GDN_CONV_DIM = 2 * GDN_KEY + GDN_VAL
CHUNK = 64
D_FF = 4 * D_MODEL
EPS = 1e-6
IN_SIZES = (LRU_WIDTH, LRU_WIDTH, GDN_CONV_DIM, GDN_VAL, GDN_HEADS, GDN_HEADS, D_MODEL, D_MODEL)
IN_WIDTH = sum(IN_SIZES)

kernel_name = "hybrid_rglru_gdn_adaln_block"


def rms_norm(x, w):
    xf = x.astype(jnp.float32)
    y = xf * lax.rsqrt(jnp.mean(xf * xf, axis=-1, keepdims=True) + EPS)
    return (y * w.astype(jnp.float32)).astype(x.dtype)


def l2_normalize(x):
    xf = x.astype(jnp.float32)
    return xf * lax.rsqrt(jnp.sum(xf * xf, axis=-1, keepdims=True) + EPS)


def causal_depthwise_conv(x, w):
    C = x.shape[-1]
    return lax.conv_general_dilated(
        x, w[:, None, :].astype(x.dtype), window_strides=(1,), padding=[(CONV_W - 1, 0)],
        dimension_numbers=("NWC", "WIO", "NWC"), feature_group_count=C)


def rg_lru(x, w_a, b_a, w_i, b_i, lam):
    B, T, _ = x.shape
    xb = x.reshape(B, T, LRU_BLOCKS, LRU_BLOCK)
    r = jax.nn.sigmoid(jnp.einsum('btni,nij->btnj', xb, w_a) + b_a).reshape(B, T, LRU_WIDTH)
    i = jax.nn.sigmoid(jnp.einsum('btni,nij->btnj', xb, w_i) + b_i).reshape(B, T, LRU_WIDTH)
    log_a = -LRU_C * r.astype(jnp.float32) * jax.nn.softplus(-lam.astype(jnp.float32))
    a = jnp.exp(log_a)
    u = jnp.sqrt(-jnp.expm1(2.0 * log_a)) * (i * x).astype(jnp.float32)

    def combine(left, right):
        a1, b1 = left
        a2, b2 = right
        return a1 * a2, a2 * b1 + b2

    _, h = lax.associative_scan(combine, (a, u), axis=1)
    return h.astype(x.dtype)


def chunked_gated_delta_rule(q, k, v, g, beta):
    B, T, H, DK = q.shape
    DV = v.shape[-1]
    N = T // CHUNK
    q = l2_normalize(q) * (DK ** -0.5)
    k = l2_normalize(k)
    v = v.astype(jnp.float32)
    chunk = lambda t: t.reshape(B, N, CHUNK, H, t.shape[-1]).transpose(0, 3, 1, 2, 4)
    q, k, v = chunk(q), chunk(k), chunk(v)
    g = g.reshape(B, N, CHUNK, H).transpose(0, 3, 1, 2)
    beta = beta.reshape(B, N, CHUNK, H).transpose(0, 3, 1, 2)
    g = jnp.cumsum(g, axis=-1)
    causal = jnp.tril(jnp.ones((CHUNK, CHUNK), dtype=bool))
    strict = jnp.tril(jnp.ones((CHUNK, CHUNK), dtype=bool), -1)
    decay = jnp.exp(jnp.where(causal, g[..., :, None] - g[..., None, :], -jnp.inf))
    k_beta = k * beta[..., None]
    v_beta = v * beta[..., None]
    lower = jnp.where(strict, jnp.einsum('bhncd,bhnsd->bhncs', k_beta, k) * decay, 0.0)
    eye = jnp.eye(CHUNK, dtype=jnp.float32)
    rhs = jnp.concatenate([v_beta, k_beta * jnp.exp(g)[..., None]], axis=-1)
    sol = lax.linalg.triangular_solve(eye + lower, rhs, left_side=True, lower=True, unit_diagonal=True)
    u, w = sol[..., :DV], sol[..., DV:]
    attn = jnp.einsum('bhncd,bhnsd->bhncs', q, k) * decay
    lead = lambda t: jnp.moveaxis(t, 2, 0)
    xs = (lead(q), lead(k), lead(u), lead(w), lead(g), lead(attn))

    def step(S, inp):
        q_c, k_c, u_c, w_c, g_c, attn_c = inp
        v_new = u_c - jnp.einsum('bhck,bhkv->bhcv', w_c, S)
        o = (jnp.einsum('bhck,bhkv->bhcv', q_c * jnp.exp(g_c)[..., None], S)
             + jnp.einsum('bhcs,bhsv->bhcv', attn_c, v_new))
        g_last = g_c[..., -1]
        S = (S * jnp.exp(g_last)[..., None, None]
             + jnp.einsum('bhck,bhcv->bhkv', k_c * jnp.exp(g_last[..., None] - g_c)[..., None], v_new))
        return S, o

    S0 = jnp.zeros((B, H, DK, DV), jnp.float32)
    _, o = lax.scan(step, S0, xs)
    return o.transpose(1, 0, 3, 2, 4).reshape(B, T, H, DV)


def hybrid_mixer(h, w_in, lru_conv_w, lru_conv_b, lru_gate_a_w, lru_gate_a_b, lru_gate_i_w,
                 lru_gate_i_b, lru_lambda, gdn_conv_w, gdn_a_log, gdn_dt_bias, gdn_out_norm,
                 w_branch_lru, w_branch_gdn, w_out):
    B, T, _ = h.shape
    points, acc = [], 0
    for s in IN_SIZES[:-1]:
        acc += s
        points.append(acc)
    proj = h @ w_in
    lru_x, lru_gate, qkv, z, a_in, b_in, gate_lru, gate_gdn = jnp.split(proj, points, axis=-1)
    xa = causal_depthwise_conv(lru_x, lru_conv_w) + lru_conv_b
    ya = rg_lru(xa, lru_gate_a_w, lru_gate_a_b, lru_gate_i_w, lru_gate_i_b, lru_lambda) * jax.nn.gelu(lru_gate)
    qkv = jax.nn.silu(causal_depthwise_conv(qkv, gdn_conv_w))
    q, k, v = jnp.split(qkv, [GDN_KEY, 2 * GDN_KEY], axis=-1)
    q = q.reshape(B, T, GDN_HEADS, GDN_DK)
    k = k.reshape(B, T, GDN_HEADS, GDN_DK)
    v = v.reshape(B, T, GDN_HEADS, GDN_DV)
    beta = jax.nn.sigmoid(b_in.astype(jnp.float32))
    g = -jnp.exp(gdn_a_log.astype(jnp.float32)) * jax.nn.softplus(
        a_in.astype(jnp.float32) + gdn_dt_bias.astype(jnp.float32))
    o = chunked_gated_delta_rule(q, k, v, g, beta)
    o = rms_norm(o, gdn_out_norm) * jax.nn.silu(z.reshape(B, T, GDN_HEADS, GDN_DV).astype(jnp.float32))
    yb = o.reshape(B, T, GDN_VAL).astype(h.dtype)
    merged = (jax.nn.sigmoid(gate_lru) * (ya @ w_branch_lru)
              + jax.nn.sigmoid(gate_gdn) * (yb @ w_branch_gdn))
    return merged @ w_out


def setup_inputs(seed: int = 0) -> dict:
    key = jax.random.key(seed)
    ks = jax.random.split(key, 26)
    f32 = jnp.float32
    L = DEPTH

    def normal(k, shape, scale):
        return jax.random.normal(k, shape, f32) * scale

    a_pow = jax.random.uniform(ks[13], (L, LRU_WIDTH), f32, 0.9, 0.999)
    s = a_pow ** (1.0 / LRU_C)
    return {
        "x": normal(ks[0], (BATCH, SEQ, D_MODEL), 1.0),
        "c": normal(ks[1], (BATCH, D_MODEL), 1.0),
        "w_ada": normal(ks[2], (L, D_MODEL, 6 * D_MODEL), 0.5 * D_MODEL ** -0.5),
        "b_ada": normal(ks[3], (L, 6 * D_MODEL), 0.01),
        "mix_pre_norm": 1.0 + normal(ks[4], (L, D_MODEL), 0.02),
        "mix_post_norm": 1.0 + normal(ks[5], (L, D_MODEL), 0.02),
        "w_in": normal(ks[6], (L, D_MODEL, IN_WIDTH), D_MODEL ** -0.5),
        "lru_conv_w": normal(ks[7], (L, CONV_W, LRU_WIDTH), CONV_W ** -0.5),
        "lru_conv_b": normal(ks[8], (L, LRU_WIDTH), 0.01),
        "lru_gate_a_w": normal(ks[9], (L, LRU_BLOCKS, LRU_BLOCK, LRU_BLOCK), LRU_BLOCK ** -0.5),
        "lru_gate_a_b": normal(ks[10], (L, LRU_BLOCKS, LRU_BLOCK), 0.01),
        "lru_gate_i_w": normal(ks[11], (L, LRU_BLOCKS, LRU_BLOCK, LRU_BLOCK), LRU_BLOCK ** -0.5),
        "lru_gate_i_b": normal(ks[12], (L, LRU_BLOCKS, LRU_BLOCK), 0.01),
        "lru_lambda": jnp.log(s) - jnp.log1p(-s),
        "gdn_conv_w": normal(ks[14], (L, CONV_W, GDN_CONV_DIM), CONV_W ** -0.5),
        "gdn_a_log": jnp.log(jax.random.uniform(ks[15], (L, GDN_HEADS), f32, 1.0, 16.0)),
        "gdn_dt_bias": 1.0 + normal(ks[16], (L, GDN_HEADS), 0.1),
        "gdn_out_norm": 1.0 + normal(ks[17], (L, GDN_DV), 0.02),
        "w_branch_lru": normal(ks[18], (L, LRU_WIDTH, D_MODEL), LRU_WIDTH ** -0.5),
        "w_branch_gdn": normal(ks[19], (L, GDN_VAL, D_MODEL), GDN_VAL ** -0.5),
        "w_out": normal(ks[20], (L, D_MODEL, D_MODEL), D_MODEL ** -0.5),
        "mlp_pre_norm": 1.0 + normal(ks[21], (L, D_MODEL), 0.02),
        "mlp_post_norm": 1.0 + normal(ks[22], (L, D_MODEL), 0.02),
        "w_mlp_up": normal(ks[23], (L, D_MODEL, D_FF), D_MODEL ** -0.5),
        "w_mlp_down": normal(ks[24], (L, D_FF, D_MODEL), D_FF ** -0.5),
    }


def reference(x, c, w_ada, b_ada, mix_pre_norm, mix_post_norm, w_in, lru_conv_w, lru_conv_b,
              lru_gate_a_w, lru_gate_a_b, lru_gate_i_w, lru_gate_i_b, lru_lambda, gdn_conv_w,
              gdn_a_log, gdn_dt_bias, gdn_out_norm, w_branch_lru, w_branch_gdn, w_out,
              mlp_pre_norm, mlp_post_norm, w_mlp_up, w_mlp_down):
    c_act = jax.nn.silu(c)
    for l in range(DEPTH):
        mod = c_act @ w_ada[l] + b_ada[l]
        shift1, scale1, gate1, shift2, scale2, gate2 = [m[:, None, :] for m in jnp.split(mod, 6, axis=-1)]
        h = rms_norm(x, mix_pre_norm[l]) * (1.0 + scale1) + shift1
        y = hybrid_mixer(h, w_in[l], lru_conv_w[l], lru_conv_b[l], lru_gate_a_w[l], lru_gate_a_b[l],
                         lru_gate_i_w[l], lru_gate_i_b[l], lru_lambda[l], gdn_conv_w[l], gdn_a_log[l],
                         gdn_dt_bias[l], gdn_out_norm[l], w_branch_lru[l], w_branch_gdn[l], w_out[l])
        x = x + gate1 * rms_norm(y, mix_post_norm[l])
        h = rms_norm(x, mlp_pre_norm[l]) * (1.0 + scale2) + shift2
        y = jnp.square(jax.nn.relu(h @ w_mlp_up[l])) @ w_mlp_down[l]
        x = x + gate2 * rms_norm(y, mlp_post_norm[l])
    return x
```

```python
import numpy as np
from contextlib import ExitStack, contextmanager
import concourse.bass as bass
import concourse.mybir as mybir
from concourse.bass_utils import run_bass_kernel_spmd

F32 = mybir.dt.float32
BF16 = mybir.dt.bfloat16
AF = mybir.ActivationFunctionType
ALU = mybir.AluOpType
EPS = 1e-6
SEM_LIMIT = 30000
import os
GSTOP = int(os.environ.get('GSTOP', '9'))
DUALQ = int(os.environ.get('DUALQ', '1'))
G2S = int(os.environ.get('G2S', '9'))
NDSEM = 40


class Cfg:
    def __init__(s, D, T, NT, SBK, PL, debug=False, NTW=None):
        s.D = D; s.T = T; s.NT = NT; s.SBK = SBK; s.PL = PL; s.debug = debug; s.NTW = NTW or NT
        s.LW = D // 2; s.LB = s.LW // 128; s.H = (D // 2) // 128; s.KEY = s.H * 128; s.VAL = s.H * 128
        s.DFF = 4 * D; s.TOK = T // 2; s.KC = D // 128
        s.o_lx = 0; s.o_lg = s.LW; s.o_q = 2 * s.LW; s.o_k = s.o_q + s.KEY; s.o_v = s.o_k + s.KEY
        s.o_z = s.o_v + s.VAL; s.o_a = s.o_z + s.VAL; s.o_b = s.o_a + s.H; s.o_gl = s.o_b + s.H; s.o_gg = s.o_gl + D
        s.INW = s.o_gg + D


class Buf:
    __slots__ = ("name", "lw", "rd", "excl")

    def __init__(s, name, excl=False):
        s.name = name; s.lw = None; s.rd = {}; s.excl = excl


def PB(name):
    return Buf(name, True)


class KB:
    def __init__(s, cfg):
        s.cfg = cfg
        s.nc = bass.Bass("TRN2", target_bir_lowering=False)
        nc = s.nc
        s.E = {"pe": nc.tensor, "act": nc.scalar, "dve": nc.vector, "pool": nc.gpsimd, "sp": nc.sync}
        s.root = ExitStack()
        s.es = s.root
        s.semh = {}
        s.sem = {}; s.cnt = {}; s.seen = {e: {} for e in s.E}
        s.nsem = 0
        for e in ("pe", "act", "dve", "pool"):
            s.sem[e] = s._newsem(); s.cnt[e] = 0
        s.dsem = [s._newsem() for _ in range(NDSEM)]
        s.dval = [0] * NDSEM
        s.dnext = 0
        s.uid = 0
        s.block = s.root.enter_context(nc.Block())

    def _newsem(s):
        s.nsem += 1
        name = "s%d" % s.nsem
        h = s.root.enter_context(s.nc.semaphore(name))
        s.semh[name] = h
        return name

    def sb(s, shape, dt, name=None):
        s.uid += 1
        t = s.es.enter_context(s.nc.sbuf_tensor("%s_%d" % (name or "t", s.uid), list(shape), dt))
        return t

    def ps(s, shape, dt=F32, name=None):
        s.uid += 1
        return s.es.enter_context(s.nc.psum_tensor("%s_%d" % (name or "p", s.uid), list(shape), dt))

    def dram(s, name, shape, dt, kind=None):
        k = kind or ("ExternalOutput" if s.cfg.debug else "Internal")
        return s.nc.dram_tensor(name, list(shape), dt, kind=k).ap()

    @contextmanager
    def phase(s):
        s.barrier()
        old = s.es
        es = ExitStack()
        s.es = es
        try:
            yield
        finally:
            s.barrier()
            es.close()
            s.es = old

    @contextmanager
    def scope(s):
        old = s.es
        es = ExitStack()
        s.es = es
        try:
            yield
        finally:
            s.barrier()
            es.close()
            s.es = old

    def _need(s, eng, reads, writes):
        toks = []
        for b in reads:
            if b.lw:
                toks.append(b.lw)
        for b in writes:
            if b.lw and not (eng == "pe" and b.lw[2] == "pe"):
                toks.append(b.lw)
            for k, v in b.rd.items():
                toks.append((k, v, None))
        return toks

    def _wait(s, eng, toks):
        mx = {}
        for t in toks:
            if t[1] > mx.get(t[0], 0):
                mx[t[0]] = t[1]
        for k, v in mx.items():
            if s.seen[eng].get(k, 0) < v:
                s.E[eng].wait_ge(s.semh[k], v)
                s.seen[eng][k] = v

    def op(s, eng, fn, r=(), w=()):
        w = list(w)
        for b in r:
            if b.excl and b not in w:
                w.append(b)
        s._wait(eng, s._need(eng, r, w))
        if s.cnt[eng] >= SEM_LIMIT:
            s.sem[eng] = s._newsem(); s.cnt[eng] = 0
        ins = fn(s.E[eng])
        s.cnt[eng] += 1
        k = s.sem[eng]
        ins.then_inc(s.semh[k], 1)
        tok = (k, s.cnt[eng], eng)
        for b in r:
            if b.rd.get(k, 0) < tok[1]:
                b.rd[k] = tok[1]
        for b in w:
            b.lw = tok; b.rd = {}
        return ins

    def dma(s, q, out, in_, r=(), w=()):
        i = s.dnext
        s.dnext = (i + 1) % NDSEM
        k = s.dsem[i]; pv = s.dval[i]
        toks = s._need("dma", r, w)
        if pv:
            toks.append((k, pv, None))
        s._wait(q, toks)
        s.E[q].dma_start(out=out, in_=in_).then_inc(s.semh[k], 16)
        s.dval[i] = pv + 16
        tok = (k, pv + 16, "dma")
        for b in r:
            b.rd[k] = tok[1]
        for b in w:
            b.lw = tok; b.rd = {}

    def barrier(s):
        toks = [(s.sem[e], s.cnt[e], None) for e in s.cnt if s.cnt[e] > 0]
        toks += [(s.dsem[i], s.dval[i], None) for i in range(NDSEM) if s.dval[i] > 0]
        for e in s.E:
            s._wait(e, toks)


_DBG = {}


def build(cfg):
    kb = KB(cfg)
    _DBG['kb'] = kb
    nc = kb.nc
    c = cfg
    D, KC, TOK, NT, H, LB, LW = c.D, c.KC, c.TOK, c.NT, c.H, c.LB, c.LW
    TT2 = 2 * TOK
    def din(name, shape):
        return nc.dram_tensor(name, list(shape), F32, kind="ExternalInput").ap()
    x_own = din("x_own", [TOK, D]); x_pre = din("x_pre", [TOK, D]); cvec = din("c", [KC, 128]); flag = din("flag", [128, 1])
    w_ada = din("w_ada", [D, 6 * D]); b_ada = din("b_ada", [6 * D])
    n_pre1 = din("mix_pre_norm", [D]); n_post1 = din("mix_post_norm", [D])
    w_in = din("w_in", [D, c.INW])
    lru_cw = din("lru_conv_w", [4, LW]); lru_cb = din("lru_conv_b", [LW])
    lru_aw = din("lru_gate_a_w", [LB, 128, 128]); lru_ab = din("lru_gate_a_b", [LW])
    lru_iw = din("lru_gate_i_w", [LB, 128, 128]); lru_ib = din("lru_gate_i_b", [LW])
    lru_lam = din("lru_lambda", [LW])
    gdn_cw = din("gdn_conv_w", [4, 3 * c.KEY]); gdn_alog = din("gdn_a_log", [H, 1]); gdn_dtb = din("gdn_dt_bias", [H, 1])
    gdn_onw = din("gdn_out_norm", [128])
    w_bl = din("w_branch_lru", [LW, D]); w_bg = din("w_branch_gdn", [c.VAL, D]); w_out = din("w_out", [D, D])
    n_pre2 = din("mlp_pre_norm", [D]); n_post2 = din("mlp_post_norm", [D])
    w_up = din("w_mlp_up", [D, c.DFF]); w_dn = din("w_mlp_down", [c.DFF, D])
    out = nc.dram_tensor("out", [TOK, D], F32, kind="ExternalOutput").ap()
    MODV = kb.dram("modv", [6 * D], F32)
    NREC = LW + 3 * c.KEY + 2 * H
    PROJR = kb.dram("projr", [NREC, TT2], F32)
    PROJN = kb.dram("projn", [c.INW - NREC, TOK], F32)

    class _Proj:
        def __getitem__(s, key):
            rs, cs = key
            r0, r1 = rs.start, rs.stop
            if r0 < c.o_lg:
                return PROJR[r0:r1, cs]
            if c.o_q <= r0 < c.o_z:
                return PROJR[r0 - c.o_q + LW:r1 - c.o_q + LW, cs]
            if c.o_a <= r0 < c.o_gl:
                return PROJR[r0 - c.o_a + LW + 3 * c.KEY:r1 - c.o_a + LW + 3 * c.KEY, cs]
            cs2 = slice(cs.start - TOK, cs.stop - TOK)
            assert cs2.start >= 0
            if r0 < c.o_q:
                return PROJN[r0 - c.o_lg:r1 - c.o_lg, cs2]
            if r0 < c.o_a:
                return PROJN[r0 - c.o_z + LW:r1 - c.o_z + LW, cs2]
            return PROJN[r0 - c.o_gl + LW + c.VAL:r1 - c.o_gl + LW + c.VAL, cs2]
    PROJ = _Proj()
    YA = kb.dram("ya", [LW, TOK], BF16)
    YB = kb.dram("yb", [c.VAL, TOK], BF16)
    X1 = kb.dram("x1", [TOK, D], F32)
    ACTT = kb.dram("actt", [c.DFF, TOK], BF16)
    b_PROJ = Buf("PROJ"); b_YA = Buf("YA"); b_YB = Buf("YB"); b_X1 = Buf("X1"); b_ACTT = Buf("ACTT"); b_MODV = Buf("MODV")
    b_out = Buf("out")

    ident = kb.sb([128, 128], F32, "ident"); identb = kb.sb([128, 128], BF16, "identb")
    ones = kb.sb([128, 128], F32, "ones")
    UPI = kb.sb([128, 128], F32, "UPI"); LOS = kb.sb([128, 128], F32, "LOS")
    BLK = kb.sb([128, 128], F32, "BLK"); CHA = kb.sb([128, 128], F32, "CHA"); CHB = kb.sb([128, 128], F32, "CHB")
    flg = kb.sb([128, 1], F32, "flg")
    NV = 6 * KC
    modfm = kb.sb([128, NV], F32, "modfm")
    w1s = kb.sb([128, KC], F32, "w1s"); w2s = kb.sb([128, KC], F32, "w2s")
    g1w = kb.sb([128, KC], F32, "g1w"); g2w = kb.sb([128, KC], F32, "g2w")
    b_const = Buf("const"); b_mod = Buf("mod")

    def P(eng, fn, r=(), w=()):
        return kb.op(eng, fn, r, w)

    P("pool", lambda e: e.memset(ident[:], 1.0), w=[b_const])
    P("pool", lambda e: e.affine_select(out=ident[:], in_=ident[:], pattern=[[-1, 128]], compare_op=ALU.is_equal,
                                        fill=0.0, base=0, channel_multiplier=1), r=[b_const], w=[b_const])
    P("pool", lambda e: e.tensor_copy(out=identb[:], in_=ident[:]), r=[b_const], w=[b_const])
    P("pool", lambda e: e.memset(ones[:], 1.0), w=[b_const])
    P("pool", lambda e: e.memset(UPI[:], 1.0), w=[b_const])
    P("pool", lambda e: e.affine_select(out=UPI[:], in_=UPI[:], pattern=[[1, 128]], compare_op=ALU.is_ge,
                                        fill=0.0, base=0, channel_multiplier=-1), r=[b_const], w=[b_const])
    P("pool", lambda e: e.memset(UPI[0:64, 64:128], 0.0), r=[b_const], w=[b_const])
    P("pool", lambda e: e.memset(LOS[:], 1.0), w=[b_const])
    P("pool", lambda e: e.affine_select(out=LOS[:], in_=LOS[:], pattern=[[-1, 128]], compare_op=ALU.is_gt,
                                        fill=0.0, base=0, channel_multiplier=1), r=[b_const], w=[b_const])
    P("pool", lambda e: e.memset(LOS[64:128, 0:64], 0.0), r=[b_const], w=[b_const])
    P("pool", lambda e: e.memset(BLK[:], 0.0), w=[b_const])
    P("pool", lambda e: e.memset(BLK[0:64, 0:64], 1.0), r=[b_const], w=[b_const])
    P("pool", lambda e: e.memset(BLK[64:128, 64:128], 1.0), r=[b_const], w=[b_const])
    P("pool", lambda e: e.memset(CHA[:], 0.0), w=[b_const])
    P("pool", lambda e: e.memset(CHA[0:64, :], 1.0), r=[b_const], w=[b_const])
    P("pool", lambda e: e.memset(CHB[:], 0.0), w=[b_const])
    P("pool", lambda e: e.memset(CHB[64:128, :], 1.0), r=[b_const], w=[b_const])
    kb.dma("sp", flg[:], flag[:, :], w=[b_const])

    def load_vec_fm(vec_ap, n, dst_ap, rbufs=(), wbuf=None):
        v2 = vec_ap.rearrange("(n p) -> n p", p=128)
        with ExitStack() as es:
            old = kb.es; kb.es = es
            for g0 in range(0, n, 128):
                gn = min(128, n - g0)
                st = kb.sb([128, 128], F32, "lv"); pt = kb.ps([128, 128], F32, "lvp")
                bs = Buf("lv"); bp = PB("lvp")
                kb.dma("sp", st[0:gn, :], v2[g0:g0 + gn, :], r=list(rbufs), w=[bs])
                P("pe", lambda e: e.transpose(pt[:, 0:gn], st[0:gn, :], ident[0:gn, 0:gn]), r=[bs, b_const], w=[bp])
                P("dve", lambda e: e.tensor_copy(out=dst_ap[:, g0:g0 + gn], in_=pt[:, 0:gn]), r=[bp], w=[wbuf])
            kb.barrier()
            kb.es = old

    with kb.phase():
        cin = kb.sb([128, 128], F32, "cin"); cact = kb.sb([128, 128], F32, "cact"); cT = kb.sb([128, KC], F32, "cT")
        cps = kb.ps([128, 128], F32, "cps")
        b_c = Buf("c"); b_cp = PB("cp"); b_cT = Buf("cT")
        kb.dma("sp", cin[0:KC, :], cvec[:, :], w=[b_c])
        P("act", lambda e: e.activation(out=cact[0:KC, :], in_=cin[0:KC, :], func=AF.Silu), r=[b_c], w=[b_c])
        P("pe", lambda e: e.transpose(cps[:, 0:KC], cact[0:KC, :], ident[0:KC, 0:KC]), r=[b_c, b_const], w=[b_cp])
        P("dve", lambda e: e.tensor_copy(out=cT[:], in_=cps[:, 0:KC]), r=[b_cp], w=[b_cT])
        KP = min(8, KC)
        NPC = KC // KP
        wst = [kb.sb([128, KP, 512], F32, "adaw") for _ in range(3)]
        bw = [Buf("adaw%d" % i) for i in range(3)]
        mps = [kb.ps([128, 512], F32, "mps") for _ in range(2)]
        bmp = [PB("mps%d" % i) for i in range(2)]
        mst = [kb.sb([1, 512], F32, "mst") for _ in range(2)]
        bms = [Buf("mst%d" % i) for i in range(2)]
        wv = w_ada.rearrange("(kc p) n -> p kc n", p=128)
        li = 0
        for nt in range(6 * D // 512):
            pp = mps[nt % 2]; bpp = bmp[nt % 2]
            for pc in range(NPC):
                wt = wst[li % 3]; bwt = bw[li % 3]; li += 1
                kb.dma("sp", wt[:], wv[:, pc * KP:(pc + 1) * KP, nt * 512:(nt + 1) * 512], w=[bwt])
                for j in range(KP):
                    kc = pc * KP + j
                    P("pe", lambda e, kc=kc, j=j, wt=wt, pp=pp: e.matmul(pp[0:1, :], cT[:, kc:kc + 1], wt[:, j, :],
                                                                        start=(kc == 0), stop=(kc == KC - 1)),
                      r=[b_cT, bwt], w=[bpp])
            ms = mst[nt % 2]; bm = bms[nt % 2]
            P("act", lambda e, ms=ms, pp=pp: e.copy(out=ms[:], in_=pp[0:1, :]), r=[bpp], w=[bm])
            kb.dma("pool", MODV[nt * 512:(nt + 1) * 512].rearrange("(o n) -> o n", o=1), ms[:], r=[bm], w=[b_MODV])
        load_vec_fm(MODV, NV, modfm, rbufs=[b_MODV], wbuf=b_mod)
        tmpv = kb.sb([128, NV], F32, "tmpv"); b_tmp = Buf("tmpv")
        load_vec_fm(b_ada, NV, tmpv, wbuf=b_tmp)
        P("dve", lambda e: e.tensor_tensor(out=modfm[:], in0=modfm[:], in1=tmpv[:], op=ALU.add), r=[b_mod, b_tmp], w=[b_mod])
        nv = kb.sb([128, 4, KC], F32, "nv"); b_nv = Buf("nv")
        for i, v in enumerate([n_pre1, n_post1, n_pre2, n_post2]):
            load_vec_fm(v, KC, nv[:, i, :], wbuf=b_nv)
        P("dve", lambda e: e.scalar_tensor_tensor(out=w1s[:], in0=modfm[:, KC:2 * KC], scalar=1.0, in1=nv[:, 0, :],
                                                  op0=ALU.add, op1=ALU.mult), r=[b_mod, b_nv], w=[b_mod])
        P("dve", lambda e: e.scalar_tensor_tensor(out=w2s[:], in0=modfm[:, 4 * KC:5 * KC], scalar=1.0, in1=nv[:, 2, :],
                                                  op0=ALU.add, op1=ALU.mult), r=[b_mod, b_nv], w=[b_mod])
        P("dve", lambda e: e.tensor_tensor(out=g1w[:], in0=modfm[:, 2 * KC:3 * KC], in1=nv[:, 1, :], op=ALU.mult),
          r=[b_mod, b_nv], w=[b_mod])
        P("dve", lambda e: e.tensor_tensor(out=g2w[:], in0=modfm[:, 5 * KC:6 * KC], in1=nv[:, 3, :], op=ALU.mult),
          r=[b_mod, b_nv], w=[b_mod])
    sh1 = modfm[:, 0:KC]; sh2 = modfm[:, 3 * KC:4 * KC]

    cast_rr = [0]

    def prenorm_block(xsrc, bsrc, t0, ntok, XT, bXT, ws, sh, tag):
        with ExitStack() as es:
            old = kb.es; kb.es = es
            xt = [kb.sb([128, D], F32, "xt") for _ in range(2)]; bx = [Buf("xt%d" % i) for i in range(2)]
            junk = kb.sb([128, D], BF16, "junk"); bj = Buf("junk")
            st = [kb.sb([128, 4], F32, "st") for _ in range(2)]; bst = [Buf("st%d" % i) for i in range(2)]
            tp = [kb.ps([128, 512], F32, "tp") for _ in range(2)]; btp = [PB("tp%d" % i) for i in range(2)]
            for i in range(ntok // 128):
                x_ = xt[i % 2]; b_ = bx[i % 2]; s_ = st[i % 2]; bs_ = bst[i % 2]
                kb.dma("sp", x_[:], xsrc[t0 + i * 128:t0 + (i + 1) * 128, :], r=[bsrc], w=[b_])
                P("act", lambda e: e.activation(out=junk[:], in_=x_[:], func=AF.Square, accum_out=s_[:, 0:1]),
                  r=[b_], w=[bj, bs_])
                P("dve", lambda e: e.tensor_scalar(out=s_[:, 1:2], in0=s_[:, 0:1], scalar1=1.0 / D, scalar2=EPS,
                                                   op0=ALU.mult, op1=ALU.add), r=[bs_], w=[bs_])
                P("act", lambda e: e.activation(out=s_[:, 2:3], in_=s_[:, 1:2], func=AF.Sqrt), r=[bs_], w=[bs_])
                P("dve", lambda e: e.reciprocal(out=s_[:, 3:4], in_=s_[:, 2:3]), r=[bs_], w=[bs_])
                P("act", lambda e: e.activation(out=x_[:], in_=x_[:], func=AF.Identity, scale=s_[:, 3:4]),
                  r=[b_, bs_], w=[b_])
                for g in range(KC // 4 if KC >= 4 else 1):
                    pt = tp[g % 2]; bp = btp[g % 2]
                    nq = min(4, KC)
                    for q in range(nq):
                        kc = g * 4 + q
                        P("pe", lambda e, kc=kc, q=q, pt=pt: e.transpose(pt[:, q * 128:(q + 1) * 128],
                                                                         x_[:, kc * 128:(kc + 1) * 128], ident[:]),
                          r=[b_, b_const], w=[bp])
                    for q in range(nq):
                        kc = g * 4 + q
                        eng = "dve" if q % 2 == 0 else "pool"
                        eng = "dve"
                        P(eng, lambda e, kc=kc, q=q, pt=pt: e.tensor_scalar(
                            out=XT[:, kc, i * 128:(i + 1) * 128], in0=pt[:, q * 128:(q + 1) * 128],
                            scalar1=ws[:, kc:kc + 1], scalar2=sh[:, kc:kc + 1], op0=ALU.mult, op1=ALU.add),
                          r=[bp, b_mod], w=[bXT])
            kb.barrier()
            kb.es = old

    class Gemm:
        def __init__(g, kpc, ntok):
            g.kpc = kpc; g.ntok = ntok
            g.nws = 4
            g.wst = [kb.sb([128, kpc, 128], F32, "wst") for _ in range(g.nws)]; g.bws = [Buf("wst%d" % i) for i in range(g.nws)]
            g.wbf = [kb.sb([128, kpc, 128], BF16, "wbf") for _ in range(3)]; g.bwb = [Buf("wbf%d" % i) for i in range(3)]
            g.acc = [kb.ps([128, max(512, ntok)], F32, "acc") for _ in range(2)]; g.bacc = [PB("acc%d" % i) for i in range(2)]
            g.li = 0; g.ai = 0

        def run(g, XT, bXT, KCn, wcols, M, evac, start_acc=True):
            ntok = g.ntok
            pp = g.acc[g.ai % 2]; bpp = g.bacc[g.ai % 2]; g.ai += 1
            wv = wcols.rearrange("(kc p) m -> p kc m", p=128)
            for p0 in range(0, KCn, g.kpc):
                pn = min(g.kpc, KCn - p0)
                ws = g.wst[g.li % g.nws]; bws = g.bws[g.li % g.nws]
                wb = g.wbf[g.li % 3]; bwb = g.bwb[g.li % 3]; g.li += 1
                kb.dma(("sp", "act")[g.li % 2] if DUALQ else "sp", ws[:, 0:pn, 0:M], wv[:, p0:p0 + pn, :], w=[bws])
                ce = ("dve", "act")[cast_rr[0] % 2]; cast_rr[0] += 1
                if ce == "act":
                    P("act", lambda e: e.copy(out=wb[:, 0:pn, 0:M], in_=ws[:, 0:pn, 0:M]), r=[bws], w=[bwb])
                else:
                    P(ce, lambda e: e.tensor_copy(out=wb[:, 0:pn, 0:M], in_=ws[:, 0:pn, 0:M]), r=[bws], w=[bwb])
                for j in range(pn):
                    kc = p0 + j
                    for s0 in range(0, ntok, 512):
                        s1 = min(ntok, s0 + 512)
                        P("pe", lambda e, j=j, kc=kc: e.matmul(pp[0:M, s0:s1], wb[:, j, 0:M], XT[:, kc, s0:s1],
                                                              start=(kc == 0), stop=(kc == KCn - 1)),
                          r=[bwb, bXT], w=[bpp])
            evac(pp[0:M, 0:ntok], bpp)

    segs_rec = [(c.o_lx, LW), (c.o_q, 3 * c.KEY), (c.o_a, H), (c.o_b, H)]
    segs_non = [(c.o_lg, LW), (c.o_z, c.VAL), (c.o_gl, D), (c.o_gg, D)]

    def win_pass(xsrc, tcol0, segs):
        NT = min(c.NTW, TOK)
        for blk in range(TOK // NT):
            with kb.phase():
                XT = kb.sb([128, KC, NT], BF16, "XT"); bXT = Buf("XT")
                prenorm_block(xsrc, Buf("xin"), blk * NT, NT, XT, bXT, w1s, sh1, "pn1")
                g = Gemm(min(KC, 32), NT)
                ost = [kb.sb([128, NT], F32, "ost") for _ in range(3)]; bo = [Buf("ost%d" % i) for i in range(3)]
                oi = [0]
                for (o0, wd) in segs:
                    for f0 in range(0, wd, 128):
                        M = min(128, wd - f0)
                        def evac(pap, bpp, o0=o0, f0=f0, M=M):
                            o_ = ost[oi[0] % 3]; b_ = bo[oi[0] % 3]; oi[0] += 1
                            P("act", lambda e: e.copy(out=o_[0:M, :], in_=pap), r=[bpp], w=[b_])
                            kb.dma("pool", PROJ[o0 + f0:o0 + f0 + M, tcol0 + blk * NT:tcol0 + (blk + 1) * NT], o_[0:M, :],
                                   r=[b_], w=[b_PROJ])
                        g.run(XT, bXT, KC, w_in[:, o0 + f0:o0 + f0 + M], M, evac)

    win_pass(x_pre, 0, segs_rec)
    win_pass(x_own, TOK, segs_rec + segs_non)

    PL = c.PL
    with kb.phase():
        cw = kb.sb([128, 4, LB], F32, "lcw"); cb = kb.sb([128, LB], F32, "lcb")
        ab = kb.sb([128, LB], F32, "lab"); ib = kb.sb([128, LB], F32, "lib"); nsp = kb.sb([128, LB], F32, "nsp")
        b_lc = Buf("lruconst")
        for j in range(4):
            load_vec_fm(lru_cw[j, :], LB, cw[:, j, :], wbuf=b_lc)
        load_vec_fm(lru_cb, LB, cb, wbuf=b_lc)
        load_vec_fm(lru_ab, LB, ab, wbuf=b_lc)
        load_vec_fm(lru_ib, LB, ib, wbuf=b_lc)
        load_vec_fm(lru_lam, LB, nsp, wbuf=b_lc)
        P("act", lambda e: e.activation(out=nsp[:], in_=nsp[:], func=AF.Exp, scale=-1.0), r=[b_lc], w=[b_lc])
        P("act", lambda e: e.activation(out=nsp[:], in_=nsp[:], func=AF.Ln, bias=1.0), r=[b_lc], w=[b_lc])
        P("dve", lambda e: e.tensor_scalar(out=nsp[:], in0=nsp[:], scalar1=-8.0, scalar2=None, op0=ALU.mult),
          r=[b_lc], w=[b_lc])
        gw32 = kb.sb([128, 2, 128], F32, "gw32"); b_gw32 = Buf("gw32")
        gwb = [kb.sb([128, 2, 128], BF16, "gwb") for _ in range(2)]; b_gwb = [Buf("gwb%d" % i) for i in range(2)]
        NB = 2
        def tiles(n, shape, dt):
            return [kb.sb(shape, dt, n) for _ in range(NB)], [Buf(n + str(i)) for i in range(NB)]
        xin, bxin = tiles("xin", [128, PL + 3], F32)
        xa, bxa = tiles("xa", [128, PL], F32)
        xab, bxab = tiles("xab", [128, PL], BF16)
        rr, brr = tiles("rr", [128, PL], F32)
        ii, bii = tiles("ii", [128, PL], F32)
        aa, baa = tiles("aa", [128, PL], F32)
        mm, bmm = tiles("mm", [128, PL], F32)
        hh, bhh = tiles("hh", [128, PL], F32)
        gg, bgg = tiles("gg", [128, PL], F32)
        g2, bg2 = tiles("g2", [128, PL], F32)
        yo, byo = tiles("yo", [128, PL], BF16)
        state = kb.sb([128, 1], F32, "lstate"); b_state = Buf("lstate")
        gps = [kb.ps([128, 512], F32, "gps") for _ in range(4)]; bgps = [PB("gps%d" % i) for i in range(4)]
        gi = 0; it = 0
        for ct in range(LB):
            wb_ = gwb[ct % 2]; bwb_ = b_gwb[ct % 2]
            kb.dma("sp", gw32[:, 0, :], lru_aw[ct, :, :], w=[b_gw32])
            kb.dma("sp", gw32[:, 1, :], lru_iw[ct, :, :], w=[b_gw32])
            P("pool", lambda e: e.tensor_copy(out=wb_[:], in_=gw32[:]), r=[b_gw32], w=[bwb_])
            for pc in range(TT2 // PL):
                t0 = pc * PL
                own = t0 >= TOK
                k = it % NB; it += 1
                xi = xin[k]; bxi = bxin[k]
                kb.dma("sp", xi[:, 3:PL + 3], PROJ[c.o_lx + ct * 128:c.o_lx + (ct + 1) * 128, t0:t0 + PL], r=[b_PROJ], w=[bxi])
                if t0 == 0:
                    P("pool", lambda e: e.memset(xi[:, 0:3], 0.0), w=[bxi])
                    P("pool", lambda e: e.memset(state[:], 0.0), w=[b_state])
                else:
                    xp = xin[(k - 1) % NB]; bxp = bxin[(k - 1) % NB]
                    if t0 == TOK:
                        P("dve", lambda e: e.tensor_scalar(out=xi[:, 0:3], in0=xp[:, PL:PL + 3], scalar1=flg[:, 0:1],
                                                           scalar2=None, op0=ALU.mult), r=[bxp, b_const], w=[bxi])
                        P("dve", lambda e: e.tensor_scalar(out=state[:], in0=state[:], scalar1=flg[:, 0:1],
                                                           scalar2=None, op0=ALU.mult), r=[b_state, b_const], w=[b_state])
                    else:
                        P("dve", lambda e: e.tensor_copy(out=xi[:, 0:3], in_=xp[:, PL:PL + 3]), r=[bxp], w=[bxi])
                xa_ = xa[k]; bxa_ = bxa[k]
                P("dve", lambda e: e.tensor_scalar(out=xa_[:], in0=xi[:, 0:PL], scalar1=cw[:, 0, ct:ct + 1],
                                                   scalar2=cb[:, ct:ct + 1], op0=ALU.mult, op1=ALU.add),
                  r=[bxi, b_lc], w=[bxa_])
                for j in range(1, 4):
                    P("dve", lambda e, j=j: e.scalar_tensor_tensor(out=xa_[:], in0=xi[:, j:j + PL], scalar=cw[:, j, ct:ct + 1],
                                                                   in1=xa_[:], op0=ALU.mult, op1=ALU.add),
                      r=[bxi, b_lc, bxa_], w=[bxa_])
                xb = xab[k]; bxb = bxab[k]
                P("pool", lambda e: e.tensor_copy(out=xb[:], in_=xa_[:]), r=[bxa_], w=[bxb])
                r_ = rr[k]; br_ = brr[k]; i_ = ii[k]; bi_ = bii[k]
                for sub in range(0, PL, 512):
                    sn = min(512, PL - sub)
                    for gsel, (dst, bdst, bias) in enumerate([(r_, br_, ab), (i_, bi_, ib)]):
                        pp = gps[gi % 4]; bpp = bgps[gi % 4]; gi += 1
                        P("pe", lambda e, pp=pp, gsel=gsel: e.matmul(pp[:, 0:sn], wb_[:, gsel, :], xb[:, sub:sub + sn],
                                                                     start=True, stop=True), r=[bwb_, bxb], w=[bpp])
                        P("act", lambda e, pp=pp, dst=dst, bias=bias: e.activation(
                            out=dst[:, sub:sub + sn], in_=pp[:, 0:sn], func=AF.Sigmoid, bias=bias[:, ct:ct + 1]),
                          r=[bpp, b_lc], w=[bdst])
                a_ = aa[k]; ba_ = baa[k]; m_ = mm[k]; bm_ = bmm[k]; h_ = hh[k]; bh_ = bhh[k]
                P("act", lambda e: e.activation(out=a_[:], in_=r_[:], func=AF.Exp, scale=nsp[:, ct:ct + 1]),
                  r=[br_, b_lc], w=[ba_])
                P("pool", lambda e: e.tensor_tensor(out=m_[:], in0=a_[:], in1=a_[:], op=ALU.mult), r=[ba_], w=[bm_])
                P("act", lambda e: e.activation(out=m_[:], in_=m_[:], func=AF.Sqrt, scale=-1.0, bias=1.0), r=[bm_], w=[bm_])
                P("pool", lambda e: e.tensor_tensor(out=i_[:], in0=i_[:], in1=xa_[:], op=ALU.mult), r=[bi_, bxa_], w=[bi_])
                P("pool", lambda e: e.tensor_tensor(out=m_[:], in0=m_[:], in1=i_[:], op=ALU.mult), r=[bm_, bi_], w=[bm_])
                P("dve", lambda e: e.tensor_tensor_scan(out=h_[:], data0=a_[:], data1=m_[:], initial=state[:, 0:1],
                                                        op0=ALU.mult, op1=ALU.add), r=[ba_, bm_, b_state], w=[bh_])
                P("dve", lambda e: e.tensor_copy(out=state[:], in_=h_[:, PL - 1:PL]), r=[bh_], w=[b_state])
                if own:
                    g_ = gg[k]; bg_ = bgg[k]; q_ = g2[k]; bq_ = bg2[k]; y_ = yo[k]; by_ = byo[k]
                    kb.dma("sp", g_[:], PROJ[c.o_lg + ct * 128:c.o_lg + (ct + 1) * 128, t0:t0 + PL], r=[b_PROJ], w=[bg_])
                    P("pool", lambda e: e.tensor_tensor(out=q_[:], in0=g_[:], in1=g_[:], op=ALU.mult), r=[bg_], w=[bq_])
                    P("pool", lambda e: e.tensor_scalar(out=q_[:], in0=q_[:], scalar1=0.044715, scalar2=1.0,
                                                        op0=ALU.mult, op1=ALU.add), r=[bq_], w=[bq_])
                    P("pool", lambda e: e.tensor_tensor(out=q_[:], in0=q_[:], in1=g_[:], op=ALU.mult), r=[bq_, bg_], w=[bq_])
                    P("act", lambda e: e.activation(out=q_[:], in_=q_[:], func=AF.Sigmoid, scale=2.0 * 0.7978845608028654),
                      r=[bq_], w=[bq_])
                    P("dve", lambda e: e.tensor_tensor(out=q_[:], in0=q_[:], in1=g_[:], op=ALU.mult), r=[bq_, bg_], w=[bq_])
                    P("dve", lambda e: e.tensor_tensor(out=y_[:], in0=q_[:], in1=h_[:], op=ALU.mult), r=[bq_, bh_], w=[by_])
                    kb.dma("pool", YA[ct * 128:(ct + 1) * 128, t0 - TOK:t0 - TOK + PL], y_[:], r=[by_], w=[b_YA])

    gdn_phase(kb, c, locals())

    tail_phases(kb, c, locals())
    kb.barrier()
    kb.root.close()
    return nc


def gdn_phase(kb, c, L):
    P = L["P"]; load_vec_fm = L["load_vec_fm"]
    ident = L["ident"]; identb = L["identb"]; ones = L["ones"]; UPI = L["UPI"]; LOS = L["LOS"]; BLK = L["BLK"]
    CHA = L["CHA"]; CHB = L["CHB"]; flg = L["flg"]; b_const = L["b_const"]
    PROJ = L["PROJ"]; b_PROJ = L["b_PROJ"]; YB = L["YB"]; b_YB = L["b_YB"]
    gdn_cw = L["gdn_cw"]; gdn_alog = L["gdn_alog"]; gdn_dtb = L["gdn_dtb"]; gdn_onw = L["gdn_onw"]
    H, TOK, SBK, KEY = c.H, c.TOK, c.SBK, c.KEY
    TT2 = 2 * TOK
    nt = SBK // 128
    HG = min(4, H)
    NG = H // HG
    with kb.phase():
        bc = Buf("gconst")
        gcw = kb.sb([128, 4, 3 * H], F32, "gcw")
        for j in range(4):
            load_vec_fm(gdn_cw[j, :], 3 * H, gcw[:, j, :], wbuf=bc)
        onw = kb.sb([128, 1], F32, "onw")
        load_vec_fm(gdn_onw, 1, onw, wbuf=bc)
        dtb = kb.sb([128, 1], F32, "dtb"); negA = kb.sb([128, 1], F32, "negA")
        kb.dma("sp", dtb[0:H, :], gdn_dtb[:, :], w=[bc])
        kb.dma("sp", negA[0:H, :], gdn_alog[:, :], w=[bc])
        P("act", lambda e: e.activation(out=negA[0:H, :], in_=negA[0:H, :], func=AF.Exp), r=[bc], w=[bc])
        P("dve", lambda e: e.tensor_scalar(out=negA[0:H, :], in0=negA[0:H, :], scalar1=-1.0, scalar2=None, op0=ALU.mult),
          r=[bc], w=[bc])
        MASK2 = kb.sb([128, 2, 128], F32, "MASK2")
        P("pool", lambda e: e.tensor_copy(out=MASK2[:, 0, :], in_=LOS[:]), r=[b_const], w=[bc])
        P("pool", lambda e: e.tensor_copy(out=MASK2[:, 1, :], in_=UPI[:]), r=[b_const], w=[bc])
        Sm = kb.sb([128, H, 128], F32, "Sm"); Sb = kb.sb([128, H, 128], BF16, "Sb")
        bSm = [Buf("Sm%d" % h) for h in range(H)]; bSb = [Buf("Sb%d" % h) for h in range(H)]
        qnT = kb.sb([128, H, SBK], BF16, "qnT"); knT = kb.sb([128, H, SBK], BF16, "knT")
        bq = [Buf("qnT%d" % h) for h in range(H)]; bk = [Buf("knT%d" % h) for h in range(H)]
        kbg = kb.sb([128, H, nt, 128], BF16, "kbg"); kdec = kb.sb([128, H, nt, 128], BF16, "kdec")
        vb = kb.sb([128, H, nt, 128], BF16, "vb")
        bkt = [Buf("ktok%d" % h) for h in range(H)]
        zs = kb.sb([128, H, SBK], F32, "zs"); bz = [Buf("zs%d" % h) for h in range(H)]
        ybT = kb.sb([128, H, SBK], BF16, "ybT"); byb = [Buf("ybT%d" % h) for h in range(H)]
        uu = kb.sb([128, H, nt, 128], F32, "uu"); nwT = kb.sb([128, H, nt, 128], BF16, "nwT")
        atT = kb.sb([128, H, nt, 128], BF16, "atT")
        bu = [[Buf("u") for _ in range(nt)] for _ in range(H)]
        bnw = [[Buf("nw") for _ in range(nt)] for _ in range(H)]
        bat = [[Buf("at") for _ in range(nt)] for _ in range(H)]
        sc = kb.sb([128, nt, 8, H], F32, "sc"); bsc = [Buf("sc%d" % j) for j in range(nt)]
        afm = kb.sb([128, 2, SBK], F32, "afm"); bafm = Buf("afm")
        pA = [kb.ps([128, 512], F32, "pA") for _ in range(1)]; bpA = [PB("pA%d" % i) for i in range(1)]
        pT = [kb.ps([128, 4, 128], BF16, "pT") for _ in range(1)]; bpT = [PB("pT%d" % i) for i in range(1)]
        itp = [0]
        def nextT():
            return pT[0], bpT[0]
        pB = [kb.ps([128, 4, 2, 128], F32, "pB") for _ in range(2)]; bpB = [PB("pB%d" % i) for i in range(2)]
        pC = [kb.ps([128, 4, 128], F32, "pC") for _ in range(2)]; bpC = [PB("pC%d" % i) for i in range(2)]
        ia = [0]; ib = [0]; ic = [0]
        def nextA():
            ia[0] += 1; return pA[0], bpA[0]
        def nextB():
            ib[0] += 1; return pB[ib[0] % 2], bpB[ib[0] % 2]
        def nextC():
            ic[0] += 1; return pC[ic[0] % 2], bpC[ic[0] % 2]
        it = [0]

        for sbi in range(TT2 // SBK):
            t0 = sbi * SBK
            own = t0 >= TOK
            kb.dma("sp", afm[0:H, 0, :], PROJ[c.o_a:c.o_a + H, t0:t0 + SBK], r=[b_PROJ], w=[bafm])
            kb.dma("sp", afm[0:H, 1, :], PROJ[c.o_b:c.o_b + H, t0:t0 + SBK], r=[b_PROJ], w=[bafm])
            P("act", lambda e: e.activation(out=afm[0:H, 0, :], in_=afm[0:H, 0, :], func=AF.Exp, bias=dtb[0:H, 0:1]),
              r=[bafm, bc], w=[bafm])
            P("act", lambda e: e.activation(out=afm[0:H, 0, :], in_=afm[0:H, 0, :], func=AF.Ln, bias=1.0), r=[bafm], w=[bafm])
            P("dve", lambda e: e.tensor_scalar(out=afm[0:H, 0, :], in0=afm[0:H, 0, :], scalar1=negA[0:H, 0:1], scalar2=None,
                                               op0=ALU.mult), r=[bafm, bc], w=[bafm])
            P("act", lambda e: e.activation(out=afm[0:H, 1, :], in_=afm[0:H, 1, :], func=AF.Sigmoid), r=[bafm], w=[bafm])
            for j in range(nt):
                pt, bpt = nextA()
                for q in range(2):
                    P("pe", lambda e: e.transpose(pt[:, q * H:(q + 1) * H], afm[0:H, q, j * 128:(j + 1) * 128], ident[0:H, 0:H]),
                      r=[bafm, b_const], w=[bpt])
                P("dve", lambda e: e.tensor_copy(out=sc[:, j, 0:2, :], in_=pt[:, 0:2 * H].rearrange("p (a h) -> p a h", a=2)),
                  r=[bpt], w=[bsc[j]])
                pt2, bpt2 = nextA()
                for q, mt in enumerate([UPI, BLK, CHA, CHB]):
                    P("pe", lambda e: e.matmul(pt2[:, q * H:(q + 1) * H], mt[:], sc[:, j, 0, :], start=True, stop=True),
                      r=[bsc[j], b_const], w=[bpt2])
                P("act", lambda e: e.activation(out=sc[:, j, 2, :], in_=pt2[:, 0:H], func=AF.Exp), r=[bpt2], w=[bsc[j]])
                P("dve", lambda e: e.tensor_copy(out=sc[:, j, 4, :], in_=pt2[:, 0:H]), r=[bpt2], w=[bsc[j]])
                P("dve", lambda e: e.tensor_tensor(out=sc[:, j, 3, :], in0=pt2[:, H:2 * H], in1=sc[:, j, 4, :], op=ALU.subtract),
                  r=[bpt2, bsc[j]], w=[bsc[j]])
                P("act", lambda e: e.activation(out=sc[:, j, 3, :], in_=sc[:, j, 3, :], func=AF.Exp), r=[bsc[j]], w=[bsc[j]])
                P("act", lambda e: e.activation(out=sc[:, j, 6:8, :], in_=pt2[:, 2 * H:4 * H].rearrange("p (a h) -> p a h", a=2),
                                                func=AF.Exp), r=[bpt2], w=[bsc[j]])
                P("dve", lambda e: e.tensor_tensor(out=sc[:, j, 4, :], in0=sc[:, j, 1, :], in1=sc[:, j, 2, :], op=ALU.mult),
                  r=[bsc[j]], w=[bsc[j]])
                P("dve", lambda e: e.tensor_scalar(out=sc[:, j, 5, :], in0=sc[:, j, 1, :], scalar1=-1.0, scalar2=None,
                                                   op0=ALU.mult), r=[bsc[j]], w=[bsc[j]])
            with kb.scope():
                xin_g = [kb.sb([128, 3, HG, SBK + 3], F32, "xin_g") for _ in range(2)]; bxg = [Buf("xin_g%d" % i) for i in range(2)]
                cv_g = [kb.sb([128, 3, HG, SBK], F32, "cv_g") for _ in range(2)]; bcg = [Buf("cv_g%d" % i) for i in range(2)]
                sq_g = kb.sb([128, 2, HG, SBK], F32, "sq_g"); bsg = Buf("sq_g")
                rn_g = kb.sb([128, 2, HG, SBK], F32, "rn_g"); brg = Buf("rn_g")
                for g in range(NG):
                    h0 = g * HG
                    xg = xin_g[g % 2]; bx = bxg[g % 2]; cg = cv_g[g % 2]; bcv_ = bcg[g % 2]
                    segs = (0, 1, 2) if own else (1, 2)
                    s_lo = segs[0]
                    for seg in segs:
                        row0 = c.o_q + seg * KEY + h0 * 128
                        if t0 == 0:
                            kb.dma("sp", xg[:, seg, :, 3:SBK + 3],
                                   PROJ[row0:row0 + HG * 128, t0:t0 + SBK].rearrange("(hh p) t -> p hh t", p=128), r=[b_PROJ], w=[bx])
                            P("pool", lambda e: e.memset(xg[:, seg, :, 0:3], 0.0), w=[bx])
                        else:
                            kb.dma("sp", xg[:, seg, :, 0:SBK + 3],
                                   PROJ[row0:row0 + HG * 128, t0 - 3:t0 + SBK].rearrange("(hh p) t -> p hh t", p=128), r=[b_PROJ], w=[bx])
                            if t0 == TOK:
                                P("dve", lambda e: e.tensor_scalar(out=xg[:, seg, :, 0:3], in0=xg[:, seg, :, 0:3], scalar1=flg[:, 0:1],
                                                                   scalar2=None, op0=ALU.mult), r=[bx, b_const], w=[bx])
                    for j in range(4):
                        for seg in segs:
                            for hh in range(HG):
                                idx = seg * H + h0 + hh
                                if j == 0:
                                    P("dve", lambda e: e.tensor_scalar(out=cg[:, seg, hh, :], in0=xg[:, seg, hh, 0:SBK], scalar1=gcw[:, 0, idx:idx + 1],
                                                                       scalar2=None, op0=ALU.mult), r=[bx, bc], w=[bcv_])
                                else:
                                    P("dve", lambda e: e.scalar_tensor_tensor(out=cg[:, seg, hh, :], in0=xg[:, seg, hh, j:j + SBK],
                                                                              scalar=gcw[:, j, idx:idx + 1], in1=cg[:, seg, hh, :],
                                                                              op0=ALU.mult, op1=ALU.add), r=[bx, bc, bcv_], w=[bcv_])
                    P("act", lambda e: e.activation(out=cg[:, s_lo:3, :, :], in_=cg[:, s_lo:3, :, :], func=AF.Silu), r=[bcv_], w=[bcv_])
                    if own:
                        zr = c.o_z + h0 * 128
                        kb.dma("sp", zs[:, h0:h0 + HG, :], PROJ[zr:zr + HG * 128, t0:t0 + SBK].rearrange("(hh p) t -> p hh t", p=128),
                               r=[b_PROJ], w=[bz[h] for h in range(h0, h0 + HG)])
                        P("act", lambda e: e.activation(out=zs[:, h0:h0 + HG, :], in_=zs[:, h0:h0 + HG, :], func=AF.Silu),
                          r=[bz[h] for h in range(h0, h0 + HG)], w=[bz[h] for h in range(h0, h0 + HG)])
                    nqk = 2 - s_lo
                    P("pool", lambda e: e.tensor_tensor(out=sq_g[:, 0:nqk, :, :], in0=cg[:, s_lo:2, :, :], in1=cg[:, s_lo:2, :, :], op=ALU.mult),
                      r=[bcv_], w=[bsg])
                    sqf = sq_g[:].rearrange("p a b c -> p (a b c)")
                    rnf = rn_g[:].rearrange("p a b c -> p (a b c)")
                    ncol = nqk * HG * SBK
                    for c0 in range(0, ncol, 1024):
                        cn = min(1024, ncol - c0)
                        pb, bpb = nextB()
                        pbf = pb[:].rearrange("p a b c -> p (a b c)")
                        for s0 in range(0, cn, 512):
                            sn = min(512, cn - s0)
                            P("pe", lambda e: e.matmul(pbf[:, s0:s0 + sn], ones[:], sqf[:, c0 + s0:c0 + s0 + sn], start=True, stop=True),
                              r=[bsg, b_const], w=[bpb])
                        P("act", lambda e: e.activation(out=rnf[:, c0:c0 + cn], in_=pbf[:, 0:cn], func=AF.Ln, bias=EPS), r=[bpb], w=[brg])
                    P("act", lambda e: e.activation(out=rn_g[:, 0:nqk, :, :], in_=rn_g[:, 0:nqk, :, :], func=AF.Exp, scale=-0.5), r=[brg], w=[brg])
                    if own:
                        P("dve", lambda e: e.scalar_tensor_tensor(out=qnT[:, h0:h0 + HG, :], in0=cg[:, 0, :, :], scalar=float(128.0 ** -0.5),
                                                                  in1=rn_g[:, 0, :, :], op0=ALU.mult, op1=ALU.mult),
                          r=[bcv_, brg], w=[bq[h] for h in range(h0, h0 + HG)])
                    P("dve", lambda e: e.tensor_tensor(out=cg[:, 1, :, :], in0=cg[:, 1, :, :], in1=rn_g[:, nqk - 1, :, :], op=ALU.mult),
                      r=[bcv_, brg], w=[bcv_])
                    P("pool", lambda e: e.tensor_copy(out=knT[:, h0:h0 + HG, :], in_=cg[:, 1, :, :]), r=[bcv_],
                      w=[bk[h] for h in range(h0, h0 + HG)])
                    for j in range(nt):
                        for seg in (1, 2):
                            pt, bpt = nextC()
                            for hh in range(HG):
                                P("pe", lambda e: e.transpose(pt[:, hh, :], cg[:, seg, hh, j * 128:(j + 1) * 128], ident[:]),
                                  r=[bcv_, b_const], w=[bpt])
                            def bcs(q):
                                return sc[:, j, q, h0:h0 + HG].unsqueeze(2).to_broadcast([128, HG, 128])
                            wk = [bkt[h] for h in range(h0, h0 + HG)]
                            if seg == 1:
                                P("dve", lambda e: e.tensor_tensor(out=kbg[:, h0:h0 + HG, j, :], in0=pt[:, 0:HG, :], in1=bcs(4), op=ALU.mult),
                                  r=[bpt, bsc[j]], w=wk)
                                P("dve", lambda e: e.tensor_tensor(out=kdec[:, h0:h0 + HG, j, :], in0=pt[:, 0:HG, :], in1=bcs(3), op=ALU.mult),
                                  r=[bpt, bsc[j]], w=wk)
                            else:
                                P("dve", lambda e: e.tensor_tensor(out=vb[:, h0:h0 + HG, j, :], in0=pt[:, 0:HG, :], in1=bcs(1), op=ALU.mult),
                                  r=[bpt, bsc[j]], w=wk)
            with kb.scope():
                lhsD = kb.sb([128, H, 128], F32, "lhsD"); blD = [Buf("lhsD") for _ in range(NG)]
                E2 = kb.sb([128, H, 2, 128], F32, "E2"); bE2 = [Buf("E2") for _ in range(NG)]
                Nc = [kb.sb([128, H, 2, 128], BF16, "Nc") for _ in range(2)]
                bNc = [[Buf("Nc") for _ in range(NG)] for _ in range(2)]
                Pm = [kb.sb([128, H, 128], BF16, "Pm") for _ in range(2)]; bPm = [[Buf("Pm") for _ in range(NG)] for _ in range(2)]
                vnew = kb.sb([128, H, 128], BF16, "vnew"); bvn = [Buf("vnew") for _ in range(NG)]
                o2s = kb.sb([128, H, 128], F32, "o2s"); bo2 = [Buf("o2s") for _ in range(NG)]
                oo = kb.sb([128, H, 128], F32, "oo"); boo = Buf("oo")
                osq = kb.sb([128, H, 128], F32, "osq"); bosq = Buf("osq")
                oss = kb.sb([128, 4, H], F32, "oss"); boss = Buf("oss")
                for j in range(nt):
                    def hs(g):
                        return range(g * HG, (g + 1) * HG)
                    cs = slice(j * 128, (j + 1) * 128)
                    for g in range(NG):
                        h0 = g * HG
                        for h in hs(g):
                            P("pool", lambda e: e.tensor_scalar(out=lhsD[:, h, :], in0=UPI[:], scalar1=sc[:, j, 0, h:h + 1], scalar2=None,
                                                                op0=ALU.mult), r=[b_const, bsc[j]], w=[blD[g]])
                        pb, bpb = nextB()
                        for h in hs(g):
                            P("pe", lambda e: e.matmul(pb[:, h - h0, 0, :], lhsD[:, h, :], LOS[:], start=True, stop=True),
                              r=[blD[g], b_const], w=[bpb])
                            P("pe", lambda e: e.matmul(pb[:, h - h0, 1, :], LOS[:], lhsD[:, h, :], start=True, stop=True),
                              r=[blD[g], b_const], w=[bpb])
                        P("act", lambda e: e.activation(out=E2[:, h0:h0 + HG, :, :], in_=pb[:, 0:HG, :, :], func=AF.Exp), r=[bpb], w=[bE2[g]])
                        for h in hs(g):
                            P("pool", lambda e: e.tensor_tensor(out=E2[:, h, :, :], in0=E2[:, h, :, :], in1=MASK2[:], op=ALU.mult),
                              r=[bE2[g], bc], w=[bE2[g]])
                        if G2S < 1:
                            continue
                        pb, bpb = nextB()
                        for h in hs(g):
                            P("pe", lambda e: e.matmul(pb[:, h - h0, 0, :], knT[:, h, cs], knT[:, h, cs], start=True, stop=True),
                              r=[bk[h]], w=[bpb])
                            if own:
                                P("pe", lambda e: e.matmul(pb[:, h - h0, 1, :], knT[:, h, cs], qnT[:, h, cs], start=True, stop=True),
                                  r=[bk[h], bq[h]], w=[bpb])
                        for h in hs(g):
                            P("dve", lambda e: e.scalar_tensor_tensor(out=Nc[0][:, h, 0, :], in0=pb[:, h - h0, 0, :], scalar=sc[:, j, 5, h:h + 1],
                                                                      in1=E2[:, h, 0, :], op0=ALU.mult, op1=ALU.mult),
                              r=[bpb, bsc[j], bE2[g]], w=[bNc[0][g]])
                        if own:
                            P("dve", lambda e: e.tensor_tensor(out=atT[:, h0:h0 + HG, j, :], in0=pb[:, 0:HG, 1, :], in1=E2[:, h0:h0 + HG, 1, :],
                                                               op=ALU.mult), r=[bpb, bE2[g]], w=[bat[h][j] for h in hs(g)])
                    if G2S < 2:
                        continue
                    for g in range(NG):
                        h0 = g * HG
                        pc_ = nextT()
                        for h in hs(g):
                            P("pe", lambda e: e.transpose(pc_[0][:, h - h0, :], Nc[0][:, h, 0, :], identb[:]), r=[bNc[0][g], b_const], w=[pc_[1]])
                        P("act", lambda e: e.copy(out=Nc[0][:, h0:h0 + HG, 1, :], in_=pc_[0][:, 0:HG, :]), r=[pc_[1]], w=[bNc[0][g]])
                        for h in hs(g):
                            P("pool", lambda e: e.tensor_tensor(out=Pm[0][:, h, :], in0=Nc[0][:, h, 1, :], in1=identb[:], op=ALU.add),
                              r=[bNc[0][g], b_const], w=[bPm[0][g]])
                    cur = 0
                    if G2S < 3:
                        continue
                    for lvl in range(1, 6):
                        nx = 1 - cur
                        for g in range(NG):
                            h0 = g * HG
                            pb, bpb = nextB()
                            for h in hs(g):
                                P("pe", lambda e: e.matmul(pb[:, h - h0, 0, :], Nc[cur][:, h, 1, :], Nc[cur][:, h, 0, :], start=True, stop=True),
                                  r=[bNc[cur][g]], w=[bpb])
                                if lvl < 5:
                                    P("pe", lambda e: e.matmul(pb[:, h - h0, 1, :], Nc[cur][:, h, 0, :], Nc[cur][:, h, 1, :], start=True, stop=True),
                                      r=[bNc[cur][g]], w=[bpb])
                            if lvl < 5:
                                P("act", lambda e: e.copy(out=Nc[nx][:, h0:h0 + HG, :, :], in_=pb[:, 0:HG, :, :]), r=[bpb], w=[bNc[nx][g]])
                            else:
                                P("act", lambda e: e.copy(out=Nc[nx][:, h0:h0 + HG, 0, :], in_=pb[:, 0:HG, 0, :]), r=[bpb], w=[bNc[nx][g]])
                            pc2, bpc2 = nextC()
                            for h in hs(g):
                                P("pe", lambda e: e.matmul(pc2[:, h - h0, :], Nc[nx][:, h, 0, :], Pm[cur][:, h, :], start=True, stop=True),
                                  r=[bNc[nx][g], bPm[cur][g]], w=[bpc2])
                            P("dve", lambda e: e.tensor_tensor(out=Pm[nx][:, h0:h0 + HG, :], in0=pc2[:, 0:HG, :], in1=Pm[cur][:, h0:h0 + HG, :],
                                                               op=ALU.add), r=[bpc2, bPm[cur][g]], w=[bPm[nx][g]])
                        cur = nx
                    if G2S < 4:
                        continue
                    for g in range(NG):
                        h0 = g * HG
                        pb, bpb = nextB()
                        for h in hs(g):
                            P("pe", lambda e: e.matmul(pb[:, h - h0, 0, :], Pm[cur][:, h, :], vb[:, h, j, :], start=True, stop=True),
                              r=[bPm[cur][g], bkt[h]], w=[bpb])
                            if G2S >= 5:
                                P("pe", lambda e: e.matmul(pb[:, h - h0, 1, :], kbg[:, h, j, :], Pm[cur][:, h, :], start=True, stop=True),
                                  r=[bPm[cur][g], bkt[h]], w=[bpb])
                        if G2S >= 6:
                            P("act", lambda e: e.copy(out=uu[:, h0:h0 + HG, j, :], in_=pb[:, 0:HG, 0, :]), r=[bpb], w=[bu[h][j] for h in hs(g)])
                        if G2S >= 7:
                            P("dve", lambda e: e.tensor_scalar(out=nwT[:, h0:h0 + HG, j, :], in0=pb[:, 0:HG, 1, :], scalar1=-1.0, scalar2=None,
                                                           op0=ALU.mult), r=[bpb], w=[bnw[h][j] for h in hs(g)])
                if t0 == 0:
                    P("pool", lambda e: e.memset(Sm[:], 0.0), w=bSm)
                    P("pool", lambda e: e.memset(Sb[:], 0.0), w=bSb)
                elif t0 == TOK:
                    P("dve", lambda e: e.tensor_scalar(out=Sm[:], in0=Sm[:], scalar1=flg[:, 0:1], scalar2=None, op0=ALU.mult),
                      r=bSm + [b_const], w=bSm)
                    P("act", lambda e: e.copy(out=Sb[:], in_=Sm[:]), r=bSm, w=bSb)
                for j in range(nt):
                    cs0 = j * 128
                    po1 = [None] * NG
                    for half in range(2):
                        rs = slice(half * 64, half * 64 + 64)
                        cols = slice(cs0 + half * 64, cs0 + half * 64 + 64)
                        for g in range(NG):
                            h0 = g * HG
                            pw, bpw = nextC()
                            for h in hs(g):
                                P("pe", lambda e: e.matmul(pw[rs, h - h0, :], nwT[:, h, j, rs], Sb[:, h, :], start=True, stop=True),
                                  r=[bnw[h][j], bSb[h]], w=[bpw])
                            P("dve", lambda e: e.tensor_tensor(out=vnew[rs, h0:h0 + HG, :], in0=pw[rs, 0:HG, :], in1=uu[rs, h0:h0 + HG, j, :],
                                                               op=ALU.add), r=[bpw] + [bu[h][j] for h in hs(g)], w=[bvn[g]])
                            if own:
                                pq, bpq = nextC()
                                for h in hs(g):
                                    P("pe", lambda e: e.matmul(pq[rs, h - h0, :], qnT[:, h, cols], Sb[:, h, :], start=True, stop=True),
                                      r=[bq[h], bSb[h]], w=[bpq])
                                for h in hs(g):
                                    P("dve", lambda e: e.tensor_scalar(out=o2s[rs, h, :], in0=pq[rs, h - h0, :], scalar1=sc[rs, j, 2, h:h + 1],
                                                                       scalar2=None, op0=ALU.mult), r=[bpq, bsc[j]], w=[bo2[g]])
                            psd, bpsd = nextC()
                            for h in hs(g):
                                P("pe", lambda e: e.matmul(psd[:, h - h0, :], kdec[rs, h, j, :], vnew[rs, h, :], start=True, stop=True),
                                  r=[bkt[h], bvn[g]], w=[bpsd])
                            for h in hs(g):
                                P("dve", lambda e: e.scalar_tensor_tensor(out=Sm[:, h, :], in0=Sm[:, h, :], scalar=sc[:, j, 6 + half, h:h + 1],
                                                                          in1=psd[:, h - h0, :], op0=ALU.mult, op1=ALU.add),
                                  r=[bSm[h], bsc[j], bpsd], w=[bSm[h]])
                            P("act", lambda e: e.copy(out=Sb[:, h0:h0 + HG, :], in_=Sm[:, h0:h0 + HG, :]), r=[bSm[h] for h in hs(g)],
                              w=[bSb[h] for h in hs(g)])
                    if own:
                        for g in range(NG):
                            h0 = g * HG
                            po2, bpo2 = nextC()
                            for h in hs(g):
                                P("pe", lambda e: e.matmul(po2[:, h - h0, :], atT[:, h, j, :], vnew[:, h, :], start=True, stop=True),
                                  r=[bat[h][j], bvn[g]], w=[bpo2])
                            P("dve", lambda e: e.tensor_tensor(out=oo[:, h0:h0 + HG, :], in0=po2[:, 0:HG, :], in1=o2s[:, h0:h0 + HG, :], op=ALU.add),
                              r=[bpo2, bo2[g]], w=[boo])
                        P("pool", lambda e: e.tensor_tensor(out=osq[:], in0=oo[:], in1=oo[:], op=ALU.mult), r=[boo], w=[bosq])
                        P("dve", lambda e: e.tensor_reduce(out=oss[:, 0, :], in_=osq[:], axis=mybir.AxisListType.X, op=ALU.add),
                          r=[bosq], w=[boss])
                        P("dve", lambda e: e.tensor_scalar(out=oss[:, 1, :], in0=oss[:, 0, :], scalar1=1.0 / 128.0, scalar2=EPS,
                                                           op0=ALU.mult, op1=ALU.add), r=[boss], w=[boss])
                        P("act", lambda e: e.activation(out=oss[:, 2, :], in_=oss[:, 1, :], func=AF.Sqrt), r=[boss], w=[boss])
                        P("dve", lambda e: e.reciprocal(out=oss[:, 3, :], in_=oss[:, 2, :]), r=[boss], w=[boss])
                        for h in range(H):
                            P("dve", lambda e: e.tensor_scalar(out=oo[:, h, :], in0=oo[:, h, :], scalar1=oss[:, 3, h:h + 1], scalar2=None,
                                                               op0=ALU.mult), r=[boo, boss], w=[boo])
                        for g in range(NG):
                            h0 = g * HG
                            pt, bpt = nextA()
                            for h in hs(g):
                                P("pe", lambda e: e.transpose(pt[:, (h - h0) * 128:(h - h0 + 1) * 128], oo[:, h, :], ident[:]),
                                  r=[boo, b_const], w=[bpt])
                            for h in hs(g):
                                P("dve", lambda e: e.scalar_tensor_tensor(out=ybT[:, h, cs0:cs0 + 128], in0=pt[:, (h - h0) * 128:(h - h0 + 1) * 128],
                                                                          scalar=onw[:, 0:1], in1=zs[:, h, cs0:cs0 + 128],
                                                                          op0=ALU.mult, op1=ALU.mult), r=[bpt, bc, bz[h]], w=[byb[h]])
            if own:
                for h in range(H):
                    kb.dma("pool", YB[h * 128:(h + 1) * 128, t0 - TOK:t0 - TOK + SBK], ybT[:, h, :], r=[byb[h]], w=[b_YB])


_pT = {}


def kb_ps_bf16(kb, name):
    key = id(kb.es)
    if key not in _pT:
        _pT.clear()
        _pT[key] = ([kb.ps([128, 4, 128], BF16, name) for _ in range(2)], [Buf(name + str(i)) for i in range(2)], [0])
    tl, bl, ctr = _pT[key]
    ctr[0] += 1
    return tl[ctr[0] % 2], bl[ctr[0] % 2]


def tail_phases(kb, c, L):
    P = L["P"]; Gemm = L["Gemm"]; prenorm_block = L["prenorm_block"]
    ident = L["ident"]; ones = L["ones"]; b_const = L["b_const"]; b_mod = L["b_mod"]
    PROJ = L["PROJ"]; b_PROJ = L["b_PROJ"]; YA = L["YA"]; b_YA = L["b_YA"]; YB = L["YB"]; b_YB = L["b_YB"]
    X1 = L["X1"]; b_X1 = L["b_X1"]; ACTT = L["ACTT"]; b_ACTT = L["b_ACTT"]
    w_bl = L["w_bl"]; w_bg = L["w_bg"]; w_out = L["w_out"]; w_up = L["w_up"]; w_dn = L["w_dn"]
    x_own = L["x_own"]; out = L["out"]; b_out = L["b_out"]
    g1w = L["g1w"]; g2w = L["g2w"]; w2s = L["w2s"]; sh2 = L["sh2"]
    D, KC, TOK, NT, LB, H = c.D, c.KC, c.TOK, c.NT, c.LB, c.H
    nt = NT // 128
    NT1 = min(c.NTW, TOK)
    YG = kb.dram("yg", [D, max(NT, NT1)], F32)
    b_YG = Buf("YG")

    def out_gemm_and_epilogue(g, XT, bXT, KCn, wmat, gw, xres, bxres, xres_row0, dst, bdst, dst_row0, NT):
        nt = NT // 128
        rstd = kb.sb([128, nt, 4], F32, "rstd"); brs = Buf("rstd")
        sss = kb.sb([128, NT], F32, "sss"); bsss = Buf("sss")
        es_in = ExitStack(); old_es = kb.es; kb.es = es_in
        g = g()
        ssb = kb.ps([128, max(512, NT)], F32, "ssb"); bssb = PB("ssb")
        ysq = [kb.sb([128, NT], F32, "ysq") for _ in range(2)]; bys = [Buf("ysq%d" % i) for i in range(2)]
        ygs = [kb.sb([128, NT], F32, "ygs") for _ in range(2)]; byg = [Buf("ygs%d" % i) for i in range(2)]
        for f in range(KC):
            def evac(pap, bpp, f=f):
                q_ = ysq[f % 2]; bq_ = bys[f % 2]; y_ = ygs[f % 2]; by_ = byg[f % 2]
                P("act", lambda e: e.activation(out=q_[:], in_=pap, func=AF.Square), r=[bpp], w=[bq_])
                P("act", lambda e: e.activation(out=y_[:], in_=pap, func=AF.Copy, scale=gw[:, f:f + 1]), r=[bpp, b_mod], w=[by_])
                for s0 in range(0, NT, 512):
                    s1 = min(NT, s0 + 512)
                    P("pe", lambda e: e.matmul(ssb[:, s0:s1], ones[:], q_[:, s0:s1], start=(f == 0), stop=(f == KC - 1)),
                      r=[bq_, b_const], w=[bssb])
                kb.dma("pool", YG[f * 128:(f + 1) * 128, 0:NT], y_[:], r=[by_], w=[b_YG])
            g.run(XT, bXT, KCn, wmat[:, f * 128:(f + 1) * 128], 128, evac)
        P("act", lambda e: e.copy(out=sss[:], in_=ssb[:, 0:NT]), r=[bssb], w=[bsss])
        kb.barrier(); es_in.close(); kb.es = old_es
        pss = kb.ps([128, 512], F32, "pss"); bpss = PB("pss")
        for i in range(nt):
            P("pe", lambda e: e.transpose(pss[:, 0:128], sss[:, i * 128:(i + 1) * 128], ident[:]), r=[bsss, b_const], w=[bpss])
            P("dve", lambda e: e.tensor_scalar(out=rstd[:, i, 0:1], in0=pss[:, 0:1], scalar1=1.0 / D, scalar2=EPS, op0=ALU.mult, op1=ALU.add),
              r=[bpss], w=[brs])
            P("act", lambda e: e.activation(out=rstd[:, i, 1:2], in_=rstd[:, i, 0:1], func=AF.Sqrt), r=[brs], w=[brs])
            P("dve", lambda e: e.reciprocal(out=rstd[:, i, 2:3], in_=rstd[:, i, 1:2]), r=[brs], w=[brs])
        xt = [kb.sb([128, D], F32, "ext") for _ in range(2)]; bxt = [Buf("ext%d" % i) for i in range(2)]
        ygt = [kb.sb([128, KC, 128], F32, "ygt") for _ in range(2)]; bygt = [Buf("ygt%d" % i) for i in range(2)]
        tp = [kb.ps([128, 512], F32, "etp") for _ in range(2)]; btp = [PB("etp%d" % i) for i in range(2)]
        YGv = YG.rearrange("(kc p) t -> p kc t", p=128)
        ti = 0
        for i in range(nt):
            x_ = xt[i % 2]; bx_ = bxt[i % 2]; yt = ygt[i % 2]; byt = bygt[i % 2]
            kb.dma("sp", x_[:], xres[xres_row0 + i * 128:xres_row0 + (i + 1) * 128, :], r=[bxres], w=[bx_])
            kb.dma("act" if DUALQ else "sp", yt[:], YGv[:, :, i * 128:(i + 1) * 128], r=[b_YG], w=[byt])
            for g0 in range(0, KC, 4):
                pt = tp[ti % 2]; bp = btp[ti % 2]; ti += 1
                for q in range(4):
                    P("pe", lambda e: e.transpose(pt[:, q * 128:(q + 1) * 128], yt[:, g0 + q, :], ident[:]), r=[byt, b_const], w=[bp])
                P("dve", lambda e: e.scalar_tensor_tensor(out=x_[:, g0 * 128:(g0 + 4) * 128], in0=pt[:], scalar=rstd[:, i, 2:3],
                                                          in1=x_[:, g0 * 128:(g0 + 4) * 128], op0=ALU.mult, op1=ALU.add),
                  r=[bp, brs, bx_], w=[bx_])
            kb.dma("pool", dst[dst_row0 + i * 128:dst_row0 + (i + 1) * 128, :], x_[:], r=[bx_], w=[bdst])

    b_xown = Buf("x_own")
    for blk in range(TOK // NT1):
        c0 = blk * NT1
        with kb.phase():
            MT = kb.sb([128, KC, NT1], BF16, "MT"); bMT = Buf("MT")
            with kb.phase():
                XA = kb.sb([128, LB, NT1], BF16, "XA"); bXA = Buf("XA")
                XB = kb.sb([128, H, NT1], BF16, "XB"); bXB = Buf("XB")
                kb.dma("sp", XA[:], YA.rearrange("(kc p) t -> p kc t", p=128)[:, :, c0:c0 + NT1], r=[b_YA], w=[bXA])
                kb.dma("sp", XB[:], YB.rearrange("(kc p) t -> p kc t", p=128)[:, :, c0:c0 + NT1], r=[b_YB], w=[bXB])
                g = Gemm(min(16, LB), NT1)
                gl = [kb.sb([128, NT1], F32, "gl") for _ in range(2)]; bgl = [Buf("gl%d" % i) for i in range(2)]
                gg_ = [kb.sb([128, NT1], F32, "gg") for _ in range(2)]; bgg_ = [Buf("gg%d" % i) for i in range(2)]
                m1 = [kb.sb([128, NT1], F32, "m1") for _ in range(2)]; bm1 = [Buf("m1%d" % i) for i in range(2)]
                for f in range(KC):
                    a_ = gl[f % 2]; ba_ = bgl[f % 2]; b_ = gg_[f % 2]; bb_ = bgg_[f % 2]; m_ = m1[f % 2]; bm_ = bm1[f % 2]
                    kb.dma("sp", a_[:], PROJ[c.o_gl + f * 128:c.o_gl + (f + 1) * 128, TOK + c0:TOK + c0 + NT1], r=[b_PROJ], w=[ba_])
                    kb.dma("sp", b_[:], PROJ[c.o_gg + f * 128:c.o_gg + (f + 1) * 128, TOK + c0:TOK + c0 + NT1], r=[b_PROJ], w=[bb_])
                    P("act", lambda e: e.activation(out=a_[:], in_=a_[:], func=AF.Sigmoid), r=[ba_], w=[ba_])
                    P("act", lambda e: e.activation(out=b_[:], in_=b_[:], func=AF.Sigmoid), r=[bb_], w=[bb_])
                    def evA(pap, bpp):
                        P("dve", lambda e: e.tensor_tensor(out=m_[:], in0=pap, in1=a_[:], op=ALU.mult), r=[bpp, ba_], w=[bm_])
                    def evB(pap, bpp):
                        P("dve", lambda e: e.tensor_tensor(out=b_[:], in0=pap, in1=b_[:], op=ALU.mult), r=[bpp, bb_], w=[bb_])
                        P("pool", lambda e: e.tensor_tensor(out=MT[:, f, :], in0=m_[:], in1=b_[:], op=ALU.add), r=[bm_, bb_], w=[bMT])
                    g.run(XA, bXA, LB, w_bl[:, f * 128:(f + 1) * 128], 128, evA)
                    g.run(XB, bXB, H, w_bg[:, f * 128:(f + 1) * 128], 128, evB)
            with kb.phase():
                out_gemm_and_epilogue(lambda: Gemm(min(16, KC), NT1), MT, bMT, KC, w_out, g1w, x_own, b_xown, c0, X1, b_X1, c0, NT1)

    NTU = min(c.NTW, TOK)
    for blk in range(TOK // NTU):
        c0 = blk * NTU
        with kb.phase():
            XT = kb.sb([128, KC, NTU], BF16, "XT2"); bXT = Buf("XT2")
            prenorm_block(X1, b_X1, c0, NTU, XT, bXT, w2s, sh2, "pn2")
            g = Gemm(min(32, KC), NTU)
            rl = [kb.sb([128, NTU], F32, "rl") for _ in range(2)]; brl = [Buf("rl%d" % i) for i in range(2)]
            ao = [kb.sb([128, NTU], BF16, "ao") for _ in range(3)]; bao = [Buf("ao%d" % i) for i in range(3)]
            for f in range(c.DFF // 128):
                def evac(pap, bpp, f=f):
                    r_ = rl[f % 2]; br_ = brl[f % 2]; a_ = ao[f % 3]; ba_ = bao[f % 3]
                    P("act", lambda e: e.activation(out=r_[:], in_=pap, func=AF.Relu), r=[bpp], w=[br_])
                    eng = "pool" if f % 2 == 0 else "dve"
                    P(eng, lambda e: e.tensor_tensor(out=a_[:], in0=r_[:], in1=r_[:], op=ALU.mult), r=[br_], w=[ba_])
                    kb.dma("pool", ACTT[f * 128:(f + 1) * 128, c0:c0 + NTU], a_[:], r=[ba_], w=[b_ACTT])
                g.run(XT, bXT, KC, w_up[:, f * 128:(f + 1) * 128], 128, evac)

    KF = c.DFF // 128
    for blk in range(TOK // NT):
        c0 = blk * NT
        with kb.phase():
            XD = kb.sb([128, KF, NT], BF16, "XD"); bXD = Buf("XD")
            AV = ACTT.rearrange("(kc p) t -> p kc t", p=128)
            for k0 in range(0, KF, 16):
                kn = min(16, KF - k0)
                kb.dma("sp", XD[:, k0:k0 + kn, :], AV[:, k0:k0 + kn, c0:c0 + NT], r=[b_ACTT], w=[bXD])
            out_gemm_and_epilogue(lambda: Gemm(min(16, KF), NT), XD, bXD, KF, w_dn, g2w, X1, b_X1, c0, out, b_out, c0, NT)


def make_in_maps(inp, cfg, ncores):
    c = cfg
    f = lambda a: np.ascontiguousarray(np.asarray(a, dtype=np.float32))
    shared = {
        "w_ada": f(inp["w_ada"][0]), "b_ada": f(inp["b_ada"][0]),
        "mix_pre_norm": f(inp["mix_pre_norm"][0]), "mix_post_norm": f(inp["mix_post_norm"][0]),
        "w_in": f(inp["w_in"][0]),
        "lru_conv_w": f(inp["lru_conv_w"][0]), "lru_conv_b": f(inp["lru_conv_b"][0]),
        "lru_gate_a_w": f(inp["lru_gate_a_w"][0]), "lru_gate_a_b": f(inp["lru_gate_a_b"][0]).reshape(-1),
        "lru_gate_i_w": f(inp["lru_gate_i_w"][0]), "lru_gate_i_b": f(inp["lru_gate_i_b"][0]).reshape(-1),
        "lru_lambda": f(inp["lru_lambda"][0]),
        "gdn_conv_w": f(inp["gdn_conv_w"][0]), "gdn_a_log": f(inp["gdn_a_log"][0]).reshape(-1, 1),
        "gdn_dt_bias": f(inp["gdn_dt_bias"][0]).reshape(-1, 1), "gdn_out_norm": f(inp["gdn_out_norm"][0]),
        "w_branch_lru": f(inp["w_branch_lru"][0]), "w_branch_gdn": f(inp["w_branch_gdn"][0]), "w_out": f(inp["w_out"][0]),
        "mlp_pre_norm": f(inp["mlp_pre_norm"][0]), "mlp_post_norm": f(inp["mlp_post_norm"][0]),
        "w_mlp_up": f(inp["w_mlp_up"][0]), "w_mlp_down": f(inp["w_mlp_down"][0]),
    }
    x = np.asarray(inp["x"], dtype=np.float32); cc = np.asarray(inp["c"], dtype=np.float32)
    maps = []
    for i in range(ncores):
        b, half = i // 2, i % 2
        m = dict(shared)
        m["x_own"] = np.ascontiguousarray(x[b, half * c.TOK:(half + 1) * c.TOK])
        m["x_pre"] = np.ascontiguousarray(x[b, 0:c.TOK])
        m["c"] = np.ascontiguousarray(cc[b].reshape(c.KC, 128))
        m["flag"] = np.full((128, 1), float(half), dtype=np.float32)
        maps.append(m)
    return maps


_CACHE = {}


def kernel(**inputs):
    cfg = Cfg(D=4096, T=4096, NT=512, SBK=256, PL=1024, NTW=1024)
    if "nc" not in _CACHE:
        _CACHE["nc"] = build(cfg)
    nc = _CACHE["nc"]
    maps = make_in_maps(inputs, cfg, 8)
    res = run_bass_kernel_spmd(nc, maps, core_ids=list(range(8)))
    outp = np.zeros((4, 4096, 4096), dtype=np.float32)
    for i in range(8):
        b, half = i // 2, i % 2
        outp[b, half * cfg.TOK:(half + 1) * cfg.TOK] = res.results[i]["out"]
    return outp
```

```python
import numpy as np
from contextlib import ExitStack, contextmanager
import concourse.bass as bass
import concourse.mybir as mybir
from concourse.bass_utils import run_bass_kernel_spmd

F32 = mybir.dt.float32
BF16 = mybir.dt.bfloat16
AF = mybir.ActivationFunctionType
ALU = mybir.AluOpType
EPS = 1e-6
SEM_LIMIT = 30000
import os
GSTOP = int(os.environ.get('GSTOP', '9'))
DUALQ = int(os.environ.get('DUALQ', '0'))
G2S = int(os.environ.get('G2S', '9'))
NDSEM = 40


class Cfg:
    def __init__(s, D, T, NT, SBK, PL, debug=False, NTW=None):
        s.D = D; s.T = T; s.NT = NT; s.SBK = SBK; s.PL = PL; s.debug = debug; s.NTW = NTW or NT
        s.LW = D // 2; s.LB = s.LW // 128; s.H = (D // 2) // 128; s.KEY = s.H * 128; s.VAL = s.H * 128
        s.DFF = 4 * D; s.TOK = T // 2; s.KC = D // 128
        s.o_lx = 0; s.o_lg = s.LW; s.o_q = 2 * s.LW; s.o_k = s.o_q + s.KEY; s.o_v = s.o_k + s.KEY
        s.o_z = s.o_v + s.VAL; s.o_a = s.o_z + s.VAL; s.o_b = s.o_a + s.H; s.o_gl = s.o_b + s.H; s.o_gg = s.o_gl + D
        s.INW = s.o_gg + D


class Buf:
    __slots__ = ("name", "lw", "rd", "excl")

    def __init__(s, name, excl=False):
        s.name = name; s.lw = None; s.rd = {}; s.excl = excl


def PB(name):
    return Buf(name, True)


class KB:
    def __init__(s, cfg):
        s.cfg = cfg
        s.nc = bass.Bass("TRN2", target_bir_lowering=False)
        nc = s.nc
        s.E = {"pe": nc.tensor, "act": nc.scalar, "dve": nc.vector, "pool": nc.gpsimd, "sp": nc.sync}
        s.root = ExitStack()
        s.es = s.root
        s.semh = {}
        s.sem = {}; s.cnt = {}; s.seen = {e: {} for e in s.E}
        s.nsem = 0
        for e in ("pe", "act", "dve", "pool"):
            s.sem[e] = s._newsem(); s.cnt[e] = 0
        s.dsem = [s._newsem() for _ in range(NDSEM)]
        s.dval = [0] * NDSEM
        s.dnext = 0
        s.uid = 0
        s.block = s.root.enter_context(nc.Block())

    def _newsem(s):
        s.nsem += 1
        name = "s%d" % s.nsem
        h = s.root.enter_context(s.nc.semaphore(name))
        s.semh[name] = h
        return name

    def sb(s, shape, dt, name=None):
        s.uid += 1
        t = s.es.enter_context(s.nc.sbuf_tensor("%s_%d" % (name or "t", s.uid), list(shape), dt))
        return t

    def ps(s, shape, dt=F32, name=None):
        s.uid += 1
        return s.es.enter_context(s.nc.psum_tensor("%s_%d" % (name or "p", s.uid), list(shape), dt))

    def dram(s, name, shape, dt, kind=None):
        k = kind or ("ExternalOutput" if s.cfg.debug else "Internal")
        return s.nc.dram_tensor(name, list(shape), dt, kind=k).ap()

    @contextmanager
    def phase(s):
        s.barrier()
        old = s.es
        es = ExitStack()
        s.es = es
        try:
            yield
        finally:
            s.barrier()
            es.close()
            s.es = old

    @contextmanager
    def scope(s):
        old = s.es
        es = ExitStack()
        s.es = es
        try:
            yield
        finally:
            s.barrier()
            es.close()
            s.es = old

    def _need(s, eng, reads, writes):
        toks = []
        for b in reads:
            if b.lw:
                toks.append(b.lw)
        for b in writes:
            if b.lw and not (eng == "pe" and b.lw[2] == "pe"):
                toks.append(b.lw)
            for k, v in b.rd.items():
                toks.append((k, v, None))
        return toks

    def _wait(s, eng, toks):
        mx = {}
        for t in toks:
            if t[1] > mx.get(t[0], 0):
                mx[t[0]] = t[1]
        for k, v in mx.items():
            if s.seen[eng].get(k, 0) < v:
                s.E[eng].wait_ge(s.semh[k], v)
                s.seen[eng][k] = v

    def op(s, eng, fn, r=(), w=()):
        w = list(w)
        for b in r:
            if b.excl and b not in w:
                w.append(b)
        s._wait(eng, s._need(eng, r, w))
        if s.cnt[eng] >= SEM_LIMIT:
            s.sem[eng] = s._newsem(); s.cnt[eng] = 0
        ins = fn(s.E[eng])
        s.cnt[eng] += 1
        k = s.sem[eng]
        ins.then_inc(s.semh[k], 1)
        tok = (k, s.cnt[eng], eng)
        for b in r:
            if b.rd.get(k, 0) < tok[1]:
                b.rd[k] = tok[1]
        for b in w:
            b.lw = tok; b.rd = {}
        return ins

    def dma(s, q, out, in_, r=(), w=()):
        i = s.dnext
        s.dnext = (i + 1) % NDSEM
        k = s.dsem[i]; pv = s.dval[i]
        toks = s._need("dma", r, w)
        if pv:
            toks.append((k, pv, None))
        s._wait(q, toks)
        s.E[q].dma_start(out=out, in_=in_).then_inc(s.semh[k], 16)
        s.dval[i] = pv + 16
        tok = (k, pv + 16, "dma")
        for b in r:
            b.rd[k] = tok[1]
        for b in w:
            b.lw = tok; b.rd = {}

    def barrier(s):
        toks = [(s.sem[e], s.cnt[e], None) for e in s.cnt if s.cnt[e] > 0]
        toks += [(s.dsem[i], s.dval[i], None) for i in range(NDSEM) if s.dval[i] > 0]
        for e in s.E:
            s._wait(e, toks)


_DBG = {}


def build(cfg):
    kb = KB(cfg)
    _DBG['kb'] = kb
    nc = kb.nc
    c = cfg
    D, KC, TOK, NT, H, LB, LW = c.D, c.KC, c.TOK, c.NT, c.H, c.LB, c.LW
    TT2 = 2 * TOK
    def din(name, shape):
        return nc.dram_tensor(name, list(shape), F32, kind="ExternalInput").ap()
    x_own = din("x_own", [TOK, D]); x_pre = din("x_pre", [TOK, D]); cvec = din("c", [KC, 128]); flag = din("flag", [128, 1])
    w_ada = din("w_ada", [D, 6 * D]); b_ada = din("b_ada", [6 * D])
    n_pre1 = din("mix_pre_norm", [D]); n_post1 = din("mix_post_norm", [D])
    w_in = din("w_in", [D, c.INW])
    lru_cw = din("lru_conv_w", [4, LW]); lru_cb = din("lru_conv_b", [LW])
    lru_aw = din("lru_gate_a_w", [LB, 128, 128]); lru_ab = din("lru_gate_a_b", [LW])
    lru_iw = din("lru_gate_i_w", [LB, 128, 128]); lru_ib = din("lru_gate_i_b", [LW])
    lru_lam = din("lru_lambda", [LW])
    gdn_cw = din("gdn_conv_w", [4, 3 * c.KEY]); gdn_alog = din("gdn_a_log", [H, 1]); gdn_dtb = din("gdn_dt_bias", [H, 1])
    gdn_onw = din("gdn_out_norm", [128])
    w_bl = din("w_branch_lru", [LW, D]); w_bg = din("w_branch_gdn", [c.VAL, D]); w_out = din("w_out", [D, D])
    n_pre2 = din("mlp_pre_norm", [D]); n_post2 = din("mlp_post_norm", [D])
    w_up = din("w_mlp_up", [D, c.DFF]); w_dn = din("w_mlp_down", [c.DFF, D])
    out = nc.dram_tensor("out", [TOK, D], F32, kind="ExternalOutput").ap()
    MODV = kb.dram("modv", [6 * D], F32)
    NREC = LW + 3 * c.KEY + 2 * H
    PROJR = kb.dram("projr", [NREC, TT2], F32)
    PROJN = kb.dram("projn", [c.INW - NREC, TOK], F32)

    class _Proj:
        def __getitem__(s, key):
            rs, cs = key
            r0, r1 = rs.start, rs.stop
            if r0 < c.o_lg:
                return PROJR[r0:r1, cs]
            if c.o_q <= r0 < c.o_z:
                return PROJR[r0 - c.o_q + LW:r1 - c.o_q + LW, cs]
            if c.o_a <= r0 < c.o_gl:
                return PROJR[r0 - c.o_a + LW + 3 * c.KEY:r1 - c.o_a + LW + 3 * c.KEY, cs]
            cs2 = slice(cs.start - TOK, cs.stop - TOK)
            assert cs2.start >= 0
            if r0 < c.o_q:
                return PROJN[r0 - c.o_lg:r1 - c.o_lg, cs2]
            if r0 < c.o_a:
                return PROJN[r0 - c.o_z + LW:r1 - c.o_z + LW, cs2]
            return PROJN[r0 - c.o_gl + LW + c.VAL:r1 - c.o_gl + LW + c.VAL, cs2]
    PROJ = _Proj()
    YA = kb.dram("ya", [LW, TOK], BF16)
    YB = kb.dram("yb", [c.VAL, TOK], BF16)
    X1 = kb.dram("x1", [TOK, D], F32)
    ACTT = kb.dram("actt", [c.DFF, TOK], BF16)
    b_PROJ = Buf("PROJ"); b_YA = Buf("YA"); b_YB = Buf("YB"); b_X1 = Buf("X1"); b_ACTT = Buf("ACTT"); b_MODV = Buf("MODV")
    b_out = Buf("out")

    ident = kb.sb([128, 128], F32, "ident"); identb = kb.sb([128, 128], BF16, "identb")
    ones = kb.sb([128, 128], F32, "ones")
    UPI = kb.sb([128, 128], F32, "UPI"); LOS = kb.sb([128, 128], F32, "LOS")
    BLK = kb.sb([128, 128], F32, "BLK"); CHA = kb.sb([128, 128], F32, "CHA"); CHB = kb.sb([128, 128], F32, "CHB")
    flg = kb.sb([128, 1], F32, "flg")
    NV = 6 * KC
    modfm = kb.sb([128, NV], F32, "modfm")
    w1s = kb.sb([128, KC], F32, "w1s"); w2s = kb.sb([128, KC], F32, "w2s")
    g1w = kb.sb([128, KC], F32, "g1w"); g2w = kb.sb([128, KC], F32, "g2w")
    b_const = Buf("const"); b_mod = Buf("mod")

    def P(eng, fn, r=(), w=()):
        return kb.op(eng, fn, r, w)

    P("pool", lambda e: e.memset(ident[:], 1.0), w=[b_const])
    P("pool", lambda e: e.affine_select(out=ident[:], in_=ident[:], pattern=[[-1, 128]], compare_op=ALU.is_equal,
                                        fill=0.0, base=0, channel_multiplier=1), r=[b_const], w=[b_const])
    P("pool", lambda e: e.tensor_copy(out=identb[:], in_=ident[:]), r=[b_const], w=[b_const])
    P("pool", lambda e: e.memset(ones[:], 1.0), w=[b_const])
    P("pool", lambda e: e.memset(UPI[:], 1.0), w=[b_const])
    P("pool", lambda e: e.affine_select(out=UPI[:], in_=UPI[:], pattern=[[1, 128]], compare_op=ALU.is_ge,
                                        fill=0.0, base=0, channel_multiplier=-1), r=[b_const], w=[b_const])
    P("pool", lambda e: e.memset(UPI[0:64, 64:128], 0.0), r=[b_const], w=[b_const])
    P("pool", lambda e: e.memset(LOS[:], 1.0), w=[b_const])
    P("pool", lambda e: e.affine_select(out=LOS[:], in_=LOS[:], pattern=[[-1, 128]], compare_op=ALU.is_gt,
                                        fill=0.0, base=0, channel_multiplier=1), r=[b_const], w=[b_const])
    P("pool", lambda e: e.memset(LOS[64:128, 0:64], 0.0), r=[b_const], w=[b_const])
    P("pool", lambda e: e.memset(BLK[:], 0.0), w=[b_const])
    P("pool", lambda e: e.memset(BLK[0:64, 0:64], 1.0), r=[b_const], w=[b_const])
    P("pool", lambda e: e.memset(BLK[64:128, 64:128], 1.0), r=[b_const], w=[b_const])
    P("pool", lambda e: e.memset(CHA[:], 0.0), w=[b_const])
    P("pool", lambda e: e.memset(CHA[0:64, :], 1.0), r=[b_const], w=[b_const])
    P("pool", lambda e: e.memset(CHB[:], 0.0), w=[b_const])
    P("pool", lambda e: e.memset(CHB[64:128, :], 1.0), r=[b_const], w=[b_const])
    kb.dma("sp", flg[:], flag[:, :], w=[b_const])

    def load_vec_fm(vec_ap, n, dst_ap, rbufs=(), wbuf=None):
        v2 = vec_ap.rearrange("(n p) -> n p", p=128)
        with ExitStack() as es:
            old = kb.es; kb.es = es
            for g0 in range(0, n, 128):
                gn = min(128, n - g0)
                st = kb.sb([128, 128], F32, "lv"); pt = kb.ps([128, 128], F32, "lvp")
                bs = Buf("lv"); bp = PB("lvp")
                kb.dma("sp", st[0:gn, :], v2[g0:g0 + gn, :], r=list(rbufs), w=[bs])
                P("pe", lambda e: e.transpose(pt[:, 0:gn], st[0:gn, :], ident[0:gn, 0:gn]), r=[bs, b_const], w=[bp])
                P("dve", lambda e: e.tensor_copy(out=dst_ap[:, g0:g0 + gn], in_=pt[:, 0:gn]), r=[bp], w=[wbuf])
            kb.barrier()
            kb.es = old

    with kb.phase():
        cin = kb.sb([128, 128], F32, "cin"); cact = kb.sb([128, 128], F32, "cact"); cT = kb.sb([128, KC], F32, "cT")
        cps = kb.ps([128, 128], F32, "cps")
        b_c = Buf("c"); b_cp = PB("cp"); b_cT = Buf("cT")
        kb.dma("sp", cin[0:KC, :], cvec[:, :], w=[b_c])
        P("act", lambda e: e.activation(out=cact[0:KC, :], in_=cin[0:KC, :], func=AF.Silu), r=[b_c], w=[b_c])
        P("pe", lambda e: e.transpose(cps[:, 0:KC], cact[0:KC, :], ident[0:KC, 0:KC]), r=[b_c, b_const], w=[b_cp])
        P("dve", lambda e: e.tensor_copy(out=cT[:], in_=cps[:, 0:KC]), r=[b_cp], w=[b_cT])
        KP = min(8, KC)
        NPC = KC // KP
        NAW = 8
        wst = [kb.sb([128, KP, 512], F32, "adaw") for _ in range(NAW)]
        bw = [Buf("adaw%d" % i) for i in range(NAW)]
        mps = [kb.ps([128, 512], F32, "mps") for _ in range(2)]
        bmp = [PB("mps%d" % i) for i in range(2)]
        mst = [kb.sb([1, 512], F32, "mst") for _ in range(2)]
        bms = [Buf("mst%d" % i) for i in range(2)]
        wv = w_ada.rearrange("(kc p) n -> p kc n", p=128)
        li = 0
        for nt in range(6 * D // 512):
            pp = mps[nt % 2]; bpp = bmp[nt % 2]
            for pc in range(NPC):
                wt = wst[li % NAW]; bwt = bw[li % NAW]; li += 1
                kb.dma("sp", wt[:], wv[:, pc * KP:(pc + 1) * KP, nt * 512:(nt + 1) * 512], w=[bwt])
                for j in range(KP):
                    kc = pc * KP + j
                    P("pe", lambda e, kc=kc, j=j, wt=wt, pp=pp: e.matmul(pp[0:1, :], cT[:, kc:kc + 1], wt[:, j, :],
                                                                        start=(kc == 0), stop=(kc == KC - 1)),
                      r=[b_cT, bwt], w=[bpp])
            ms = mst[nt % 2]; bm = bms[nt % 2]
            P("act", lambda e, ms=ms, pp=pp: e.copy(out=ms[:], in_=pp[0:1, :]), r=[bpp], w=[bm])
            kb.dma("pool", MODV[nt * 512:(nt + 1) * 512].rearrange("(o n) -> o n", o=1), ms[:], r=[bm], w=[b_MODV])
        load_vec_fm(MODV, NV, modfm, rbufs=[b_MODV], wbuf=b_mod)
        tmpv = kb.sb([128, NV], F32, "tmpv"); b_tmp = Buf("tmpv")
        load_vec_fm(b_ada, NV, tmpv, wbuf=b_tmp)
        P("dve", lambda e: e.tensor_tensor(out=modfm[:], in0=modfm[:], in1=tmpv[:], op=ALU.add), r=[b_mod, b_tmp], w=[b_mod])
        nv = kb.sb([128, 4, KC], F32, "nv"); b_nv = Buf("nv")
        for i, v in enumerate([n_pre1, n_post1, n_pre2, n_post2]):
            load_vec_fm(v, KC, nv[:, i, :], wbuf=b_nv)
        P("dve", lambda e: e.scalar_tensor_tensor(out=w1s[:], in0=modfm[:, KC:2 * KC], scalar=1.0, in1=nv[:, 0, :],
                                                  op0=ALU.add, op1=ALU.mult), r=[b_mod, b_nv], w=[b_mod])
        P("dve", lambda e: e.scalar_tensor_tensor(out=w2s[:], in0=modfm[:, 4 * KC:5 * KC], scalar=1.0, in1=nv[:, 2, :],
                                                  op0=ALU.add, op1=ALU.mult), r=[b_mod, b_nv], w=[b_mod])
        P("dve", lambda e: e.tensor_tensor(out=g1w[:], in0=modfm[:, 2 * KC:3 * KC], in1=nv[:, 1, :], op=ALU.mult),
          r=[b_mod, b_nv], w=[b_mod])
        P("dve", lambda e: e.tensor_tensor(out=g2w[:], in0=modfm[:, 5 * KC:6 * KC], in1=nv[:, 3, :], op=ALU.mult),
          r=[b_mod, b_nv], w=[b_mod])
    sh1 = modfm[:, 0:KC]; sh2 = modfm[:, 3 * KC:4 * KC]

    cast_rr = [0]

    def prenorm_block(xsrc, bsrc, t0, ntok, XT, bXT, ws, sh, tag):
        with ExitStack() as es:
            old = kb.es; kb.es = es
            xt = [kb.sb([128, D], F32, "xt") for _ in range(2)]; bx = [Buf("xt%d" % i) for i in range(2)]
            junk = kb.sb([128, D], BF16, "junk"); bj = Buf("junk")
            st = [kb.sb([128, 4], F32, "st") for _ in range(2)]; bst = [Buf("st%d" % i) for i in range(2)]
            tp = [kb.ps([128, 512], F32, "tp") for _ in range(2)]; btp = [PB("tp%d" % i) for i in range(2)]
            for i in range(ntok // 128):
                x_ = xt[i % 2]; b_ = bx[i % 2]; s_ = st[i % 2]; bs_ = bst[i % 2]
                kb.dma("sp", x_[:], xsrc[t0 + i * 128:t0 + (i + 1) * 128, :], r=[bsrc], w=[b_])
                P("act", lambda e: e.activation(out=junk[:], in_=x_[:], func=AF.Square, accum_out=s_[:, 0:1]),
                  r=[b_], w=[bj, bs_])
                P("dve", lambda e: e.tensor_scalar(out=s_[:, 1:2], in0=s_[:, 0:1], scalar1=1.0 / D, scalar2=EPS,
                                                   op0=ALU.mult, op1=ALU.add), r=[bs_], w=[bs_])
                P("act", lambda e: e.activation(out=s_[:, 2:3], in_=s_[:, 1:2], func=AF.Sqrt), r=[bs_], w=[bs_])
                P("dve", lambda e: e.reciprocal(out=s_[:, 3:4], in_=s_[:, 2:3]), r=[bs_], w=[bs_])
                P("act", lambda e: e.activation(out=x_[:], in_=x_[:], func=AF.Identity, scale=s_[:, 3:4]),
                  r=[b_, bs_], w=[b_])
                for g in range(KC // 4 if KC >= 4 else 1):
                    pt = tp[g % 2]; bp = btp[g % 2]
                    nq = min(4, KC)
                    for q in range(nq):
                        kc = g * 4 + q
                        P("pe", lambda e, kc=kc, q=q, pt=pt: e.transpose(pt[:, q * 128:(q + 1) * 128],
                                                                         x_[:, kc * 128:(kc + 1) * 128], ident[:]),
                          r=[b_, b_const], w=[bp])
                    for q in range(nq):
                        kc = g * 4 + q
                        eng = "dve" if q % 2 == 0 else "pool"
                        eng = "dve"
                        P(eng, lambda e, kc=kc, q=q, pt=pt: e.tensor_scalar(
                            out=XT[:, kc, i * 128:(i + 1) * 128], in0=pt[:, q * 128:(q + 1) * 128],
                            scalar1=ws[:, kc:kc + 1], scalar2=sh[:, kc:kc + 1], op0=ALU.mult, op1=ALU.add),
                          r=[bp, b_mod], w=[bXT])
            kb.barrier()
            kb.es = old

    class Gemm:
        def __init__(g, kpc, ntok, nacc=2):
            g.kpc = kpc; g.ntok = ntok
            g.nws = 4
            g.wst = [kb.sb([128, kpc, 128], F32, "wst") for _ in range(g.nws)]; g.bws = [Buf("wst%d" % i) for i in range(g.nws)]
            g.wbf = [kb.sb([128, kpc, 128], BF16, "wbf") for _ in range(3)]; g.bwb = [Buf("wbf%d" % i) for i in range(3)]
            g.acc = [kb.ps([128, max(512, ntok)], F32, "acc") for _ in range(nacc)]; g.bacc = [PB("acc%d" % i) for i in range(nacc)]
            g.nacc = nacc
            g.li = 0; g.ai = 0

        def run_multi(g, XT, bXT, KCn, wcols, nf, evacs):
            ntok = g.ntok
            accs = []
            for t in range(nf):
                accs.append((g.acc[g.ai % g.nacc], g.bacc[g.ai % g.nacc])); g.ai += 1
            wv = wcols.rearrange("(kc p) m -> p kc m", p=128)
            kp = g.kpc // nf
            W = nf * 128
            for p0 in range(0, KCn, kp):
                pn = min(kp, KCn - p0)
                ws = g.wst[g.li % g.nws]; bws = g.bws[g.li % g.nws]
                wb = g.wbf[g.li % 3]; bwb = g.bwb[g.li % 3]; g.li += 1
                wsv = ws[:].rearrange("p a b -> p (a b)").rearrange("p (a b) -> p a b", b=W)
                wbv = wb[:].rearrange("p a b -> p (a b)").rearrange("p (a b) -> p a b", b=W)
                kb.dma("sp", wsv[:, 0:pn, :], wv[:, p0:p0 + pn, :], w=[bws])
                ce = ("dve", "act")[cast_rr[0] % 2]; cast_rr[0] += 1
                if ce == "act":
                    P("act", lambda e: e.copy(out=wbv[:, 0:pn, :], in_=wsv[:, 0:pn, :]), r=[bws], w=[bwb])
                else:
                    P(ce, lambda e: e.tensor_copy(out=wbv[:, 0:pn, :], in_=wsv[:, 0:pn, :]), r=[bws], w=[bwb])
                for j in range(pn):
                    kc = p0 + j
                    for t in range(nf):
                        pp, bpp = accs[t]
                        for s0 in range(0, ntok, 512):
                            s1 = min(ntok, s0 + 512)
                            P("pe", lambda e: e.matmul(pp[:, s0:s1], wbv[:, j, t * 128:(t + 1) * 128], XT[:, kc, s0:s1],
                                                      start=(kc == 0), stop=(kc == KCn - 1)), r=[bwb, bXT], w=[bpp])
            for t in range(nf):
                evacs[t](accs[t][0][:, 0:ntok], accs[t][1])

        def run(g, XT, bXT, KCn, wcols, M, evac, start_acc=True):
            ntok = g.ntok
            pp = g.acc[g.ai % 2]; bpp = g.bacc[g.ai % 2]; g.ai += 1
            wv = wcols.rearrange("(kc p) m -> p kc m", p=128)
            for p0 in range(0, KCn, g.kpc):
                pn = min(g.kpc, KCn - p0)
                ws = g.wst[g.li % g.nws]; bws = g.bws[g.li % g.nws]
                wb = g.wbf[g.li % 3]; bwb = g.bwb[g.li % 3]; g.li += 1
                kb.dma(("sp", "act")[g.li % 2] if DUALQ else "sp", ws[:, 0:pn, 0:M], wv[:, p0:p0 + pn, :], w=[bws])
                ce = ("dve", "act")[cast_rr[0] % 2]; cast_rr[0] += 1
                if ce == "act":
                    P("act", lambda e: e.copy(out=wb[:, 0:pn, 0:M], in_=ws[:, 0:pn, 0:M]), r=[bws], w=[bwb])
                else:
                    P(ce, lambda e: e.tensor_copy(out=wb[:, 0:pn, 0:M], in_=ws[:, 0:pn, 0:M]), r=[bws], w=[bwb])
                for j in range(pn):
                    kc = p0 + j
                    for s0 in range(0, ntok, 512):
                        s1 = min(ntok, s0 + 512)
                        P("pe", lambda e, j=j, kc=kc: e.matmul(pp[0:M, s0:s1], wb[:, j, 0:M], XT[:, kc, s0:s1],
                                                              start=(kc == 0), stop=(kc == KCn - 1)),
                          r=[bwb, bXT], w=[bpp])
            evac(pp[0:M, 0:ntok], bpp)

    segs_rec = [(c.o_lx, LW), (c.o_q, 3 * c.KEY), (c.o_a, H), (c.o_b, H)]
    segs_non = [(c.o_lg, LW), (c.o_z, c.VAL), (c.o_gl, D), (c.o_gg, D)]

    def win_pass(xsrc, tcol0, segs):
        NT = min(c.NTW, TOK)
        for blk in range(TOK // NT):
            with kb.phase():
                XT = kb.sb([128, KC, NT], BF16, "XT"); bXT = Buf("XT")
                prenorm_block(xsrc, Buf("xin"), blk * NT, NT, XT, bXT, w1s, sh1, "pn1")
                g = Gemm(min(KC, 32), NT)
                ost = [kb.sb([128, NT], F32, "ost") for _ in range(3)]; bo = [Buf("ost%d" % i) for i in range(3)]
                oi = [0]
                for (o0, wd) in segs:
                    for f0 in range(0, wd, 128):
                        M = min(128, wd - f0)
                        def evac(pap, bpp, o0=o0, f0=f0, M=M):
                            o_ = ost[oi[0] % 3]; b_ = bo[oi[0] % 3]; oi[0] += 1
                            P("act", lambda e: e.copy(out=o_[0:M, :], in_=pap), r=[bpp], w=[b_])
                            kb.dma("pool", PROJ[o0 + f0:o0 + f0 + M, tcol0 + blk * NT:tcol0 + (blk + 1) * NT], o_[0:M, :],
                                   r=[b_], w=[b_PROJ])
                        g.run(XT, bXT, KC, w_in[:, o0 + f0:o0 + f0 + M], M, evac)

    win_pass(x_pre, 0, segs_rec)
    win_pass(x_own, TOK, segs_rec + segs_non)

    PL = c.PL
    with kb.phase():
        cw = kb.sb([128, 4, LB], F32, "lcw"); cb = kb.sb([128, LB], F32, "lcb")
        ab = kb.sb([128, LB], F32, "lab"); ib = kb.sb([128, LB], F32, "lib"); nsp = kb.sb([128, LB], F32, "nsp")
        b_lc = Buf("lruconst")
        for j in range(4):
            load_vec_fm(lru_cw[j, :], LB, cw[:, j, :], wbuf=b_lc)
        load_vec_fm(lru_cb, LB, cb, wbuf=b_lc)
        load_vec_fm(lru_ab, LB, ab, wbuf=b_lc)
        load_vec_fm(lru_ib, LB, ib, wbuf=b_lc)
        load_vec_fm(lru_lam, LB, nsp, wbuf=b_lc)
        P("act", lambda e: e.activation(out=nsp[:], in_=nsp[:], func=AF.Exp, scale=-1.0), r=[b_lc], w=[b_lc])
        P("act", lambda e: e.activation(out=nsp[:], in_=nsp[:], func=AF.Ln, bias=1.0), r=[b_lc], w=[b_lc])
        P("dve", lambda e: e.tensor_scalar(out=nsp[:], in0=nsp[:], scalar1=-8.0, scalar2=None, op0=ALU.mult),
          r=[b_lc], w=[b_lc])
        gw32 = kb.sb([128, 2, 128], F32, "gw32"); b_gw32 = Buf("gw32")
        gwb = [kb.sb([128, 2, 128], BF16, "gwb") for _ in range(2)]; b_gwb = [Buf("gwb%d" % i) for i in range(2)]
        NB = 2
        def tiles(n, shape, dt):
            return [kb.sb(shape, dt, n) for _ in range(NB)], [Buf(n + str(i)) for i in range(NB)]
        xin, bxin = tiles("xin", [128, PL + 3], F32)
        xa, bxa = tiles("xa", [128, PL], F32)
        xab, bxab = tiles("xab", [128, PL], BF16)
        rr, brr = tiles("rr", [128, PL], F32)
        ii, bii = tiles("ii", [128, PL], F32)
        aa, baa = tiles("aa", [128, PL], F32)
        mm, bmm = tiles("mm", [128, PL], F32)
        hh, bhh = tiles("hh", [128, PL], F32)
        gg, bgg = tiles("gg", [128, PL], F32)
        g2, bg2 = tiles("g2", [128, PL], F32)
        yo, byo = tiles("yo", [128, PL], BF16)
        state = kb.sb([128, 1], F32, "lstate"); b_state = Buf("lstate")
        gps = [kb.ps([128, 512], F32, "gps") for _ in range(4)]; bgps = [PB("gps%d" % i) for i in range(4)]
        gi = 0; it = 0
        for ct in range(LB):
            wb_ = gwb[ct % 2]; bwb_ = b_gwb[ct % 2]
            kb.dma("sp", gw32[:, 0, :], lru_aw[ct, :, :], w=[b_gw32])
            kb.dma("sp", gw32[:, 1, :], lru_iw[ct, :, :], w=[b_gw32])
            P("pool", lambda e: e.tensor_copy(out=wb_[:], in_=gw32[:]), r=[b_gw32], w=[bwb_])
            for pc in range(TT2 // PL):
                t0 = pc * PL
                own = t0 >= TOK
                k = it % NB; it += 1
                xi = xin[k]; bxi = bxin[k]
                kb.dma("sp", xi[:, 3:PL + 3], PROJ[c.o_lx + ct * 128:c.o_lx + (ct + 1) * 128, t0:t0 + PL], r=[b_PROJ], w=[bxi])
                if t0 == 0:
                    P("pool", lambda e: e.memset(xi[:, 0:3], 0.0), w=[bxi])
                    P("pool", lambda e: e.memset(state[:], 0.0), w=[b_state])
                else:
                    xp = xin[(k - 1) % NB]; bxp = bxin[(k - 1) % NB]
                    if t0 == TOK:
                        P("dve", lambda e: e.tensor_scalar(out=xi[:, 0:3], in0=xp[:, PL:PL + 3], scalar1=flg[:, 0:1],
                                                           scalar2=None, op0=ALU.mult), r=[bxp, b_const], w=[bxi])
                        P("dve", lambda e: e.tensor_scalar(out=state[:], in0=state[:], scalar1=flg[:, 0:1],
                                                           scalar2=None, op0=ALU.mult), r=[b_state, b_const], w=[b_state])
                    else:
                        P("dve", lambda e: e.tensor_copy(out=xi[:, 0:3], in_=xp[:, PL:PL + 3]), r=[bxp], w=[bxi])
                xa_ = xa[k]; bxa_ = bxa[k]
                P("dve", lambda e: e.tensor_scalar(out=xa_[:], in0=xi[:, 0:PL], scalar1=cw[:, 0, ct:ct + 1],
                                                   scalar2=cb[:, ct:ct + 1], op0=ALU.mult, op1=ALU.add),
                  r=[bxi, b_lc], w=[bxa_])
                for j in range(1, 4):
                    P("dve", lambda e, j=j: e.scalar_tensor_tensor(out=xa_[:], in0=xi[:, j:j + PL], scalar=cw[:, j, ct:ct + 1],
                                                                   in1=xa_[:], op0=ALU.mult, op1=ALU.add),
                      r=[bxi, b_lc, bxa_], w=[bxa_])
                xb = xab[k]; bxb = bxab[k]
                P("pool", lambda e: e.tensor_copy(out=xb[:], in_=xa_[:]), r=[bxa_], w=[bxb])
                r_ = rr[k]; br_ = brr[k]; i_ = ii[k]; bi_ = bii[k]
                for sub in range(0, PL, 512):
                    sn = min(512, PL - sub)
                    for gsel, (dst, bdst, bias) in enumerate([(r_, br_, ab), (i_, bi_, ib)]):
                        pp = gps[gi % 4]; bpp = bgps[gi % 4]; gi += 1
                        P("pe", lambda e, pp=pp, gsel=gsel: e.matmul(pp[:, 0:sn], wb_[:, gsel, :], xb[:, sub:sub + sn],
                                                                     start=True, stop=True), r=[bwb_, bxb], w=[bpp])
                        P("act", lambda e, pp=pp, dst=dst, bias=bias: e.activation(
                            out=dst[:, sub:sub + sn], in_=pp[:, 0:sn], func=AF.Sigmoid, bias=bias[:, ct:ct + 1]),
                          r=[bpp, b_lc], w=[bdst])
                a_ = aa[k]; ba_ = baa[k]; m_ = mm[k]; bm_ = bmm[k]; h_ = hh[k]; bh_ = bhh[k]
                P("act", lambda e: e.activation(out=a_[:], in_=r_[:], func=AF.Exp, scale=nsp[:, ct:ct + 1]),
                  r=[br_, b_lc], w=[ba_])
                P("pool", lambda e: e.tensor_tensor(out=m_[:], in0=a_[:], in1=a_[:], op=ALU.mult), r=[ba_], w=[bm_])
                P("act", lambda e: e.activation(out=m_[:], in_=m_[:], func=AF.Sqrt, scale=-1.0, bias=1.0), r=[bm_], w=[bm_])
                P("pool", lambda e: e.tensor_tensor(out=i_[:], in0=i_[:], in1=xa_[:], op=ALU.mult), r=[bi_, bxa_], w=[bi_])
                P("pool", lambda e: e.tensor_tensor(out=m_[:], in0=m_[:], in1=i_[:], op=ALU.mult), r=[bm_, bi_], w=[bm_])
                P("dve", lambda e: e.tensor_tensor_scan(out=h_[:], data0=a_[:], data1=m_[:], initial=state[:, 0:1],
                                                        op0=ALU.mult, op1=ALU.add), r=[ba_, bm_, b_state], w=[bh_])
                P("dve", lambda e: e.tensor_copy(out=state[:], in_=h_[:, PL - 1:PL]), r=[bh_], w=[b_state])
                if own:
                    g_ = gg[k]; bg_ = bgg[k]; q_ = g2[k]; bq_ = bg2[k]; y_ = yo[k]; by_ = byo[k]
                    kb.dma("sp", g_[:], PROJ[c.o_lg + ct * 128:c.o_lg + (ct + 1) * 128, t0:t0 + PL], r=[b_PROJ], w=[bg_])
                    P("pool", lambda e: e.tensor_tensor(out=q_[:], in0=g_[:], in1=g_[:], op=ALU.mult), r=[bg_], w=[bq_])
                    P("pool", lambda e: e.tensor_scalar(out=q_[:], in0=q_[:], scalar1=0.044715, scalar2=1.0,
                                                        op0=ALU.mult, op1=ALU.add), r=[bq_], w=[bq_])
                    P("pool", lambda e: e.tensor_tensor(out=q_[:], in0=q_[:], in1=g_[:], op=ALU.mult), r=[bq_, bg_], w=[bq_])
                    P("act", lambda e: e.activation(out=q_[:], in_=q_[:], func=AF.Sigmoid, scale=2.0 * 0.7978845608028654),
                      r=[bq_], w=[bq_])
                    P("dve", lambda e: e.tensor_tensor(out=q_[:], in0=q_[:], in1=g_[:], op=ALU.mult), r=[bq_, bg_], w=[bq_])
                    P("dve", lambda e: e.tensor_tensor(out=y_[:], in0=q_[:], in1=h_[:], op=ALU.mult), r=[bq_, bh_], w=[by_])
                    kb.dma("pool", YA[ct * 128:(ct + 1) * 128, t0 - TOK:t0 - TOK + PL], y_[:], r=[by_], w=[b_YA])

    gdn_phase(kb, c, locals())

    tail_phases(kb, c, locals())
    kb.barrier()
    kb.root.close()
    return nc


def gdn_phase(kb, c, L):
    P = L["P"]; load_vec_fm = L["load_vec_fm"]
    ident = L["ident"]; identb = L["identb"]; ones = L["ones"]; UPI = L["UPI"]; LOS = L["LOS"]; BLK = L["BLK"]
    CHA = L["CHA"]; CHB = L["CHB"]; flg = L["flg"]; b_const = L["b_const"]
    PROJ = L["PROJ"]; b_PROJ = L["b_PROJ"]; YB = L["YB"]; b_YB = L["b_YB"]
    gdn_cw = L["gdn_cw"]; gdn_alog = L["gdn_alog"]; gdn_dtb = L["gdn_dtb"]; gdn_onw = L["gdn_onw"]
    H, TOK, SBK, KEY = c.H, c.TOK, c.SBK, c.KEY
    TT2 = 2 * TOK
    nt = SBK // 128
    HG = min(4, H)
    NG = H // HG
    with kb.phase():
        bc = Buf("gconst")
        gcw = kb.sb([128, 4, 3 * H], F32, "gcw")
        for j in range(4):
            load_vec_fm(gdn_cw[j, :], 3 * H, gcw[:, j, :], wbuf=bc)
        onw = kb.sb([128, 1], F32, "onw")
        load_vec_fm(gdn_onw, 1, onw, wbuf=bc)
        dtb = kb.sb([128, 1], F32, "dtb"); negA = kb.sb([128, 1], F32, "negA")
        kb.dma("sp", dtb[0:H, :], gdn_dtb[:, :], w=[bc])
        kb.dma("sp", negA[0:H, :], gdn_alog[:, :], w=[bc])
        P("act", lambda e: e.activation(out=negA[0:H, :], in_=negA[0:H, :], func=AF.Exp), r=[bc], w=[bc])
        P("dve", lambda e: e.tensor_scalar(out=negA[0:H, :], in0=negA[0:H, :], scalar1=-1.0, scalar2=None, op0=ALU.mult),
          r=[bc], w=[bc])
        MASK2 = kb.sb([128, 2, 128], F32, "MASK2")
        P("pool", lambda e: e.tensor_copy(out=MASK2[:, 0, :], in_=LOS[:]), r=[b_const], w=[bc])
        P("pool", lambda e: e.tensor_copy(out=MASK2[:, 1, :], in_=UPI[:]), r=[b_const], w=[bc])
        Sm = kb.sb([128, H, 128], F32, "Sm"); Sb = kb.sb([128, H, 128], BF16, "Sb")
        bSm = [Buf("Sm%d" % h) for h in range(H)]; bSb = [Buf("Sb%d" % h) for h in range(H)]
        qnT = kb.sb([128, H, SBK], BF16, "qnT"); knT = kb.sb([128, H, SBK], BF16, "knT")
        bq = [Buf("qnT%d" % h) for h in range(H)]; bk = [Buf("knT%d" % h) for h in range(H)]
        kbg = kb.sb([128, H, nt, 128], BF16, "kbg"); kdec = kb.sb([128, H, nt, 128], BF16, "kdec")
        vb = kb.sb([128, H, nt, 128], BF16, "vb")
        bkt = [Buf("ktok%d" % h) for h in range(H)]
        zs = kb.sb([128, H, SBK], F32, "zs"); bz = [Buf("zs%d" % h) for h in range(H)]
        ybT = kb.sb([128, H, SBK], BF16, "ybT"); byb = [Buf("ybT%d" % h) for h in range(H)]
        uu = kb.sb([128, H, nt, 128], F32, "uu"); nwT = kb.sb([128, H, nt, 128], BF16, "nwT")
        atT = kb.sb([128, H, nt, 128], BF16, "atT")
        bu = [[Buf("u") for _ in range(nt)] for _ in range(H)]
        bnw = [[Buf("nw") for _ in range(nt)] for _ in range(H)]
        bat = [[Buf("at") for _ in range(nt)] for _ in range(H)]
        sc = kb.sb([128, nt, 8, H], F32, "sc"); bsc = [Buf("sc%d" % j) for j in range(nt)]
        afm = kb.sb([128, 2, SBK], F32, "afm"); bafm = Buf("afm")
        pA = [kb.ps([128, 512], F32, "pA") for _ in range(1)]; bpA = [PB("pA%d" % i) for i in range(1)]
        pT = [kb.ps([128, 4, 128], BF16, "pT") for _ in range(1)]; bpT = [PB("pT%d" % i) for i in range(1)]
        itp = [0]
        def nextT():
            return pT[0], bpT[0]
        pB = [kb.ps([128, 4, 2, 128], F32, "pB") for _ in range(2)]; bpB = [PB("pB%d" % i) for i in range(2)]
        pC = [kb.ps([128, 4, 128], F32, "pC") for _ in range(2)]; bpC = [PB("pC%d" % i) for i in range(2)]
        ia = [0]; ib = [0]; ic = [0]
        def nextA():
            ia[0] += 1; return pA[0], bpA[0]
        def nextB():
            ib[0] += 1; return pB[ib[0] % 2], bpB[ib[0] % 2]
        def nextC():
            ic[0] += 1; return pC[ic[0] % 2], bpC[ic[0] % 2]
        it = [0]

        for sbi in range(TT2 // SBK):
            t0 = sbi * SBK
            own = t0 >= TOK
            kb.dma("sp", afm[0:H, 0, :], PROJ[c.o_a:c.o_a + H, t0:t0 + SBK], r=[b_PROJ], w=[bafm])
            kb.dma("sp", afm[0:H, 1, :], PROJ[c.o_b:c.o_b + H, t0:t0 + SBK], r=[b_PROJ], w=[bafm])
            P("act", lambda e: e.activation(out=afm[0:H, 0, :], in_=afm[0:H, 0, :], func=AF.Exp, bias=dtb[0:H, 0:1]),
              r=[bafm, bc], w=[bafm])
            P("act", lambda e: e.activation(out=afm[0:H, 0, :], in_=afm[0:H, 0, :], func=AF.Ln, bias=1.0), r=[bafm], w=[bafm])
            P("dve", lambda e: e.tensor_scalar(out=afm[0:H, 0, :], in0=afm[0:H, 0, :], scalar1=negA[0:H, 0:1], scalar2=None,
                                               op0=ALU.mult), r=[bafm, bc], w=[bafm])
            P("act", lambda e: e.activation(out=afm[0:H, 1, :], in_=afm[0:H, 1, :], func=AF.Sigmoid), r=[bafm], w=[bafm])
            for j in range(nt):
                pt, bpt = nextA()
                for q in range(2):
                    P("pe", lambda e: e.transpose(pt[:, q * H:(q + 1) * H], afm[0:H, q, j * 128:(j + 1) * 128], ident[0:H, 0:H]),
                      r=[bafm, b_const], w=[bpt])
                P("dve", lambda e: e.tensor_copy(out=sc[:, j, 0:2, :], in_=pt[:, 0:2 * H].rearrange("p (a h) -> p a h", a=2)),
                  r=[bpt], w=[bsc[j]])
                pt2, bpt2 = nextA()
                for q, mt in enumerate([UPI, BLK, CHA, CHB]):
                    P("pe", lambda e: e.matmul(pt2[:, q * H:(q + 1) * H], mt[:], sc[:, j, 0, :], start=True, stop=True),
                      r=[bsc[j], b_const], w=[bpt2])
                P("act", lambda e: e.activation(out=sc[:, j, 2, :], in_=pt2[:, 0:H], func=AF.Exp), r=[bpt2], w=[bsc[j]])
                P("dve", lambda e: e.tensor_copy(out=sc[:, j, 4, :], in_=pt2[:, 0:H]), r=[bpt2], w=[bsc[j]])
                P("dve", lambda e: e.tensor_tensor(out=sc[:, j, 3, :], in0=pt2[:, H:2 * H], in1=sc[:, j, 4, :], op=ALU.subtract),
                  r=[bpt2, bsc[j]], w=[bsc[j]])
                P("act", lambda e: e.activation(out=sc[:, j, 3, :], in_=sc[:, j, 3, :], func=AF.Exp), r=[bsc[j]], w=[bsc[j]])
                P("act", lambda e: e.activation(out=sc[:, j, 6:8, :], in_=pt2[:, 2 * H:4 * H].rearrange("p (a h) -> p a h", a=2),
                                                func=AF.Exp), r=[bpt2], w=[bsc[j]])
                P("dve", lambda e: e.tensor_tensor(out=sc[:, j, 4, :], in0=sc[:, j, 1, :], in1=sc[:, j, 2, :], op=ALU.mult),
                  r=[bsc[j]], w=[bsc[j]])
                P("dve", lambda e: e.tensor_scalar(out=sc[:, j, 5, :], in0=sc[:, j, 1, :], scalar1=-1.0, scalar2=None,
                                                   op0=ALU.mult), r=[bsc[j]], w=[bsc[j]])
            with kb.scope():
                xin_g = [kb.sb([128, 3, HG, SBK + 3], F32, "xin_g") for _ in range(2)]; bxg = [Buf("xin_g%d" % i) for i in range(2)]
                cv_g = [kb.sb([128, 3, HG, SBK], F32, "cv_g") for _ in range(2)]; bcg = [Buf("cv_g%d" % i) for i in range(2)]
                sq_g = kb.sb([128, 2, HG, SBK], F32, "sq_g"); bsg = Buf("sq_g")
                rn_g = kb.sb([128, 2, HG, SBK], F32, "rn_g"); brg = Buf("rn_g")
                for g in range(NG):
                    h0 = g * HG
                    xg = xin_g[g % 2]; bx = bxg[g % 2]; cg = cv_g[g % 2]; bcv_ = bcg[g % 2]
                    segs = (0, 1, 2) if own else (1, 2)
                    s_lo = segs[0]
                    for seg in segs:
                        row0 = c.o_q + seg * KEY + h0 * 128
                        if t0 == 0:
                            kb.dma("sp", xg[:, seg, :, 3:SBK + 3],
                                   PROJ[row0:row0 + HG * 128, t0:t0 + SBK].rearrange("(hh p) t -> p hh t", p=128), r=[b_PROJ], w=[bx])
                            P("pool", lambda e: e.memset(xg[:, seg, :, 0:3], 0.0), w=[bx])
                        else:
                            kb.dma("sp", xg[:, seg, :, 0:SBK + 3],
                                   PROJ[row0:row0 + HG * 128, t0 - 3:t0 + SBK].rearrange("(hh p) t -> p hh t", p=128), r=[b_PROJ], w=[bx])
                            if t0 == TOK:
                                P("dve", lambda e: e.tensor_scalar(out=xg[:, seg, :, 0:3], in0=xg[:, seg, :, 0:3], scalar1=flg[:, 0:1],
                                                                   scalar2=None, op0=ALU.mult), r=[bx, b_const], w=[bx])
                    for j in range(4):
                        for seg in segs:
                            for hh in range(HG):
                                idx = seg * H + h0 + hh
                                if j == 0:
                                    P("dve", lambda e: e.tensor_scalar(out=cg[:, seg, hh, :], in0=xg[:, seg, hh, 0:SBK], scalar1=gcw[:, 0, idx:idx + 1],
                                                                       scalar2=None, op0=ALU.mult), r=[bx, bc], w=[bcv_])
                                else:
                                    P("dve", lambda e: e.scalar_tensor_tensor(out=cg[:, seg, hh, :], in0=xg[:, seg, hh, j:j + SBK],
                                                                              scalar=gcw[:, j, idx:idx + 1], in1=cg[:, seg, hh, :],
                                                                              op0=ALU.mult, op1=ALU.add), r=[bx, bc, bcv_], w=[bcv_])
                    P("act", lambda e: e.activation(out=cg[:, s_lo:3, :, :], in_=cg[:, s_lo:3, :, :], func=AF.Silu), r=[bcv_], w=[bcv_])
                    if own:
                        zr = c.o_z + h0 * 128
                        kb.dma("sp", zs[:, h0:h0 + HG, :], PROJ[zr:zr + HG * 128, t0:t0 + SBK].rearrange("(hh p) t -> p hh t", p=128),
                               r=[b_PROJ], w=[bz[h] for h in range(h0, h0 + HG)])
                        P("act", lambda e: e.activation(out=zs[:, h0:h0 + HG, :], in_=zs[:, h0:h0 + HG, :], func=AF.Silu),
                          r=[bz[h] for h in range(h0, h0 + HG)], w=[bz[h] for h in range(h0, h0 + HG)])
                    nqk = 2 - s_lo
                    P("pool", lambda e: e.tensor_tensor(out=sq_g[:, 0:nqk, :, :], in0=cg[:, s_lo:2, :, :], in1=cg[:, s_lo:2, :, :], op=ALU.mult),
                      r=[bcv_], w=[bsg])
                    sqf = sq_g[:].rearrange("p a b c -> p (a b c)")
                    rnf = rn_g[:].rearrange("p a b c -> p (a b c)")
                    ncol = nqk * HG * SBK
                    for c0 in range(0, ncol, 1024):
                        cn = min(1024, ncol - c0)
                        pb, bpb = nextB()
                        pbf = pb[:].rearrange("p a b c -> p (a b c)")
                        for s0 in range(0, cn, 512):
                            sn = min(512, cn - s0)
                            P("pe", lambda e: e.matmul(pbf[:, s0:s0 + sn], ones[:], sqf[:, c0 + s0:c0 + s0 + sn], start=True, stop=True),
                              r=[bsg, b_const], w=[bpb])
                        P("act", lambda e: e.activation(out=rnf[:, c0:c0 + cn], in_=pbf[:, 0:cn], func=AF.Ln, bias=EPS), r=[bpb], w=[brg])
                    P("act", lambda e: e.activation(out=rn_g[:, 0:nqk, :, :], in_=rn_g[:, 0:nqk, :, :], func=AF.Exp, scale=-0.5), r=[brg], w=[brg])
                    if own:
                        P("dve", lambda e: e.scalar_tensor_tensor(out=qnT[:, h0:h0 + HG, :], in0=cg[:, 0, :, :], scalar=float(128.0 ** -0.5),
                                                                  in1=rn_g[:, 0, :, :], op0=ALU.mult, op1=ALU.mult),
                          r=[bcv_, brg], w=[bq[h] for h in range(h0, h0 + HG)])
                    P("dve", lambda e: e.tensor_tensor(out=cg[:, 1, :, :], in0=cg[:, 1, :, :], in1=rn_g[:, nqk - 1, :, :], op=ALU.mult),
                      r=[bcv_, brg], w=[bcv_])
                    P("pool", lambda e: e.tensor_copy(out=knT[:, h0:h0 + HG, :], in_=cg[:, 1, :, :]), r=[bcv_],
                      w=[bk[h] for h in range(h0, h0 + HG)])
                    for j in range(nt):
                        for seg in (1, 2):
                            pt, bpt = nextC()
                            for hh in range(HG):
                                P("pe", lambda e: e.transpose(pt[:, hh, :], cg[:, seg, hh, j * 128:(j + 1) * 128], ident[:]),
                                  r=[bcv_, b_const], w=[bpt])
                            def bcs(q):
                                return sc[:, j, q, h0:h0 + HG].unsqueeze(2).to_broadcast([128, HG, 128])
                            wk = [bkt[h] for h in range(h0, h0 + HG)]
                            if seg == 1:
                                P("dve", lambda e: e.tensor_tensor(out=kbg[:, h0:h0 + HG, j, :], in0=pt[:, 0:HG, :], in1=bcs(4), op=ALU.mult),
                                  r=[bpt, bsc[j]], w=wk)
                                P("dve", lambda e: e.tensor_tensor(out=kdec[:, h0:h0 + HG, j, :], in0=pt[:, 0:HG, :], in1=bcs(3), op=ALU.mult),
                                  r=[bpt, bsc[j]], w=wk)
                            else:
                                P("dve", lambda e: e.tensor_tensor(out=vb[:, h0:h0 + HG, j, :], in0=pt[:, 0:HG, :], in1=bcs(1), op=ALU.mult),
                                  r=[bpt, bsc[j]], w=wk)
            with kb.scope():
                lhsD = kb.sb([128, H, 128], F32, "lhsD"); blD = [Buf("lhsD") for _ in range(NG)]
                E2 = kb.sb([128, H, 2, 128], F32, "E2"); bE2 = [Buf("E2") for _ in range(NG)]
                Nc = [kb.sb([128, H, 2, 128], BF16, "Nc") for _ in range(2)]
                bNc = [[Buf("Nc") for _ in range(NG)] for _ in range(2)]
                Pm = [kb.sb([128, H, 128], BF16, "Pm") for _ in range(2)]; bPm = [[Buf("Pm") for _ in range(NG)] for _ in range(2)]
                vnew = kb.sb([128, H, 128], BF16, "vnew"); bvn = [Buf("vnew") for _ in range(NG)]
                o2s = kb.sb([128, H, 128], F32, "o2s"); bo2 = [Buf("o2s") for _ in range(NG)]
                oo = kb.sb([128, H, 128], F32, "oo"); boo = Buf("oo")
                osq = kb.sb([128, H, 128], F32, "osq"); bosq = Buf("osq")
                oss = kb.sb([128, 4, H], F32, "oss"); boss = Buf("oss")
                for j in range(nt):
                    def hs(g):
                        return range(g * HG, (g + 1) * HG)
                    cs = slice(j * 128, (j + 1) * 128)
                    for g in range(NG):
                        h0 = g * HG
                        for h in hs(g):
                            P("pool", lambda e: e.tensor_scalar(out=lhsD[:, h, :], in0=UPI[:], scalar1=sc[:, j, 0, h:h + 1], scalar2=None,
                                                                op0=ALU.mult), r=[b_const, bsc[j]], w=[blD[g]])
                        pb, bpb = nextB()
                        for h in hs(g):
                            P("pe", lambda e: e.matmul(pb[:, h - h0, 0, :], lhsD[:, h, :], LOS[:], start=True, stop=True),
                              r=[blD[g], b_const], w=[bpb])
                            P("pe", lambda e: e.matmul(pb[:, h - h0, 1, :], LOS[:], lhsD[:, h, :], start=True, stop=True),
                              r=[blD[g], b_const], w=[bpb])
                        P("act", lambda e: e.activation(out=E2[:, h0:h0 + HG, :, :], in_=pb[:, 0:HG, :, :], func=AF.Exp), r=[bpb], w=[bE2[g]])
                        for h in hs(g):
                            P("pool", lambda e: e.tensor_tensor(out=E2[:, h, :, :], in0=E2[:, h, :, :], in1=MASK2[:], op=ALU.mult),
                              r=[bE2[g], bc], w=[bE2[g]])
                        if G2S < 1:
                            continue
                        pb, bpb = nextB()
                        for h in hs(g):
                            P("pe", lambda e: e.matmul(pb[:, h - h0, 0, :], knT[:, h, cs], knT[:, h, cs], start=True, stop=True),
                              r=[bk[h]], w=[bpb])
                            if own:
                                P("pe", lambda e: e.matmul(pb[:, h - h0, 1, :], knT[:, h, cs], qnT[:, h, cs], start=True, stop=True),
                                  r=[bk[h], bq[h]], w=[bpb])
                        for h in hs(g):
                            P("dve", lambda e: e.scalar_tensor_tensor(out=Nc[0][:, h, 0, :], in0=pb[:, h - h0, 0, :], scalar=sc[:, j, 5, h:h + 1],
                                                                      in1=E2[:, h, 0, :], op0=ALU.mult, op1=ALU.mult),
                              r=[bpb, bsc[j], bE2[g]], w=[bNc[0][g]])
                        if own:
                            P("dve", lambda e: e.tensor_tensor(out=atT[:, h0:h0 + HG, j, :], in0=pb[:, 0:HG, 1, :], in1=E2[:, h0:h0 + HG, 1, :],
                                                               op=ALU.mult), r=[bpb, bE2[g]], w=[bat[h][j] for h in hs(g)])
                    if G2S < 2:
                        continue
                    for g in range(NG):
                        h0 = g * HG
                        pc_ = nextT()
                        for h in hs(g):
                            P("pe", lambda e: e.transpose(pc_[0][:, h - h0, :], Nc[0][:, h, 0, :], identb[:]), r=[bNc[0][g], b_const], w=[pc_[1]])
                        P("act", lambda e: e.copy(out=Nc[0][:, h0:h0 + HG, 1, :], in_=pc_[0][:, 0:HG, :]), r=[pc_[1]], w=[bNc[0][g]])
                        for h in hs(g):
                            P("pool", lambda e: e.tensor_tensor(out=Pm[0][:, h, :], in0=Nc[0][:, h, 1, :], in1=identb[:], op=ALU.add),
                              r=[bNc[0][g], b_const], w=[bPm[0][g]])
                    cur = 0
                    if G2S < 3:
                        continue
                    for lvl in range(1, 6):
                        nx = 1 - cur
                        for g in range(NG):
                            h0 = g * HG
                            pb, bpb = nextB()
                            for h in hs(g):
                                P("pe", lambda e: e.matmul(pb[:, h - h0, 0, :], Nc[cur][:, h, 1, :], Nc[cur][:, h, 0, :], start=True, stop=True),
                                  r=[bNc[cur][g]], w=[bpb])
                                if lvl < 5:
                                    P("pe", lambda e: e.matmul(pb[:, h - h0, 1, :], Nc[cur][:, h, 0, :], Nc[cur][:, h, 1, :], start=True, stop=True),
                                      r=[bNc[cur][g]], w=[bpb])
                            if lvl < 5:
                                P("act", lambda e: e.copy(out=Nc[nx][:, h0:h0 + HG, :, :], in_=pb[:, 0:HG, :, :]), r=[bpb], w=[bNc[nx][g]])
                            else:
                                P("act", lambda e: e.copy(out=Nc[nx][:, h0:h0 + HG, 0, :], in_=pb[:, 0:HG, 0, :]), r=[bpb], w=[bNc[nx][g]])
                            pc2, bpc2 = nextC()
                            for h in hs(g):
                                P("pe", lambda e: e.matmul(pc2[:, h - h0, :], Nc[nx][:, h, 0, :], Pm[cur][:, h, :], start=True, stop=True),
                                  r=[bNc[nx][g], bPm[cur][g]], w=[bpc2])
                            P("dve", lambda e: e.tensor_tensor(out=Pm[nx][:, h0:h0 + HG, :], in0=pc2[:, 0:HG, :], in1=Pm[cur][:, h0:h0 + HG, :],
                                                               op=ALU.add), r=[bpc2, bPm[cur][g]], w=[bPm[nx][g]])
                        cur = nx
                    if G2S < 4:
                        continue
                    for g in range(NG):
                        h0 = g * HG
                        pb, bpb = nextB()
                        for h in hs(g):
                            P("pe", lambda e: e.matmul(pb[:, h - h0, 0, :], Pm[cur][:, h, :], vb[:, h, j, :], start=True, stop=True),
                              r=[bPm[cur][g], bkt[h]], w=[bpb])
                            if G2S >= 5:
                                P("pe", lambda e: e.matmul(pb[:, h - h0, 1, :], kbg[:, h, j, :], Pm[cur][:, h, :], start=True, stop=True),
                                  r=[bPm[cur][g], bkt[h]], w=[bpb])
                        if G2S >= 6:
                            P("act", lambda e: e.copy(out=uu[:, h0:h0 + HG, j, :], in_=pb[:, 0:HG, 0, :]), r=[bpb], w=[bu[h][j] for h in hs(g)])
                        if G2S >= 7:
                            P("dve", lambda e: e.tensor_scalar(out=nwT[:, h0:h0 + HG, j, :], in0=pb[:, 0:HG, 1, :], scalar1=-1.0, scalar2=None,
                                                           op0=ALU.mult), r=[bpb], w=[bnw[h][j] for h in hs(g)])
                if t0 == 0:
                    P("pool", lambda e: e.memset(Sm[:], 0.0), w=bSm)
                    P("pool", lambda e: e.memset(Sb[:], 0.0), w=bSb)
                elif t0 == TOK:
                    P("dve", lambda e: e.tensor_scalar(out=Sm[:], in0=Sm[:], scalar1=flg[:, 0:1], scalar2=None, op0=ALU.mult),
                      r=bSm + [b_const], w=bSm)
                    P("act", lambda e: e.copy(out=Sb[:], in_=Sm[:]), r=bSm, w=bSb)
                for j in range(nt):
                    cs0 = j * 128
                    po1 = [None] * NG
                    for half in range(2):
                        rs = slice(half * 64, half * 64 + 64)
                        cols = slice(cs0 + half * 64, cs0 + half * 64 + 64)
                        for g in range(NG):
                            h0 = g * HG
                            pw, bpw = nextC()
                            for h in hs(g):
                                P("pe", lambda e: e.matmul(pw[rs, h - h0, :], nwT[:, h, j, rs], Sb[:, h, :], start=True, stop=True),
                                  r=[bnw[h][j], bSb[h]], w=[bpw])
                            P("dve", lambda e: e.tensor_tensor(out=vnew[rs, h0:h0 + HG, :], in0=pw[rs, 0:HG, :], in1=uu[rs, h0:h0 + HG, j, :],
                                                               op=ALU.add), r=[bpw] + [bu[h][j] for h in hs(g)], w=[bvn[g]])
                            if own:
                                pq, bpq = nextC()
                                for h in hs(g):
                                    P("pe", lambda e: e.matmul(pq[rs, h - h0, :], qnT[:, h, cols], Sb[:, h, :], start=True, stop=True),
                                      r=[bq[h], bSb[h]], w=[bpq])
                                for h in hs(g):
                                    P("dve", lambda e: e.tensor_scalar(out=o2s[rs, h, :], in0=pq[rs, h - h0, :], scalar1=sc[rs, j, 2, h:h + 1],
                                                                       scalar2=None, op0=ALU.mult), r=[bpq, bsc[j]], w=[bo2[g]])
                            psd, bpsd = nextC()
                            for h in hs(g):
                                P("pe", lambda e: e.matmul(psd[:, h - h0, :], kdec[rs, h, j, :], vnew[rs, h, :], start=True, stop=True),
                                  r=[bkt[h], bvn[g]], w=[bpsd])
                            for h in hs(g):
                                P("dve", lambda e: e.scalar_tensor_tensor(out=Sm[:, h, :], in0=Sm[:, h, :], scalar=sc[:, j, 6 + half, h:h + 1],
                                                                          in1=psd[:, h - h0, :], op0=ALU.mult, op1=ALU.add),
                                  r=[bSm[h], bsc[j], bpsd], w=[bSm[h]])
                            P("act", lambda e: e.copy(out=Sb[:, h0:h0 + HG, :], in_=Sm[:, h0:h0 + HG, :]), r=[bSm[h] for h in hs(g)],
                              w=[bSb[h] for h in hs(g)])
                    if own:
                        for g in range(NG):
                            h0 = g * HG
                            po2, bpo2 = nextC()
                            for h in hs(g):
                                P("pe", lambda e: e.matmul(po2[:, h - h0, :], atT[:, h, j, :], vnew[:, h, :], start=True, stop=True),
                                  r=[bat[h][j], bvn[g]], w=[bpo2])
                            P("dve", lambda e: e.tensor_tensor(out=oo[:, h0:h0 + HG, :], in0=po2[:, 0:HG, :], in1=o2s[:, h0:h0 + HG, :], op=ALU.add),
                              r=[bpo2, bo2[g]], w=[boo])
                        P("pool", lambda e: e.tensor_tensor(out=osq[:], in0=oo[:], in1=oo[:], op=ALU.mult), r=[boo], w=[bosq])
                        P("dve", lambda e: e.tensor_reduce(out=oss[:, 0, :], in_=osq[:], axis=mybir.AxisListType.X, op=ALU.add),
                          r=[bosq], w=[boss])
                        P("dve", lambda e: e.tensor_scalar(out=oss[:, 1, :], in0=oss[:, 0, :], scalar1=1.0 / 128.0, scalar2=EPS,
                                                           op0=ALU.mult, op1=ALU.add), r=[boss], w=[boss])
                        P("act", lambda e: e.activation(out=oss[:, 2, :], in_=oss[:, 1, :], func=AF.Sqrt), r=[boss], w=[boss])
                        P("dve", lambda e: e.reciprocal(out=oss[:, 3, :], in_=oss[:, 2, :]), r=[boss], w=[boss])
                        for h in range(H):
                            P("dve", lambda e: e.tensor_scalar(out=oo[:, h, :], in0=oo[:, h, :], scalar1=oss[:, 3, h:h + 1], scalar2=None,
                                                               op0=ALU.mult), r=[boo, boss], w=[boo])
                        for g in range(NG):
                            h0 = g * HG
                            pt, bpt = nextA()
                            for h in hs(g):
                                P("pe", lambda e: e.transpose(pt[:, (h - h0) * 128:(h - h0 + 1) * 128], oo[:, h, :], ident[:]),
                                  r=[boo, b_const], w=[bpt])
                            for h in hs(g):
                                P("dve", lambda e: e.scalar_tensor_tensor(out=ybT[:, h, cs0:cs0 + 128], in0=pt[:, (h - h0) * 128:(h - h0 + 1) * 128],
                                                                          scalar=onw[:, 0:1], in1=zs[:, h, cs0:cs0 + 128],
                                                                          op0=ALU.mult, op1=ALU.mult), r=[bpt, bc, bz[h]], w=[byb[h]])
            if own:
                for h in range(H):
                    kb.dma("pool", YB[h * 128:(h + 1) * 128, t0 - TOK:t0 - TOK + SBK], ybT[:, h, :], r=[byb[h]], w=[b_YB])


_pT = {}


def kb_ps_bf16(kb, name):
    key = id(kb.es)
    if key not in _pT:
        _pT.clear()
        _pT[key] = ([kb.ps([128, 4, 128], BF16, name) for _ in range(2)], [Buf(name + str(i)) for i in range(2)], [0])
    tl, bl, ctr = _pT[key]
    ctr[0] += 1
    return tl[ctr[0] % 2], bl[ctr[0] % 2]


def tail_phases(kb, c, L):
    P = L["P"]; Gemm = L["Gemm"]; prenorm_block = L["prenorm_block"]
    ident = L["ident"]; ones = L["ones"]; b_const = L["b_const"]; b_mod = L["b_mod"]
    PROJ = L["PROJ"]; b_PROJ = L["b_PROJ"]; YA = L["YA"]; b_YA = L["b_YA"]; YB = L["YB"]; b_YB = L["b_YB"]
    X1 = L["X1"]; b_X1 = L["b_X1"]; ACTT = L["ACTT"]; b_ACTT = L["b_ACTT"]
    w_bl = L["w_bl"]; w_bg = L["w_bg"]; w_out = L["w_out"]; w_up = L["w_up"]; w_dn = L["w_dn"]
    x_own = L["x_own"]; out = L["out"]; b_out = L["b_out"]
    g1w = L["g1w"]; g2w = L["g2w"]; w2s = L["w2s"]; sh2 = L["sh2"]
    D, KC, TOK, NT, LB, H = c.D, c.KC, c.TOK, c.NT, c.LB, c.H
    nt = NT // 128
    NT1 = min(c.NTW, TOK)
    YG = kb.dram("yg", [D, max(NT, NT1)], F32)
    b_YG = Buf("YG")

    def out_gemm_and_epilogue(g, XT, bXT, KCn, wmat, gw, xres, bxres, xres_row0, dst, bdst, dst_row0, NT, nfp=1):
        nt = NT // 128
        rstd = kb.sb([128, nt, 4], F32, "rstd"); brs = Buf("rstd")
        sss = kb.sb([128, NT], F32, "sss"); bsss = Buf("sss")
        es_in = ExitStack(); old_es = kb.es; kb.es = es_in
        g = g()
        ssb = kb.ps([128, max(512, NT)], F32, "ssb"); bssb = PB("ssb")
        ysq = [kb.sb([128, NT], F32, "ysq") for _ in range(2)]; bys = [Buf("ysq%d" % i) for i in range(2)]
        ygs = [kb.sb([128, NT], F32, "ygs") for _ in range(2)]; byg = [Buf("ygs%d" % i) for i in range(2)]
        def mk_evac(f):
            def evac(pap, bpp, f=f):
                q_ = ysq[f % 2]; bq_ = bys[f % 2]; y_ = ygs[f % 2]; by_ = byg[f % 2]
                P("act", lambda e: e.activation(out=q_[:], in_=pap, func=AF.Square), r=[bpp], w=[bq_])
                P("act", lambda e: e.activation(out=y_[:], in_=pap, func=AF.Copy, scale=gw[:, f:f + 1]), r=[bpp, b_mod], w=[by_])
                for s0 in range(0, NT, 512):
                    s1 = min(NT, s0 + 512)
                    P("pe", lambda e: e.matmul(ssb[:, s0:s1], ones[:], q_[:, s0:s1], start=(f == 0), stop=(f == KC - 1)),
                      r=[bq_, b_const], w=[bssb])
                kb.dma("pool", YG[f * 128:(f + 1) * 128, 0:NT], y_[:], r=[by_], w=[b_YG])
            return evac
        if nfp > 1:
            for f in range(0, KC, nfp):
                g.run_multi(XT, bXT, KCn, wmat[:, f * 128:(f + nfp) * 128], nfp, [mk_evac(f + t) for t in range(nfp)])
        else:
            for f in range(KC):
                g.run(XT, bXT, KCn, wmat[:, f * 128:(f + 1) * 128], 128, mk_evac(f))
        P("act", lambda e: e.copy(out=sss[:], in_=ssb[:, 0:NT]), r=[bssb], w=[bsss])
        kb.barrier(); es_in.close(); kb.es = old_es
        pss = kb.ps([128, 512], F32, "pss"); bpss = PB("pss")
        for i in range(nt):
            P("pe", lambda e: e.transpose(pss[:, 0:128], sss[:, i * 128:(i + 1) * 128], ident[:]), r=[bsss, b_const], w=[bpss])
            P("dve", lambda e: e.tensor_scalar(out=rstd[:, i, 0:1], in0=pss[:, 0:1], scalar1=1.0 / D, scalar2=EPS, op0=ALU.mult, op1=ALU.add),
              r=[bpss], w=[brs])
            P("act", lambda e: e.activation(out=rstd[:, i, 1:2], in_=rstd[:, i, 0:1], func=AF.Sqrt), r=[brs], w=[brs])
            P("dve", lambda e: e.reciprocal(out=rstd[:, i, 2:3], in_=rstd[:, i, 1:2]), r=[brs], w=[brs])
        xt = [kb.sb([128, D], F32, "ext") for _ in range(2)]; bxt = [Buf("ext%d" % i) for i in range(2)]
        ygt = [kb.sb([128, KC, 128], F32, "ygt") for _ in range(2)]; bygt = [Buf("ygt%d" % i) for i in range(2)]
        tp = [kb.ps([128, 512], F32, "etp") for _ in range(2)]; btp = [PB("etp%d" % i) for i in range(2)]
        YGv = YG.rearrange("(kc p) t -> p kc t", p=128)
        ti = 0
        for i in range(nt):
            x_ = xt[i % 2]; bx_ = bxt[i % 2]; yt = ygt[i % 2]; byt = bygt[i % 2]
            kb.dma("sp", x_[:], xres[xres_row0 + i * 128:xres_row0 + (i + 1) * 128, :], r=[bxres], w=[bx_])
            kb.dma("act" if DUALQ else "sp", yt[:], YGv[:, :, i * 128:(i + 1) * 128], r=[b_YG], w=[byt])
            for g0 in range(0, KC, 4):
                pt = tp[ti % 2]; bp = btp[ti % 2]; ti += 1
                for q in range(4):
                    P("pe", lambda e: e.transpose(pt[:, q * 128:(q + 1) * 128], yt[:, g0 + q, :], ident[:]), r=[byt, b_const], w=[bp])
                P("dve", lambda e: e.scalar_tensor_tensor(out=x_[:, g0 * 128:(g0 + 4) * 128], in0=pt[:], scalar=rstd[:, i, 2:3],
                                                          in1=x_[:, g0 * 128:(g0 + 4) * 128], op0=ALU.mult, op1=ALU.add),
                  r=[bp, brs, bx_], w=[bx_])
            kb.dma("pool", dst[dst_row0 + i * 128:dst_row0 + (i + 1) * 128, :], x_[:], r=[bx_], w=[bdst])

    b_xown = Buf("x_own")
    for blk in range(TOK // NT1):
        c0 = blk * NT1
        with kb.phase():
            MT = kb.sb([128, KC, NT1], BF16, "MT"); bMT = Buf("MT")
            with kb.phase():
                XA = kb.sb([128, LB, NT1], BF16, "XA"); bXA = Buf("XA")
                XB = kb.sb([128, H, NT1], BF16, "XB"); bXB = Buf("XB")
                kb.dma("sp", XA[:], YA.rearrange("(kc p) t -> p kc t", p=128)[:, :, c0:c0 + NT1], r=[b_YA], w=[bXA])
                kb.dma("sp", XB[:], YB.rearrange("(kc p) t -> p kc t", p=128)[:, :, c0:c0 + NT1], r=[b_YB], w=[bXB])
                g = Gemm(min(16, LB), NT1)
                gl = [kb.sb([128, NT1], F32, "gl") for _ in range(2)]; bgl = [Buf("gl%d" % i) for i in range(2)]
                gg_ = [kb.sb([128, NT1], F32, "gg") for _ in range(2)]; bgg_ = [Buf("gg%d" % i) for i in range(2)]
                m1 = [kb.sb([128, NT1], F32, "m1") for _ in range(2)]; bm1 = [Buf("m1%d" % i) for i in range(2)]
                for f in range(KC):
                    a_ = gl[f % 2]; ba_ = bgl[f % 2]; b_ = gg_[f % 2]; bb_ = bgg_[f % 2]; m_ = m1[f % 2]; bm_ = bm1[f % 2]
                    kb.dma("sp", a_[:], PROJ[c.o_gl + f * 128:c.o_gl + (f + 1) * 128, TOK + c0:TOK + c0 + NT1], r=[b_PROJ], w=[ba_])
                    kb.dma("sp", b_[:], PROJ[c.o_gg + f * 128:c.o_gg + (f + 1) * 128, TOK + c0:TOK + c0 + NT1], r=[b_PROJ], w=[bb_])
                    P("act", lambda e: e.activation(out=a_[:], in_=a_[:], func=AF.Sigmoid), r=[ba_], w=[ba_])
                    P("act", lambda e: e.activation(out=b_[:], in_=b_[:], func=AF.Sigmoid), r=[bb_], w=[bb_])
                    def evA(pap, bpp):
                        P("dve", lambda e: e.tensor_tensor(out=m_[:], in0=pap, in1=a_[:], op=ALU.mult), r=[bpp, ba_], w=[bm_])
                    def evB(pap, bpp):
                        P("dve", lambda e: e.tensor_tensor(out=b_[:], in0=pap, in1=b_[:], op=ALU.mult), r=[bpp, bb_], w=[bb_])
                        P("pool", lambda e: e.tensor_tensor(out=MT[:, f, :], in0=m_[:], in1=b_[:], op=ALU.add), r=[bm_, bb_], w=[bMT])
                    g.run(XA, bXA, LB, w_bl[:, f * 128:(f + 1) * 128], 128, evA)
                    g.run(XB, bXB, H, w_bg[:, f * 128:(f + 1) * 128], 128, evB)
            with kb.phase():
                out_gemm_and_epilogue(lambda: Gemm(min(16, KC), NT1), MT, bMT, KC, w_out, g1w, x_own, b_xown, c0, X1, b_X1, c0, NT1)

    NTU = min(c.NTW, TOK)
    for blk in range(TOK // NTU):
        c0 = blk * NTU
        with kb.phase():
            XT = kb.sb([128, KC, NTU], BF16, "XT2"); bXT = Buf("XT2")
            prenorm_block(X1, b_X1, c0, NTU, XT, bXT, w2s, sh2, "pn2")
            g = Gemm(min(32, KC), NTU)
            rl = [kb.sb([128, NTU], F32, "rl") for _ in range(2)]; brl = [Buf("rl%d" % i) for i in range(2)]
            ao = [kb.sb([128, NTU], BF16, "ao") for _ in range(3)]; bao = [Buf("ao%d" % i) for i in range(3)]
            for f in range(c.DFF // 128):
                def evac(pap, bpp, f=f):
                    r_ = rl[f % 2]; br_ = brl[f % 2]; a_ = ao[f % 3]; ba_ = bao[f % 3]
                    P("act", lambda e: e.activation(out=r_[:], in_=pap, func=AF.Relu), r=[bpp], w=[br_])
                    eng = "pool" if f % 2 == 0 else "dve"
                    P(eng, lambda e: e.tensor_tensor(out=a_[:], in0=r_[:], in1=r_[:], op=ALU.mult), r=[br_], w=[ba_])
                    kb.dma("pool", ACTT[f * 128:(f + 1) * 128, c0:c0 + NTU], a_[:], r=[ba_], w=[b_ACTT])
                g.run(XT, bXT, KC, w_up[:, f * 128:(f + 1) * 128], 128, evac)

    KF = c.DFF // 128
    for blk in range(TOK // NT):
        c0 = blk * NT
        with kb.phase():
            XD = kb.sb([128, KF, NT], BF16, "XD"); bXD = Buf("XD")
            AV = ACTT.rearrange("(kc p) t -> p kc t", p=128)
            for k0 in range(0, KF, 16):
                kn = min(16, KF - k0)
                kb.dma("sp", XD[:, k0:k0 + kn, :], AV[:, k0:k0 + kn, c0:c0 + NT], r=[b_ACTT], w=[bXD])
            out_gemm_and_epilogue(lambda: Gemm(min(16, KF), NT, nacc=4), XD, bXD, KF, w_dn, g2w, X1, b_X1, c0, out, b_out, c0, NT, nfp=(2 if KC % 2 == 0 else 1))


def make_in_maps(inp, cfg, ncores):
    c = cfg
    f = lambda a: np.ascontiguousarray(np.asarray(a, dtype=np.float32))
    shared = {
        "w_ada": f(inp["w_ada"][0]), "b_ada": f(inp["b_ada"][0]),
        "mix_pre_norm": f(inp["mix_pre_norm"][0]), "mix_post_norm": f(inp["mix_post_norm"][0]),
        "w_in": f(inp["w_in"][0]),
        "lru_conv_w": f(inp["lru_conv_w"][0]), "lru_conv_b": f(inp["lru_conv_b"][0]),
        "lru_gate_a_w": f(inp["lru_gate_a_w"][0]), "lru_gate_a_b": f(inp["lru_gate_a_b"][0]).reshape(-1),
        "lru_gate_i_w": f(inp["lru_gate_i_w"][0]), "lru_gate_i_b": f(inp["lru_gate_i_b"][0]).reshape(-1),
        "lru_lambda": f(inp["lru_lambda"][0]),
        "gdn_conv_w": f(inp["gdn_conv_w"][0]), "gdn_a_log": f(inp["gdn_a_log"][0]).reshape(-1, 1),
        "gdn_dt_bias": f(inp["gdn_dt_bias"][0]).reshape(-1, 1), "gdn_out_norm": f(inp["gdn_out_norm"][0]),
        "w_branch_lru": f(inp["w_branch_lru"][0]), "w_branch_gdn": f(inp["w_branch_gdn"][0]), "w_out": f(inp["w_out"][0]),
        "mlp_pre_norm": f(inp["mlp_pre_norm"][0]), "mlp_post_norm": f(inp["mlp_post_norm"][0]),
        "w_mlp_up": f(inp["w_mlp_up"][0]), "w_mlp_down": f(inp["w_mlp_down"][0]),
    }
    x = np.asarray(inp["x"], dtype=np.float32); cc = np.asarray(inp["c"], dtype=np.float32)
    maps = []
    for i in range(ncores):
        b, half = i // 2, i % 2
        m = dict(shared)
        m["x_own"] = np.ascontiguousarray(x[b, half * c.TOK:(half + 1) * c.TOK])
        m["x_pre"] = np.ascontiguousarray(x[b, 0:c.TOK])
        m["c"] = np.ascontiguousarray(cc[b].reshape(c.KC, 128))
        m["flag"] = np.full((128, 1), float(half), dtype=np.float32)
        maps.append(m)
    return maps


_CACHE = {}


def kernel(**inputs):
    cfg = Cfg(D=4096, T=4096, NT=512, SBK=256, PL=1024, NTW=1024)
    if "nc" not in _CACHE:
        _CACHE["nc"] = build(cfg)
    nc = _CACHE["nc"]
    maps = make_in_maps(inputs, cfg, 8)
    res = run_bass_kernel_spmd(nc, maps, core_ids=list(range(8)))
    outp = np.zeros((4, 4096, 4096), dtype=np.float32)
    for i in range(8):
        b, half = i // 2, i % 2
        outp[b, half * cfg.TOK:(half + 1) * cfg.TOK] = res.results[i]["out"]
    return outp
```

```python
import numpy as np
from contextlib import ExitStack, contextmanager
import concourse.bass as bass
import concourse.mybir as mybir
from concourse.bass_utils import run_bass_kernel_spmd

F32 = mybir.dt.float32
BF16 = mybir.dt.bfloat16
AF = mybir.ActivationFunctionType
ALU = mybir.AluOpType
EPS = 1e-6
SEM_LIMIT = 30000
import os
GSTOP = int(os.environ.get('GSTOP', '9'))
DUALQ = int(os.environ.get('DUALQ', '0'))
G2S = int(os.environ.get('G2S', '9'))
NDSEM = 40


class Cfg:
    def __init__(s, D, T, NT, SBK, PL, debug=False, NTW=None):
        s.D = D; s.T = T; s.NT = NT; s.SBK = SBK; s.PL = PL; s.debug = debug; s.NTW = NTW or NT
        s.LW = D // 2; s.LB = s.LW // 128; s.H = (D // 2) // 128; s.KEY = s.H * 128; s.VAL = s.H * 128
        s.DFF = 4 * D; s.TOK = T // 2; s.KC = D // 128
        s.o_lx = 0; s.o_lg = s.LW; s.o_q = 2 * s.LW; s.o_k = s.o_q + s.KEY; s.o_v = s.o_k + s.KEY
        s.o_z = s.o_v + s.VAL; s.o_a = s.o_z + s.VAL; s.o_b = s.o_a + s.H; s.o_gl = s.o_b + s.H; s.o_gg = s.o_gl + D
        s.INW = s.o_gg + D


class Buf:
    __slots__ = ("name", "lw", "rd", "excl")

    def __init__(s, name, excl=False):
        s.name = name; s.lw = None; s.rd = {}; s.excl = excl


def PB(name):
    return Buf(name, True)


class KB:
    def __init__(s, cfg):
        s.cfg = cfg
        s.nc = bass.Bass("TRN2", target_bir_lowering=False)
        nc = s.nc
        s.E = {"pe": nc.tensor, "act": nc.scalar, "dve": nc.vector, "pool": nc.gpsimd, "sp": nc.sync}
        s.root = ExitStack()
        s.es = s.root
        s.semh = {}
        s.sem = {}; s.cnt = {}; s.seen = {e: {} for e in s.E}
        s.nsem = 0
        for e in ("pe", "act", "dve", "pool"):
            s.sem[e] = s._newsem(); s.cnt[e] = 0
        s.dsem = [s._newsem() for _ in range(NDSEM)]
        s.dval = [0] * NDSEM
        s.dnext = 0
        s.uid = 0
        s.block = s.root.enter_context(nc.Block())

    def _newsem(s):
        s.nsem += 1
        name = "s%d" % s.nsem
        h = s.root.enter_context(s.nc.semaphore(name))
        s.semh[name] = h
        return name

    def sb(s, shape, dt, name=None):
        s.uid += 1
        t = s.es.enter_context(s.nc.sbuf_tensor("%s_%d" % (name or "t", s.uid), list(shape), dt))
        return t

    def ps(s, shape, dt=F32, name=None):
        s.uid += 1
        return s.es.enter_context(s.nc.psum_tensor("%s_%d" % (name or "p", s.uid), list(shape), dt))

    def dram(s, name, shape, dt, kind=None):
        k = kind or ("ExternalOutput" if s.cfg.debug else "Internal")
        return s.nc.dram_tensor(name, list(shape), dt, kind=k).ap()

    @contextmanager
    def phase(s):
        s.barrier()
        old = s.es
        es = ExitStack()
        s.es = es
        try:
            yield
        finally:
            s.barrier()
            es.close()
            s.es = old

    @contextmanager
    def scope(s):
        old = s.es
        es = ExitStack()
        s.es = es
        try:
            yield
        finally:
            s.barrier()
            es.close()
            s.es = old

    def _need(s, eng, reads, writes):
        toks = []
        for b in reads:
            if b.lw:
                toks.append(b.lw)
        for b in writes:
            if b.lw and not (eng == "pe" and b.lw[2] == "pe"):
                toks.append(b.lw)
            for k, v in b.rd.items():
                toks.append((k, v, None))
        return toks

    def _wait(s, eng, toks):
        mx = {}
        for t in toks:
            if t[1] > mx.get(t[0], 0):
                mx[t[0]] = t[1]
        for k, v in mx.items():
            if s.seen[eng].get(k, 0) < v:
                s.E[eng].wait_ge(s.semh[k], v)
                s.seen[eng][k] = v

    def op(s, eng, fn, r=(), w=()):
        w = list(w)
        for b in r:
            if b.excl and b not in w:
                w.append(b)
        s._wait(eng, s._need(eng, r, w))
        if s.cnt[eng] >= SEM_LIMIT:
            s.sem[eng] = s._newsem(); s.cnt[eng] = 0
        ins = fn(s.E[eng])
        s.cnt[eng] += 1
        k = s.sem[eng]
        ins.then_inc(s.semh[k], 1)
        tok = (k, s.cnt[eng], eng)
        for b in r:
            if b.rd.get(k, 0) < tok[1]:
                b.rd[k] = tok[1]
        for b in w:
            b.lw = tok; b.rd = {}
        return ins

    def dma(s, q, out, in_, r=(), w=()):
        i = s.dnext
        s.dnext = (i + 1) % NDSEM
        k = s.dsem[i]; pv = s.dval[i]
        toks = s._need("dma", r, w)
        if pv:
            toks.append((k, pv, None))
        s._wait(q, toks)
        s.E[q].dma_start(out=out, in_=in_).then_inc(s.semh[k], 16)
        s.dval[i] = pv + 16
        tok = (k, pv + 16, "dma")
        for b in r:
            b.rd[k] = tok[1]
        for b in w:
            b.lw = tok; b.rd = {}

    def barrier(s):
        toks = [(s.sem[e], s.cnt[e], None) for e in s.cnt if s.cnt[e] > 0]
        toks += [(s.dsem[i], s.dval[i], None) for i in range(NDSEM) if s.dval[i] > 0]
        for e in s.E:
            s._wait(e, toks)


_DBG = {}


def build(cfg):
    kb = KB(cfg)
    _DBG['kb'] = kb
    nc = kb.nc
    c = cfg
    D, KC, TOK, NT, H, LB, LW = c.D, c.KC, c.TOK, c.NT, c.H, c.LB, c.LW
    TT2 = 2 * TOK
    def din(name, shape):
        return nc.dram_tensor(name, list(shape), F32, kind="ExternalInput").ap()
    x_own = din("x_own", [TOK, D]); x_pre = din("x_pre", [TOK, D]); cvec = din("c", [KC, 128]); flag = din("flag", [128, 1])
    w_ada = din("w_ada", [D, 6 * D]); b_ada = din("b_ada", [6 * D])
    n_pre1 = din("mix_pre_norm", [D]); n_post1 = din("mix_post_norm", [D])
    w_in = din("w_in", [D, c.INW])
    lru_cw = din("lru_conv_w", [4, LW]); lru_cb = din("lru_conv_b", [LW])
    lru_aw = din("lru_gate_a_w", [LB, 128, 128]); lru_ab = din("lru_gate_a_b", [LW])
    lru_iw = din("lru_gate_i_w", [LB, 128, 128]); lru_ib = din("lru_gate_i_b", [LW])
    lru_lam = din("lru_lambda", [LW])
    gdn_cw = din("gdn_conv_w", [4, 3 * c.KEY]); gdn_alog = din("gdn_a_log", [H, 1]); gdn_dtb = din("gdn_dt_bias", [H, 1])
    gdn_onw = din("gdn_out_norm", [128])
    w_bl = din("w_branch_lru", [LW, D]); w_bg = din("w_branch_gdn", [c.VAL, D]); w_out = din("w_out", [D, D])
    n_pre2 = din("mlp_pre_norm", [D]); n_post2 = din("mlp_post_norm", [D])
    w_up = din("w_mlp_up", [D, c.DFF]); w_dn = din("w_mlp_down", [c.DFF, D])
    out = nc.dram_tensor("out", [TOK, D], F32, kind="ExternalOutput").ap()
    MODV = kb.dram("modv", [6 * D], F32)
    NREC = LW + 3 * c.KEY + 2 * H
    PROJR = kb.dram("projr", [NREC, TT2], F32)
    PROJN = kb.dram("projn", [c.INW - NREC, TOK], F32)

    class _Proj:
        def __getitem__(s, key):
            rs, cs = key
            r0, r1 = rs.start, rs.stop
            if r0 < c.o_lg:
                return PROJR[r0:r1, cs]
            if c.o_q <= r0 < c.o_z:
                return PROJR[r0 - c.o_q + LW:r1 - c.o_q + LW, cs]
            if c.o_a <= r0 < c.o_gl:
                return PROJR[r0 - c.o_a + LW + 3 * c.KEY:r1 - c.o_a + LW + 3 * c.KEY, cs]
            cs2 = slice(cs.start - TOK, cs.stop - TOK)
            assert cs2.start >= 0
            if r0 < c.o_q:
                return PROJN[r0 - c.o_lg:r1 - c.o_lg, cs2]
            if r0 < c.o_a:
                return PROJN[r0 - c.o_z + LW:r1 - c.o_z + LW, cs2]
            return PROJN[r0 - c.o_gl + LW + c.VAL:r1 - c.o_gl + LW + c.VAL, cs2]
    PROJ = _Proj()
    YA = kb.dram("ya", [LW, TOK], BF16)
    YB = kb.dram("yb", [c.VAL, TOK], BF16)
    X1 = kb.dram("x1", [TOK, D], F32)
    ACTT = kb.dram("actt", [c.DFF, TOK], BF16)
    b_PROJ = Buf("PROJ"); b_YA = Buf("YA"); b_YB = Buf("YB"); b_X1 = Buf("X1"); b_ACTT = Buf("ACTT"); b_MODV = Buf("MODV")
    b_out = Buf("out")

    ident = kb.sb([128, 128], F32, "ident"); identb = kb.sb([128, 128], BF16, "identb")
    ones = kb.sb([128, 128], F32, "ones")
    UPI = kb.sb([128, 128], F32, "UPI"); LOS = kb.sb([128, 128], F32, "LOS")
    BLK = kb.sb([128, 128], F32, "BLK"); CHA = kb.sb([128, 128], F32, "CHA"); CHB = kb.sb([128, 128], F32, "CHB")
    flg = kb.sb([128, 1], F32, "flg")
    NV = 6 * KC
    modfm = kb.sb([128, NV], F32, "modfm")
    w1s = kb.sb([128, KC], F32, "w1s"); w2s = kb.sb([128, KC], F32, "w2s")
    g1w = kb.sb([128, KC], F32, "g1w"); g2w = kb.sb([128, KC], F32, "g2w")
    b_const = Buf("const"); b_mod = Buf("mod")

    def P(eng, fn, r=(), w=()):
        return kb.op(eng, fn, r, w)

    P("pool", lambda e: e.memset(ident[:], 1.0), w=[b_const])
    P("pool", lambda e: e.affine_select(out=ident[:], in_=ident[:], pattern=[[-1, 128]], compare_op=ALU.is_equal,
                                        fill=0.0, base=0, channel_multiplier=1), r=[b_const], w=[b_const])
    P("pool", lambda e: e.tensor_copy(out=identb[:], in_=ident[:]), r=[b_const], w=[b_const])
    P("pool", lambda e: e.memset(ones[:], 1.0), w=[b_const])
    P("pool", lambda e: e.memset(UPI[:], 1.0), w=[b_const])
    P("pool", lambda e: e.affine_select(out=UPI[:], in_=UPI[:], pattern=[[1, 128]], compare_op=ALU.is_ge,
                                        fill=0.0, base=0, channel_multiplier=-1), r=[b_const], w=[b_const])
    P("pool", lambda e: e.memset(UPI[0:64, 64:128], 0.0), r=[b_const], w=[b_const])
    P("pool", lambda e: e.memset(LOS[:], 1.0), w=[b_const])
    P("pool", lambda e: e.affine_select(out=LOS[:], in_=LOS[:], pattern=[[-1, 128]], compare_op=ALU.is_gt,
                                        fill=0.0, base=0, channel_multiplier=1), r=[b_const], w=[b_const])
    P("pool", lambda e: e.memset(LOS[64:128, 0:64], 0.0), r=[b_const], w=[b_const])
    P("pool", lambda e: e.memset(BLK[:], 0.0), w=[b_const])
    P("pool", lambda e: e.memset(BLK[0:64, 0:64], 1.0), r=[b_const], w=[b_const])
    P("pool", lambda e: e.memset(BLK[64:128, 64:128], 1.0), r=[b_const], w=[b_const])
    P("pool", lambda e: e.memset(CHA[:], 0.0), w=[b_const])
    P("pool", lambda e: e.memset(CHA[0:64, :], 1.0), r=[b_const], w=[b_const])
    P("pool", lambda e: e.memset(CHB[:], 0.0), w=[b_const])
    P("pool", lambda e: e.memset(CHB[64:128, :], 1.0), r=[b_const], w=[b_const])
    kb.dma("sp", flg[:], flag[:, :], w=[b_const])

    def load_vec_fm(vec_ap, n, dst_ap, rbufs=(), wbuf=None):
        v2 = vec_ap.rearrange("(n p) -> n p", p=128)
        with ExitStack() as es:
            old = kb.es; kb.es = es
            for g0 in range(0, n, 128):
                gn = min(128, n - g0)
                st = kb.sb([128, 128], F32, "lv"); pt = kb.ps([128, 128], F32, "lvp")
                bs = Buf("lv"); bp = PB("lvp")
                kb.dma("sp", st[0:gn, :], v2[g0:g0 + gn, :], r=list(rbufs), w=[bs])
                P("pe", lambda e: e.transpose(pt[:, 0:gn], st[0:gn, :], ident[0:gn, 0:gn]), r=[bs, b_const], w=[bp])
                P("dve", lambda e: e.tensor_copy(out=dst_ap[:, g0:g0 + gn], in_=pt[:, 0:gn]), r=[bp], w=[wbuf])
            kb.barrier()
            kb.es = old

    with kb.phase():
        cin = kb.sb([128, 128], F32, "cin"); cact = kb.sb([128, 128], F32, "cact"); cT = kb.sb([128, KC], F32, "cT")
        cps = kb.ps([128, 128], F32, "cps")
        b_c = Buf("c"); b_cp = PB("cp"); b_cT = Buf("cT")
        kb.dma("sp", cin[0:KC, :], cvec[:, :], w=[b_c])
        P("act", lambda e: e.activation(out=cact[0:KC, :], in_=cin[0:KC, :], func=AF.Silu), r=[b_c], w=[b_c])
        P("pe", lambda e: e.transpose(cps[:, 0:KC], cact[0:KC, :], ident[0:KC, 0:KC]), r=[b_c, b_const], w=[b_cp])
        P("dve", lambda e: e.tensor_copy(out=cT[:], in_=cps[:, 0:KC]), r=[b_cp], w=[b_cT])
        KP = min(8, KC)
        NPC = KC // KP
        NAW = 8
        wst = [kb.sb([128, KP, 512], F32, "adaw") for _ in range(NAW)]
        bw = [Buf("adaw%d" % i) for i in range(NAW)]
        mps = [kb.ps([128, 512], F32, "mps") for _ in range(2)]
        bmp = [PB("mps%d" % i) for i in range(2)]
        mst = [kb.sb([1, 512], F32, "mst") for _ in range(2)]
        bms = [Buf("mst%d" % i) for i in range(2)]
        wv = w_ada.rearrange("(kc p) n -> p kc n", p=128)
        li = 0
        for nt in range(6 * D // 512):
            pp = mps[nt % 2]; bpp = bmp[nt % 2]
            for pc in range(NPC):
                wt = wst[li % NAW]; bwt = bw[li % NAW]; li += 1
                kb.dma("sp", wt[:], wv[:, pc * KP:(pc + 1) * KP, nt * 512:(nt + 1) * 512], w=[bwt])
                for j in range(KP):
                    kc = pc * KP + j
                    P("pe", lambda e, kc=kc, j=j, wt=wt, pp=pp: e.matmul(pp[0:1, :], cT[:, kc:kc + 1], wt[:, j, :],
                                                                        start=(kc == 0), stop=(kc == KC - 1)),
                      r=[b_cT, bwt], w=[bpp])
            ms = mst[nt % 2]; bm = bms[nt % 2]
            P("act", lambda e, ms=ms, pp=pp: e.copy(out=ms[:], in_=pp[0:1, :]), r=[bpp], w=[bm])
            kb.dma("pool", MODV[nt * 512:(nt + 1) * 512].rearrange("(o n) -> o n", o=1), ms[:], r=[bm], w=[b_MODV])
        load_vec_fm(MODV, NV, modfm, rbufs=[b_MODV], wbuf=b_mod)
        tmpv = kb.sb([128, NV], F32, "tmpv"); b_tmp = Buf("tmpv")
        load_vec_fm(b_ada, NV, tmpv, wbuf=b_tmp)
        P("dve", lambda e: e.tensor_tensor(out=modfm[:], in0=modfm[:], in1=tmpv[:], op=ALU.add), r=[b_mod, b_tmp], w=[b_mod])
        nv = kb.sb([128, 4, KC], F32, "nv"); b_nv = Buf("nv")
        for i, v in enumerate([n_pre1, n_post1, n_pre2, n_post2]):
            load_vec_fm(v, KC, nv[:, i, :], wbuf=b_nv)
        P("dve", lambda e: e.scalar_tensor_tensor(out=w1s[:], in0=modfm[:, KC:2 * KC], scalar=1.0, in1=nv[:, 0, :],
                                                  op0=ALU.add, op1=ALU.mult), r=[b_mod, b_nv], w=[b_mod])
        P("dve", lambda e: e.scalar_tensor_tensor(out=w2s[:], in0=modfm[:, 4 * KC:5 * KC], scalar=1.0, in1=nv[:, 2, :],
                                                  op0=ALU.add, op1=ALU.mult), r=[b_mod, b_nv], w=[b_mod])
        P("dve", lambda e: e.tensor_tensor(out=g1w[:], in0=modfm[:, 2 * KC:3 * KC], in1=nv[:, 1, :], op=ALU.mult),
          r=[b_mod, b_nv], w=[b_mod])
        P("dve", lambda e: e.tensor_tensor(out=g2w[:], in0=modfm[:, 5 * KC:6 * KC], in1=nv[:, 3, :], op=ALU.mult),
          r=[b_mod, b_nv], w=[b_mod])
    sh1 = modfm[:, 0:KC]; sh2 = modfm[:, 3 * KC:4 * KC]

    cast_rr = [0]

    def prenorm_block(xsrc, bsrc, t0, ntok, XT, bXT, ws, sh, tag):
        with ExitStack() as es:
            old = kb.es; kb.es = es
            xt = [kb.sb([128, D], F32, "xt") for _ in range(2)]; bx = [Buf("xt%d" % i) for i in range(2)]
            junk = kb.sb([128, D], BF16, "junk"); bj = Buf("junk")
            st = [kb.sb([128, 4], F32, "st") for _ in range(2)]; bst = [Buf("st%d" % i) for i in range(2)]
            tp = [kb.ps([128, 512], F32, "tp") for _ in range(2)]; btp = [PB("tp%d" % i) for i in range(2)]
            for i in range(ntok // 128):
                x_ = xt[i % 2]; b_ = bx[i % 2]; s_ = st[i % 2]; bs_ = bst[i % 2]
                kb.dma("sp", x_[:], xsrc[t0 + i * 128:t0 + (i + 1) * 128, :], r=[bsrc], w=[b_])
                P("act", lambda e: e.activation(out=junk[:], in_=x_[:], func=AF.Square, accum_out=s_[:, 0:1]),
                  r=[b_], w=[bj, bs_])
                P("dve", lambda e: e.tensor_scalar(out=s_[:, 1:2], in0=s_[:, 0:1], scalar1=1.0 / D, scalar2=EPS,
                                                   op0=ALU.mult, op1=ALU.add), r=[bs_], w=[bs_])
                P("act", lambda e: e.activation(out=s_[:, 2:3], in_=s_[:, 1:2], func=AF.Sqrt), r=[bs_], w=[bs_])
                P("dve", lambda e: e.reciprocal(out=s_[:, 3:4], in_=s_[:, 2:3]), r=[bs_], w=[bs_])
                P("act", lambda e: e.activation(out=x_[:], in_=x_[:], func=AF.Identity, scale=s_[:, 3:4]),
                  r=[b_, bs_], w=[b_])
                for g in range(KC // 4 if KC >= 4 else 1):
                    pt = tp[g % 2]; bp = btp[g % 2]
                    nq = min(4, KC)
                    for q in range(nq):
                        kc = g * 4 + q
                        P("pe", lambda e, kc=kc, q=q, pt=pt: e.transpose(pt[:, q * 128:(q + 1) * 128],
                                                                         x_[:, kc * 128:(kc + 1) * 128], ident[:]),
                          r=[b_, b_const], w=[bp])
                    for q in range(nq):
                        kc = g * 4 + q
                        eng = "dve" if q % 2 == 0 else "pool"
                        eng = "dve"
                        P(eng, lambda e, kc=kc, q=q, pt=pt: e.tensor_scalar(
                            out=XT[:, kc, i * 128:(i + 1) * 128], in0=pt[:, q * 128:(q + 1) * 128],
                            scalar1=ws[:, kc:kc + 1], scalar2=sh[:, kc:kc + 1], op0=ALU.mult, op1=ALU.add),
                          r=[bp, b_mod], w=[bXT])
            kb.barrier()
            kb.es = old

    class Gemm:
        def __init__(g, kpc, ntok, nacc=2):
            g.kpc = kpc; g.ntok = ntok
            g.nws = 4
            g.wst = [kb.sb([128, kpc, 128], F32, "wst") for _ in range(g.nws)]; g.bws = [Buf("wst%d" % i) for i in range(g.nws)]
            g.wbf = [kb.sb([128, kpc, 128], BF16, "wbf") for _ in range(3)]; g.bwb = [Buf("wbf%d" % i) for i in range(3)]
            g.acc = [kb.ps([128, max(512, ntok)], F32, "acc") for _ in range(nacc)]; g.bacc = [PB("acc%d" % i) for i in range(nacc)]
            g.nacc = nacc
            g.li = 0; g.ai = 0

        def run_multi(g, XT, bXT, KCn, wcols, nf, evacs):
            ntok = g.ntok
            accs = []
            for t in range(nf):
                accs.append((g.acc[g.ai % g.nacc], g.bacc[g.ai % g.nacc])); g.ai += 1
            wv = wcols.rearrange("(kc p) m -> p kc m", p=128)
            kp = g.kpc // nf
            W = nf * 128
            for p0 in range(0, KCn, kp):
                pn = min(kp, KCn - p0)
                ws = g.wst[g.li % g.nws]; bws = g.bws[g.li % g.nws]
                wb = g.wbf[g.li % 3]; bwb = g.bwb[g.li % 3]; g.li += 1
                wsv = ws[:].rearrange("p a b -> p (a b)").rearrange("p (a b) -> p a b", b=W)
                wbv = wb[:].rearrange("p a b -> p (a b)").rearrange("p (a b) -> p a b", b=W)
                kb.dma("sp", wsv[:, 0:pn, :], wv[:, p0:p0 + pn, :], w=[bws])
                ce = ("dve", "act")[cast_rr[0] % 2]; cast_rr[0] += 1
                if ce == "act":
                    P("act", lambda e: e.copy(out=wbv[:, 0:pn, :], in_=wsv[:, 0:pn, :]), r=[bws], w=[bwb])
                else:
                    P(ce, lambda e: e.tensor_copy(out=wbv[:, 0:pn, :], in_=wsv[:, 0:pn, :]), r=[bws], w=[bwb])
                for j in range(pn):
                    kc = p0 + j
                    for t in range(nf):
                        pp, bpp = accs[t]
                        for s0 in range(0, ntok, 512):
                            s1 = min(ntok, s0 + 512)
                            P("pe", lambda e: e.matmul(pp[:, s0:s1], wbv[:, j, t * 128:(t + 1) * 128], XT[:, kc, s0:s1],
                                                      start=(kc == 0), stop=(kc == KCn - 1)), r=[bwb, bXT], w=[bpp])
            for t in range(nf):
                evacs[t](accs[t][0][:, 0:ntok], accs[t][1])

        def run_jobs(g, jobs, LA_D=int(os.environ.get("LAD", "3")), LA_C=int(os.environ.get("LAC", "1"))):
            ntok = g.ntok
            pieces = []
            for jb in jobs:
                kp = g.kpc // jb["nf"]
                for p0 in range(0, jb["KCn"], kp):
                    pieces.append((jb, p0, min(kp, jb["KCn"] - p0)))
            n = len(pieces)
            base = g.li
            g.li += n
            st = {"d": 0, "c": 0}
            views = {}

            def bufs(idx):
                gi = base + idx
                jb, p0, pn = pieces[idx]
                nf = jb["nf"]
                ws = g.wst[gi % g.nws]; bws = g.bws[gi % g.nws]
                wb = g.wbf[gi % 3]; bwb = g.bwb[gi % 3]
                if nf > 1:
                    W = nf * 128
                    wsv = ws[:].rearrange("p a b -> p (a b)").rearrange("p (a b) -> p a b", b=W)[:, 0:pn, :]
                    wbv = wb[:].rearrange("p a b -> p (a b)").rearrange("p (a b) -> p a b", b=W)
                else:
                    M = jb["M"]
                    wsv = ws[:, 0:pn, 0:M]
                    wbv = wb
                return wsv, bws, wbv, bwb

            def emit_dma(idx):
                jb, p0, pn = pieces[idx]
                wsv, bws, wbv, bwb = bufs(idx)
                wv = jb["wcols"].rearrange("(kc p) m -> p kc m", p=128)
                kb.dma("sp", wsv, wv[:, p0:p0 + pn, :], w=[bws])

            def emit_cast(idx):
                jb, p0, pn = pieces[idx]
                if p0 == 0 and jb.get("pre"):
                    jb["pre"]()
                wsv, bws, wbv, bwb = bufs(idx)
                dst = wbv[:, 0:pn, :] if jb["nf"] > 1 else wbv[:, 0:pn, 0:jb["M"]]
                ce = ("dve", "act")[cast_rr[0] % 2]; cast_rr[0] += 1
                if ce == "act":
                    P("act", lambda e: e.copy(out=dst, in_=wsv), r=[bws], w=[bwb])
                else:
                    P(ce, lambda e: e.tensor_copy(out=dst, in_=wsv), r=[bws], w=[bwb])

            accs = None
            for idx in range(n):
                while st["d"] <= min(idx + LA_D, n - 1):
                    emit_dma(st["d"]); st["d"] += 1
                while st["c"] <= min(idx + LA_C, n - 1):
                    emit_cast(st["c"]); st["c"] += 1
                jb, p0, pn = pieces[idx]
                nf = jb["nf"]; KCn = jb["KCn"]; M = jb["M"]; XT = jb["XT"]; bXT = jb["bXT"]
                if p0 == 0:
                    accs = []
                    for t in range(nf):
                        accs.append((g.acc[g.ai % g.nacc], g.bacc[g.ai % g.nacc])); g.ai += 1
                wsv, bws, wbv, bwb = bufs(idx)
                for j in range(pn):
                    kc = p0 + j
                    for t in range(nf):
                        pp, bpp = accs[t]
                        lhs = wbv[:, j, t * 128:(t + 1) * 128] if nf > 1 else wbv[:, j, 0:M]
                        for s0 in range(0, ntok, 512):
                            s1 = min(ntok, s0 + 512)
                            P("pe", lambda e: e.matmul(pp[0:M, s0:s1], lhs, XT[:, kc, s0:s1],
                                                      start=(kc == 0), stop=(kc == KCn - 1)), r=[bwb, bXT], w=[bpp])
                if p0 + pn >= KCn:
                    for t in range(nf):
                        jb["evacs"][t](accs[t][0][0:M, 0:ntok], accs[t][1])

        def run(g, XT, bXT, KCn, wcols, M, evac, start_acc=True):
            ntok = g.ntok
            pp = g.acc[g.ai % 2]; bpp = g.bacc[g.ai % 2]; g.ai += 1
            wv = wcols.rearrange("(kc p) m -> p kc m", p=128)
            for p0 in range(0, KCn, g.kpc):
                pn = min(g.kpc, KCn - p0)
                ws = g.wst[g.li % g.nws]; bws = g.bws[g.li % g.nws]
                wb = g.wbf[g.li % 3]; bwb = g.bwb[g.li % 3]; g.li += 1
                kb.dma(("sp", "act")[g.li % 2] if DUALQ else "sp", ws[:, 0:pn, 0:M], wv[:, p0:p0 + pn, :], w=[bws])
                ce = ("dve", "act")[cast_rr[0] % 2]; cast_rr[0] += 1
                if ce == "act":
                    P("act", lambda e: e.copy(out=wb[:, 0:pn, 0:M], in_=ws[:, 0:pn, 0:M]), r=[bws], w=[bwb])
                else:
                    P(ce, lambda e: e.tensor_copy(out=wb[:, 0:pn, 0:M], in_=ws[:, 0:pn, 0:M]), r=[bws], w=[bwb])
                for j in range(pn):
                    kc = p0 + j
                    for s0 in range(0, ntok, 512):
                        s1 = min(ntok, s0 + 512)
                        P("pe", lambda e, j=j, kc=kc: e.matmul(pp[0:M, s0:s1], wb[:, j, 0:M], XT[:, kc, s0:s1],
                                                              start=(kc == 0), stop=(kc == KCn - 1)),
                          r=[bwb, bXT], w=[bpp])
            evac(pp[0:M, 0:ntok], bpp)

    segs_rec = [(c.o_lx, LW), (c.o_q, 3 * c.KEY), (c.o_a, H), (c.o_b, H)]
    segs_non = [(c.o_lg, LW), (c.o_z, c.VAL), (c.o_gl, D), (c.o_gg, D)]

    def win_pass(xsrc, tcol0, segs):
        NT = min(c.NTW, TOK)
        for blk in range(TOK // NT):
            with kb.phase():
                XT = kb.sb([128, KC, NT], BF16, "XT"); bXT = Buf("XT")
                prenorm_block(xsrc, Buf("xin"), blk * NT, NT, XT, bXT, w1s, sh1, "pn1")
                g = Gemm(min(KC, 32), NT)
                ost = [kb.sb([128, NT], F32, "ost") for _ in range(3)]; bo = [Buf("ost%d" % i) for i in range(3)]
                oi = [0]
                jobs = []
                for (o0, wd) in segs:
                    for f0 in range(0, wd, 128):
                        M = min(128, wd - f0)
                        def evac(pap, bpp, o0=o0, f0=f0, M=M):
                            o_ = ost[oi[0] % 3]; b_ = bo[oi[0] % 3]; oi[0] += 1
                            P("act", lambda e: e.copy(out=o_[0:M, :], in_=pap), r=[bpp], w=[b_])
                            kb.dma("pool", PROJ[o0 + f0:o0 + f0 + M, tcol0 + blk * NT:tcol0 + (blk + 1) * NT], o_[0:M, :],
                                   r=[b_], w=[b_PROJ])
                        jobs.append(dict(XT=XT, bXT=bXT, KCn=KC, wcols=w_in[:, o0 + f0:o0 + f0 + M], nf=1, M=M, evacs=[evac]))
                g.run_jobs(jobs)

    win_pass(x_pre, 0, segs_rec)
    win_pass(x_own, TOK, segs_rec + segs_non)

    PL = c.PL
    with kb.phase():
        cw = kb.sb([128, 4, LB], F32, "lcw"); cb = kb.sb([128, LB], F32, "lcb")
        ab = kb.sb([128, LB], F32, "lab"); ib = kb.sb([128, LB], F32, "lib"); nsp = kb.sb([128, LB], F32, "nsp")
        b_lc = Buf("lruconst")
        for j in range(4):
            load_vec_fm(lru_cw[j, :], LB, cw[:, j, :], wbuf=b_lc)
        load_vec_fm(lru_cb, LB, cb, wbuf=b_lc)
        load_vec_fm(lru_ab, LB, ab, wbuf=b_lc)
        load_vec_fm(lru_ib, LB, ib, wbuf=b_lc)
        load_vec_fm(lru_lam, LB, nsp, wbuf=b_lc)
        P("act", lambda e: e.activation(out=nsp[:], in_=nsp[:], func=AF.Exp, scale=-1.0), r=[b_lc], w=[b_lc])
        P("act", lambda e: e.activation(out=nsp[:], in_=nsp[:], func=AF.Ln, bias=1.0), r=[b_lc], w=[b_lc])
        P("dve", lambda e: e.tensor_scalar(out=nsp[:], in0=nsp[:], scalar1=-8.0, scalar2=None, op0=ALU.mult),
          r=[b_lc], w=[b_lc])
        gw32 = kb.sb([128, 2, 128], F32, "gw32"); b_gw32 = Buf("gw32")
        gwb = [kb.sb([128, 2, 128], BF16, "gwb") for _ in range(2)]; b_gwb = [Buf("gwb%d" % i) for i in range(2)]
        NB = 2
        def tiles(n, shape, dt):
            return [kb.sb(shape, dt, n) for _ in range(NB)], [Buf(n + str(i)) for i in range(NB)]
        xin, bxin = tiles("xin", [128, PL + 3], F32)
        xa, bxa = tiles("xa", [128, PL], F32)
        xab, bxab = tiles("xab", [128, PL], BF16)
        rr, brr = tiles("rr", [128, PL], F32)
        ii, bii = tiles("ii", [128, PL], F32)
        aa, baa = tiles("aa", [128, PL], F32)
        mm, bmm = tiles("mm", [128, PL], F32)
        hh, bhh = tiles("hh", [128, PL], F32)
        gg, bgg = tiles("gg", [128, PL], F32)
        g2, bg2 = tiles("g2", [128, PL], F32)
        yo, byo = tiles("yo", [128, PL], BF16)
        state = kb.sb([128, 1], F32, "lstate"); b_state = Buf("lstate")
        gps = [kb.ps([128, 512], F32, "gps") for _ in range(4)]; bgps = [PB("gps%d" % i) for i in range(4)]
        gi = 0; it = 0
        for ct in range(LB):
            wb_ = gwb[ct % 2]; bwb_ = b_gwb[ct % 2]
            kb.dma("sp", gw32[:, 0, :], lru_aw[ct, :, :], w=[b_gw32])
            kb.dma("sp", gw32[:, 1, :], lru_iw[ct, :, :], w=[b_gw32])
            P("pool", lambda e: e.tensor_copy(out=wb_[:], in_=gw32[:]), r=[b_gw32], w=[bwb_])
            for pc in range(TT2 // PL):
                t0 = pc * PL
                own = t0 >= TOK
                k = it % NB; it += 1
                xi = xin[k]; bxi = bxin[k]
                kb.dma("sp", xi[:, 3:PL + 3], PROJ[c.o_lx + ct * 128:c.o_lx + (ct + 1) * 128, t0:t0 + PL], r=[b_PROJ], w=[bxi])
                if t0 == 0:
                    P("pool", lambda e: e.memset(xi[:, 0:3], 0.0), w=[bxi])
                    P("pool", lambda e: e.memset(state[:], 0.0), w=[b_state])
                else:
                    xp = xin[(k - 1) % NB]; bxp = bxin[(k - 1) % NB]
                    if t0 == TOK:
                        P("dve", lambda e: e.tensor_scalar(out=xi[:, 0:3], in0=xp[:, PL:PL + 3], scalar1=flg[:, 0:1],
                                                           scalar2=None, op0=ALU.mult), r=[bxp, b_const], w=[bxi])
                        P("dve", lambda e: e.tensor_scalar(out=state[:], in0=state[:], scalar1=flg[:, 0:1],
                                                           scalar2=None, op0=ALU.mult), r=[b_state, b_const], w=[b_state])
                    else:
                        P("dve", lambda e: e.tensor_copy(out=xi[:, 0:3], in_=xp[:, PL:PL + 3]), r=[bxp], w=[bxi])
                xa_ = xa[k]; bxa_ = bxa[k]
                P("dve", lambda e: e.tensor_scalar(out=xa_[:], in0=xi[:, 0:PL], scalar1=cw[:, 0, ct:ct + 1],
                                                   scalar2=cb[:, ct:ct + 1], op0=ALU.mult, op1=ALU.add),
                  r=[bxi, b_lc], w=[bxa_])
                for j in range(1, 4):
                    P("dve", lambda e, j=j: e.scalar_tensor_tensor(out=xa_[:], in0=xi[:, j:j + PL], scalar=cw[:, j, ct:ct + 1],
                                                                   in1=xa_[:], op0=ALU.mult, op1=ALU.add),
                      r=[bxi, b_lc, bxa_], w=[bxa_])
                xb = xab[k]; bxb = bxab[k]
                P("pool", lambda e: e.tensor_copy(out=xb[:], in_=xa_[:]), r=[bxa_], w=[bxb])
                r_ = rr[k]; br_ = brr[k]; i_ = ii[k]; bi_ = bii[k]
                for sub in range(0, PL, 512):
                    sn = min(512, PL - sub)
                    for gsel, (dst, bdst, bias) in enumerate([(r_, br_, ab), (i_, bi_, ib)]):
                        pp = gps[gi % 4]; bpp = bgps[gi % 4]; gi += 1
                        P("pe", lambda e, pp=pp, gsel=gsel: e.matmul(pp[:, 0:sn], wb_[:, gsel, :], xb[:, sub:sub + sn],
                                                                     start=True, stop=True), r=[bwb_, bxb], w=[bpp])
                        P("act", lambda e, pp=pp, dst=dst, bias=bias: e.activation(
                            out=dst[:, sub:sub + sn], in_=pp[:, 0:sn], func=AF.Sigmoid, bias=bias[:, ct:ct + 1]),
                          r=[bpp, b_lc], w=[bdst])
                a_ = aa[k]; ba_ = baa[k]; m_ = mm[k]; bm_ = bmm[k]; h_ = hh[k]; bh_ = bhh[k]
                P("act", lambda e: e.activation(out=a_[:], in_=r_[:], func=AF.Exp, scale=nsp[:, ct:ct + 1]),
                  r=[br_, b_lc], w=[ba_])
                P("pool", lambda e: e.tensor_tensor(out=m_[:], in0=a_[:], in1=a_[:], op=ALU.mult), r=[ba_], w=[bm_])
                P("act", lambda e: e.activation(out=m_[:], in_=m_[:], func=AF.Sqrt, scale=-1.0, bias=1.0), r=[bm_], w=[bm_])
                P("pool", lambda e: e.tensor_tensor(out=i_[:], in0=i_[:], in1=xa_[:], op=ALU.mult), r=[bi_, bxa_], w=[bi_])
                P("pool", lambda e: e.tensor_tensor(out=m_[:], in0=m_[:], in1=i_[:], op=ALU.mult), r=[bm_, bi_], w=[bm_])
                P("dve", lambda e: e.tensor_tensor_scan(out=h_[:], data0=a_[:], data1=m_[:], initial=state[:, 0:1],
                                                        op0=ALU.mult, op1=ALU.add), r=[ba_, bm_, b_state], w=[bh_])
                P("dve", lambda e: e.tensor_copy(out=state[:], in_=h_[:, PL - 1:PL]), r=[bh_], w=[b_state])
                if own:
                    g_ = gg[k]; bg_ = bgg[k]; q_ = g2[k]; bq_ = bg2[k]; y_ = yo[k]; by_ = byo[k]
                    kb.dma("sp", g_[:], PROJ[c.o_lg + ct * 128:c.o_lg + (ct + 1) * 128, t0:t0 + PL], r=[b_PROJ], w=[bg_])
                    P("pool", lambda e: e.tensor_tensor(out=q_[:], in0=g_[:], in1=g_[:], op=ALU.mult), r=[bg_], w=[bq_])
                    P("pool", lambda e: e.tensor_scalar(out=q_[:], in0=q_[:], scalar1=0.044715, scalar2=1.0,
                                                        op0=ALU.mult, op1=ALU.add), r=[bq_], w=[bq_])
                    P("pool", lambda e: e.tensor_tensor(out=q_[:], in0=q_[:], in1=g_[:], op=ALU.mult), r=[bq_, bg_], w=[bq_])
                    P("act", lambda e: e.activation(out=q_[:], in_=q_[:], func=AF.Sigmoid, scale=2.0 * 0.7978845608028654),
                      r=[bq_], w=[bq_])
                    P("dve", lambda e: e.tensor_tensor(out=q_[:], in0=q_[:], in1=g_[:], op=ALU.mult), r=[bq_, bg_], w=[bq_])
                    P("dve", lambda e: e.tensor_tensor(out=y_[:], in0=q_[:], in1=h_[:], op=ALU.mult), r=[bq_, bh_], w=[by_])
                    kb.dma("pool", YA[ct * 128:(ct + 1) * 128, t0 - TOK:t0 - TOK + PL], y_[:], r=[by_], w=[b_YA])

    gdn_phase(kb, c, locals())

    tail_phases(kb, c, locals())
    kb.barrier()
    kb.root.close()
    return nc


def gdn_phase(kb, c, L):
    P = L["P"]; load_vec_fm = L["load_vec_fm"]
    ident = L["ident"]; identb = L["identb"]; ones = L["ones"]; UPI = L["UPI"]; LOS = L["LOS"]; BLK = L["BLK"]
    CHA = L["CHA"]; CHB = L["CHB"]; flg = L["flg"]; b_const = L["b_const"]
    PROJ = L["PROJ"]; b_PROJ = L["b_PROJ"]; YB = L["YB"]; b_YB = L["b_YB"]
    gdn_cw = L["gdn_cw"]; gdn_alog = L["gdn_alog"]; gdn_dtb = L["gdn_dtb"]; gdn_onw = L["gdn_onw"]
    H, TOK, SBK, KEY = c.H, c.TOK, c.SBK, c.KEY
    TT2 = 2 * TOK
    nt = SBK // 128
    HG = min(4, H)
    NG = H // HG
    with kb.phase():
        bc = Buf("gconst")
        gcw = kb.sb([128, 4, 3 * H], F32, "gcw")
        for j in range(4):
            load_vec_fm(gdn_cw[j, :], 3 * H, gcw[:, j, :], wbuf=bc)
        onw = kb.sb([128, 1], F32, "onw")
        load_vec_fm(gdn_onw, 1, onw, wbuf=bc)
        dtb = kb.sb([128, 1], F32, "dtb"); negA = kb.sb([128, 1], F32, "negA")
        kb.dma("sp", dtb[0:H, :], gdn_dtb[:, :], w=[bc])
        kb.dma("sp", negA[0:H, :], gdn_alog[:, :], w=[bc])
        P("act", lambda e: e.activation(out=negA[0:H, :], in_=negA[0:H, :], func=AF.Exp), r=[bc], w=[bc])
        P("dve", lambda e: e.tensor_scalar(out=negA[0:H, :], in0=negA[0:H, :], scalar1=-1.0, scalar2=None, op0=ALU.mult),
          r=[bc], w=[bc])
        MASK2 = kb.sb([128, 2, 128], F32, "MASK2")
        P("pool", lambda e: e.tensor_copy(out=MASK2[:, 0, :], in_=LOS[:]), r=[b_const], w=[bc])
        P("pool", lambda e: e.tensor_copy(out=MASK2[:, 1, :], in_=UPI[:]), r=[b_const], w=[bc])
        Sm = kb.sb([128, H, 128], F32, "Sm"); Sb = kb.sb([128, H, 128], BF16, "Sb")
        bSm = [Buf("Sm%d" % h) for h in range(H)]; bSb = [Buf("Sb%d" % h) for h in range(H)]
        qnT = kb.sb([128, H, SBK], BF16, "qnT"); knT = kb.sb([128, H, SBK], BF16, "knT")
        bq = [Buf("qnT%d" % h) for h in range(H)]; bk = [Buf("knT%d" % h) for h in range(H)]
        kbg = kb.sb([128, H, nt, 128], BF16, "kbg"); kdec = kb.sb([128, H, nt, 128], BF16, "kdec")
        vb = kb.sb([128, H, nt, 128], BF16, "vb")
        bkt = [Buf("ktok%d" % h) for h in range(H)]
        zs = kb.sb([128, H, SBK], F32, "zs"); bz = [Buf("zs%d" % h) for h in range(H)]
        ybT = kb.sb([128, H, SBK], BF16, "ybT"); byb = [Buf("ybT%d" % h) for h in range(H)]
        uu = kb.sb([128, H, nt, 128], F32, "uu"); nwT = kb.sb([128, H, nt, 128], BF16, "nwT")
        atT = kb.sb([128, H, nt, 128], BF16, "atT")
        bu = [[Buf("u") for _ in range(nt)] for _ in range(H)]
        bnw = [[Buf("nw") for _ in range(nt)] for _ in range(H)]
        bat = [[Buf("at") for _ in range(nt)] for _ in range(H)]
        sc = kb.sb([128, nt, 8, H], F32, "sc"); bsc = [Buf("sc%d" % j) for j in range(nt)]
        afm = kb.sb([128, 2, SBK], F32, "afm"); bafm = Buf("afm")
        pA = [kb.ps([128, 512], F32, "pA") for _ in range(1)]; bpA = [PB("pA%d" % i) for i in range(1)]
        pT = [kb.ps([128, 4, 128], BF16, "pT") for _ in range(1)]; bpT = [PB("pT%d" % i) for i in range(1)]
        itp = [0]
        def nextT():
            return pT[0], bpT[0]
        pB = [kb.ps([128, 4, 2, 128], F32, "pB") for _ in range(2)]; bpB = [PB("pB%d" % i) for i in range(2)]
        pC = [kb.ps([128, 4, 128], F32, "pC") for _ in range(2)]; bpC = [PB("pC%d" % i) for i in range(2)]
        ia = [0]; ib = [0]; ic = [0]
        def nextA():
            ia[0] += 1; return pA[0], bpA[0]
        def nextB():
            ib[0] += 1; return pB[ib[0] % 2], bpB[ib[0] % 2]
        def nextC():
            ic[0] += 1; return pC[ic[0] % 2], bpC[ic[0] % 2]
        it = [0]

        for sbi in range(TT2 // SBK):
            t0 = sbi * SBK
            own = t0 >= TOK
            kb.dma("sp", afm[0:H, 0, :], PROJ[c.o_a:c.o_a + H, t0:t0 + SBK], r=[b_PROJ], w=[bafm])
            kb.dma("sp", afm[0:H, 1, :], PROJ[c.o_b:c.o_b + H, t0:t0 + SBK], r=[b_PROJ], w=[bafm])
            P("act", lambda e: e.activation(out=afm[0:H, 0, :], in_=afm[0:H, 0, :], func=AF.Exp, bias=dtb[0:H, 0:1]),
              r=[bafm, bc], w=[bafm])
            P("act", lambda e: e.activation(out=afm[0:H, 0, :], in_=afm[0:H, 0, :], func=AF.Ln, bias=1.0), r=[bafm], w=[bafm])
            P("dve", lambda e: e.tensor_scalar(out=afm[0:H, 0, :], in0=afm[0:H, 0, :], scalar1=negA[0:H, 0:1], scalar2=None,
                                               op0=ALU.mult), r=[bafm, bc], w=[bafm])
            P("act", lambda e: e.activation(out=afm[0:H, 1, :], in_=afm[0:H, 1, :], func=AF.Sigmoid), r=[bafm], w=[bafm])
            for j in range(nt):
                pt, bpt = nextA()
                for q in range(2):
                    P("pe", lambda e: e.transpose(pt[:, q * H:(q + 1) * H], afm[0:H, q, j * 128:(j + 1) * 128], ident[0:H, 0:H]),
                      r=[bafm, b_const], w=[bpt])
                P("dve", lambda e: e.tensor_copy(out=sc[:, j, 0:2, :], in_=pt[:, 0:2 * H].rearrange("p (a h) -> p a h", a=2)),
                  r=[bpt], w=[bsc[j]])
                pt2, bpt2 = nextA()
                for q, mt in enumerate([UPI, BLK, CHA, CHB]):
                    P("pe", lambda e: e.matmul(pt2[:, q * H:(q + 1) * H], mt[:], sc[:, j, 0, :], start=True, stop=True),
                      r=[bsc[j], b_const], w=[bpt2])
                P("act", lambda e: e.activation(out=sc[:, j, 2, :], in_=pt2[:, 0:H], func=AF.Exp), r=[bpt2], w=[bsc[j]])
                P("dve", lambda e: e.tensor_copy(out=sc[:, j, 4, :], in_=pt2[:, 0:H]), r=[bpt2], w=[bsc[j]])
                P("dve", lambda e: e.tensor_tensor(out=sc[:, j, 3, :], in0=pt2[:, H:2 * H], in1=sc[:, j, 4, :], op=ALU.subtract),
                  r=[bpt2, bsc[j]], w=[bsc[j]])
                P("act", lambda e: e.activation(out=sc[:, j, 3, :], in_=sc[:, j, 3, :], func=AF.Exp), r=[bsc[j]], w=[bsc[j]])
                P("act", lambda e: e.activation(out=sc[:, j, 6:8, :], in_=pt2[:, 2 * H:4 * H].rearrange("p (a h) -> p a h", a=2),
                                                func=AF.Exp), r=[bpt2], w=[bsc[j]])
                P("dve", lambda e: e.tensor_tensor(out=sc[:, j, 4, :], in0=sc[:, j, 1, :], in1=sc[:, j, 2, :], op=ALU.mult),
                  r=[bsc[j]], w=[bsc[j]])
                P("dve", lambda e: e.tensor_scalar(out=sc[:, j, 5, :], in0=sc[:, j, 1, :], scalar1=-1.0, scalar2=None,
                                                   op0=ALU.mult), r=[bsc[j]], w=[bsc[j]])
            with kb.scope():
                xin_g = [kb.sb([128, 3, HG, SBK + 3], F32, "xin_g") for _ in range(2)]; bxg = [Buf("xin_g%d" % i) for i in range(2)]
                cv_g = [kb.sb([128, 3, HG, SBK], F32, "cv_g") for _ in range(2)]; bcg = [Buf("cv_g%d" % i) for i in range(2)]
                sq_g = kb.sb([128, 2, HG, SBK], F32, "sq_g"); bsg = Buf("sq_g")
                rn_g = kb.sb([128, 2, HG, SBK], F32, "rn_g"); brg = Buf("rn_g")
                for g in range(NG):
                    h0 = g * HG
                    xg = xin_g[g % 2]; bx = bxg[g % 2]; cg = cv_g[g % 2]; bcv_ = bcg[g % 2]
                    segs = (0, 1, 2) if own else (1, 2)
                    s_lo = segs[0]
                    for seg in segs:
                        row0 = c.o_q + seg * KEY + h0 * 128
                        if t0 == 0:
                            kb.dma("sp", xg[:, seg, :, 3:SBK + 3],
                                   PROJ[row0:row0 + HG * 128, t0:t0 + SBK].rearrange("(hh p) t -> p hh t", p=128), r=[b_PROJ], w=[bx])
                            P("pool", lambda e: e.memset(xg[:, seg, :, 0:3], 0.0), w=[bx])
                        else:
                            kb.dma("sp", xg[:, seg, :, 0:SBK + 3],
                                   PROJ[row0:row0 + HG * 128, t0 - 3:t0 + SBK].rearrange("(hh p) t -> p hh t", p=128), r=[b_PROJ], w=[bx])
                            if t0 == TOK:
                                P("dve", lambda e: e.tensor_scalar(out=xg[:, seg, :, 0:3], in0=xg[:, seg, :, 0:3], scalar1=flg[:, 0:1],
                                                                   scalar2=None, op0=ALU.mult), r=[bx, b_const], w=[bx])
                    for j in range(4):
                        for seg in segs:
                            for hh in range(HG):
                                idx = seg * H + h0 + hh
                                if j == 0:
                                    P("dve", lambda e: e.tensor_scalar(out=cg[:, seg, hh, :], in0=xg[:, seg, hh, 0:SBK], scalar1=gcw[:, 0, idx:idx + 1],
                                                                       scalar2=None, op0=ALU.mult), r=[bx, bc], w=[bcv_])
                                else:
                                    P("dve", lambda e: e.scalar_tensor_tensor(out=cg[:, seg, hh, :], in0=xg[:, seg, hh, j:j + SBK],
                                                                              scalar=gcw[:, j, idx:idx + 1], in1=cg[:, seg, hh, :],
                                                                              op0=ALU.mult, op1=ALU.add), r=[bx, bc, bcv_], w=[bcv_])
                    P("act", lambda e: e.activation(out=cg[:, s_lo:3, :, :], in_=cg[:, s_lo:3, :, :], func=AF.Silu), r=[bcv_], w=[bcv_])
                    if own:
                        zr = c.o_z + h0 * 128
                        kb.dma("sp", zs[:, h0:h0 + HG, :], PROJ[zr:zr + HG * 128, t0:t0 + SBK].rearrange("(hh p) t -> p hh t", p=128),
                               r=[b_PROJ], w=[bz[h] for h in range(h0, h0 + HG)])
                        P("act", lambda e: e.activation(out=zs[:, h0:h0 + HG, :], in_=zs[:, h0:h0 + HG, :], func=AF.Silu),
                          r=[bz[h] for h in range(h0, h0 + HG)], w=[bz[h] for h in range(h0, h0 + HG)])
                    nqk = 2 - s_lo
                    P("pool", lambda e: e.tensor_tensor(out=sq_g[:, 0:nqk, :, :], in0=cg[:, s_lo:2, :, :], in1=cg[:, s_lo:2, :, :], op=ALU.mult),
                      r=[bcv_], w=[bsg])
                    sqf = sq_g[:].rearrange("p a b c -> p (a b c)")
                    rnf = rn_g[:].rearrange("p a b c -> p (a b c)")
                    ncol = nqk * HG * SBK
                    for c0 in range(0, ncol, 1024):
                        cn = min(1024, ncol - c0)
                        pb, bpb = nextB()
                        pbf = pb[:].rearrange("p a b c -> p (a b c)")
                        for s0 in range(0, cn, 512):
                            sn = min(512, cn - s0)
                            P("pe", lambda e: e.matmul(pbf[:, s0:s0 + sn], ones[:], sqf[:, c0 + s0:c0 + s0 + sn], start=True, stop=True),
                              r=[bsg, b_const], w=[bpb])
                        P("act", lambda e: e.activation(out=rnf[:, c0:c0 + cn], in_=pbf[:, 0:cn], func=AF.Ln, bias=EPS), r=[bpb], w=[brg])
                    P("act", lambda e: e.activation(out=rn_g[:, 0:nqk, :, :], in_=rn_g[:, 0:nqk, :, :], func=AF.Exp, scale=-0.5), r=[brg], w=[brg])
                    if own:
                        P("dve", lambda e: e.scalar_tensor_tensor(out=qnT[:, h0:h0 + HG, :], in0=cg[:, 0, :, :], scalar=float(128.0 ** -0.5),
                                                                  in1=rn_g[:, 0, :, :], op0=ALU.mult, op1=ALU.mult),
                          r=[bcv_, brg], w=[bq[h] for h in range(h0, h0 + HG)])
                    P("dve", lambda e: e.tensor_tensor(out=cg[:, 1, :, :], in0=cg[:, 1, :, :], in1=rn_g[:, nqk - 1, :, :], op=ALU.mult),
                      r=[bcv_, brg], w=[bcv_])
                    P("pool", lambda e: e.tensor_copy(out=knT[:, h0:h0 + HG, :], in_=cg[:, 1, :, :]), r=[bcv_],
                      w=[bk[h] for h in range(h0, h0 + HG)])
                    for j in range(nt):
                        for seg in (1, 2):
                            pt, bpt = nextC()
                            for hh in range(HG):
                                P("pe", lambda e: e.transpose(pt[:, hh, :], cg[:, seg, hh, j * 128:(j + 1) * 128], ident[:]),
                                  r=[bcv_, b_const], w=[bpt])
                            def bcs(q):
                                return sc[:, j, q, h0:h0 + HG].unsqueeze(2).to_broadcast([128, HG, 128])
                            wk = [bkt[h] for h in range(h0, h0 + HG)]
                            if seg == 1:
                                P("dve", lambda e: e.tensor_tensor(out=kbg[:, h0:h0 + HG, j, :], in0=pt[:, 0:HG, :], in1=bcs(4), op=ALU.mult),
                                  r=[bpt, bsc[j]], w=wk)
                                P("dve", lambda e: e.tensor_tensor(out=kdec[:, h0:h0 + HG, j, :], in0=pt[:, 0:HG, :], in1=bcs(3), op=ALU.mult),
                                  r=[bpt, bsc[j]], w=wk)
                            else:
                                P("dve", lambda e: e.tensor_tensor(out=vb[:, h0:h0 + HG, j, :], in0=pt[:, 0:HG, :], in1=bcs(1), op=ALU.mult),
                                  r=[bpt, bsc[j]], w=wk)
            with kb.scope():
                lhsD = kb.sb([128, H, 128], F32, "lhsD"); blD = [Buf("lhsD") for _ in range(NG)]
                E2 = kb.sb([128, H, 2, 128], F32, "E2"); bE2 = [Buf("E2") for _ in range(NG)]
                Nc = [kb.sb([128, H, 2, 128], BF16, "Nc") for _ in range(2)]
                bNc = [[Buf("Nc") for _ in range(NG)] for _ in range(2)]
                Pm = [kb.sb([128, H, 128], BF16, "Pm") for _ in range(2)]; bPm = [[Buf("Pm") for _ in range(NG)] for _ in range(2)]
                vnew = kb.sb([128, H, 128], BF16, "vnew"); bvn = [Buf("vnew") for _ in range(NG)]
                o2s = kb.sb([128, H, 128], F32, "o2s"); bo2 = [Buf("o2s") for _ in range(NG)]
                oo = kb.sb([128, H, 128], F32, "oo"); boo = Buf("oo")
                osq = kb.sb([128, H, 128], F32, "osq"); bosq = Buf("osq")
                oss = kb.sb([128, 4, H], F32, "oss"); boss = Buf("oss")
                for j in range(nt):
                    def hs(g):
                        return range(g * HG, (g + 1) * HG)
                    cs = slice(j * 128, (j + 1) * 128)
                    for g in range(NG):
                        h0 = g * HG
                        for h in hs(g):
                            P("pool", lambda e: e.tensor_scalar(out=lhsD[:, h, :], in0=UPI[:], scalar1=sc[:, j, 0, h:h + 1], scalar2=None,
                                                                op0=ALU.mult), r=[b_const, bsc[j]], w=[blD[g]])
                        pb, bpb = nextB()
                        for h in hs(g):
                            P("pe", lambda e: e.matmul(pb[:, h - h0, 0, :], lhsD[:, h, :], LOS[:], start=True, stop=True),
                              r=[blD[g], b_const], w=[bpb])
                            P("pe", lambda e: e.matmul(pb[:, h - h0, 1, :], LOS[:], lhsD[:, h, :], start=True, stop=True),
                              r=[blD[g], b_const], w=[bpb])
                        P("act", lambda e: e.activation(out=E2[:, h0:h0 + HG, :, :], in_=pb[:, 0:HG, :, :], func=AF.Exp), r=[bpb], w=[bE2[g]])
                        for h in hs(g):
                            P("pool", lambda e: e.tensor_tensor(out=E2[:, h, :, :], in0=E2[:, h, :, :], in1=MASK2[:], op=ALU.mult),
                              r=[bE2[g], bc], w=[bE2[g]])
                        if G2S < 1:
                            continue
                        pb, bpb = nextB()
                        for h in hs(g):
                            P("pe", lambda e: e.matmul(pb[:, h - h0, 0, :], knT[:, h, cs], knT[:, h, cs], start=True, stop=True),
                              r=[bk[h]], w=[bpb])
                            if own:
                                P("pe", lambda e: e.matmul(pb[:, h - h0, 1, :], knT[:, h, cs], qnT[:, h, cs], start=True, stop=True),
                                  r=[bk[h], bq[h]], w=[bpb])
                        for h in hs(g):
                            P("dve", lambda e: e.scalar_tensor_tensor(out=Nc[0][:, h, 0, :], in0=pb[:, h - h0, 0, :], scalar=sc[:, j, 5, h:h + 1],
                                                                      in1=E2[:, h, 0, :], op0=ALU.mult, op1=ALU.mult),
                              r=[bpb, bsc[j], bE2[g]], w=[bNc[0][g]])
                        if own:
                            P("dve", lambda e: e.tensor_tensor(out=atT[:, h0:h0 + HG, j, :], in0=pb[:, 0:HG, 1, :], in1=E2[:, h0:h0 + HG, 1, :],
                                                               op=ALU.mult), r=[bpb, bE2[g]], w=[bat[h][j] for h in hs(g)])
                    if G2S < 2:
                        continue
                    for g in range(NG):
                        h0 = g * HG
                        pc_ = nextT()
                        for h in hs(g):
                            P("pe", lambda e: e.transpose(pc_[0][:, h - h0, :], Nc[0][:, h, 0, :], identb[:]), r=[bNc[0][g], b_const], w=[pc_[1]])
                        P("act", lambda e: e.copy(out=Nc[0][:, h0:h0 + HG, 1, :], in_=pc_[0][:, 0:HG, :]), r=[pc_[1]], w=[bNc[0][g]])
                        for h in hs(g):
                            P("pool", lambda e: e.tensor_tensor(out=Pm[0][:, h, :], in0=Nc[0][:, h, 1, :], in1=identb[:], op=ALU.add),
                              r=[bNc[0][g], b_const], w=[bPm[0][g]])
                    cur = 0
                    if G2S < 3:
                        continue
                    for lvl in range(1, 6):
                        nx = 1 - cur
                        for g in range(NG):
                            h0 = g * HG
                            pb, bpb = nextB()
                            for h in hs(g):
                                P("pe", lambda e: e.matmul(pb[:, h - h0, 0, :], Nc[cur][:, h, 1, :], Nc[cur][:, h, 0, :], start=True, stop=True),
                                  r=[bNc[cur][g]], w=[bpb])
                                if lvl < 5:
                                    P("pe", lambda e: e.matmul(pb[:, h - h0, 1, :], Nc[cur][:, h, 0, :], Nc[cur][:, h, 1, :], start=True, stop=True),
                                      r=[bNc[cur][g]], w=[bpb])
                            if lvl < 5:
                                P("act", lambda e: e.copy(out=Nc[nx][:, h0:h0 + HG, :, :], in_=pb[:, 0:HG, :, :]), r=[bpb], w=[bNc[nx][g]])
                            else:
                                P("act", lambda e: e.copy(out=Nc[nx][:, h0:h0 + HG, 0, :], in_=pb[:, 0:HG, 0, :]), r=[bpb], w=[bNc[nx][g]])
                            pc2, bpc2 = nextC()
                            for h in hs(g):
                                P("pe", lambda e: e.matmul(pc2[:, h - h0, :], Nc[nx][:, h, 0, :], Pm[cur][:, h, :], start=True, stop=True),
                                  r=[bNc[nx][g], bPm[cur][g]], w=[bpc2])
                            P("dve", lambda e: e.tensor_tensor(out=Pm[nx][:, h0:h0 + HG, :], in0=pc2[:, 0:HG, :], in1=Pm[cur][:, h0:h0 + HG, :],
                                                               op=ALU.add), r=[bpc2, bPm[cur][g]], w=[bPm[nx][g]])
                        cur = nx
                    if G2S < 4:
                        continue
                    for g in range(NG):
                        h0 = g * HG
                        pb, bpb = nextB()
                        for h in hs(g):
                            P("pe", lambda e: e.matmul(pb[:, h - h0, 0, :], Pm[cur][:, h, :], vb[:, h, j, :], start=True, stop=True),
                              r=[bPm[cur][g], bkt[h]], w=[bpb])
                            if G2S >= 5:
                                P("pe", lambda e: e.matmul(pb[:, h - h0, 1, :], kbg[:, h, j, :], Pm[cur][:, h, :], start=True, stop=True),
                                  r=[bPm[cur][g], bkt[h]], w=[bpb])
                        if G2S >= 6:
                            P("act", lambda e: e.copy(out=uu[:, h0:h0 + HG, j, :], in_=pb[:, 0:HG, 0, :]), r=[bpb], w=[bu[h][j] for h in hs(g)])
                        if G2S >= 7:
                            P("dve", lambda e: e.tensor_scalar(out=nwT[:, h0:h0 + HG, j, :], in0=pb[:, 0:HG, 1, :], scalar1=-1.0, scalar2=None,
                                                           op0=ALU.mult), r=[bpb], w=[bnw[h][j] for h in hs(g)])
                if t0 == 0:
                    P("pool", lambda e: e.memset(Sm[:], 0.0), w=bSm)
                    P("pool", lambda e: e.memset(Sb[:], 0.0), w=bSb)
                elif t0 == TOK:
                    P("dve", lambda e: e.tensor_scalar(out=Sm[:], in0=Sm[:], scalar1=flg[:, 0:1], scalar2=None, op0=ALU.mult),
                      r=bSm + [b_const], w=bSm)
                    P("act", lambda e: e.copy(out=Sb[:], in_=Sm[:]), r=bSm, w=bSb)
                for j in range(nt):
                    cs0 = j * 128
                    po1 = [None] * NG
                    for half in range(2):
                        rs = slice(half * 64, half * 64 + 64)
                        cols = slice(cs0 + half * 64, cs0 + half * 64 + 64)
                        for g in range(NG):
                            h0 = g * HG
                            pw, bpw = nextC()
                            for h in hs(g):
                                P("pe", lambda e: e.matmul(pw[rs, h - h0, :], nwT[:, h, j, rs], Sb[:, h, :], start=True, stop=True),
                                  r=[bnw[h][j], bSb[h]], w=[bpw])
                            P("dve", lambda e: e.tensor_tensor(out=vnew[rs, h0:h0 + HG, :], in0=pw[rs, 0:HG, :], in1=uu[rs, h0:h0 + HG, j, :],
                                                               op=ALU.add), r=[bpw] + [bu[h][j] for h in hs(g)], w=[bvn[g]])
                            if own:
                                pq, bpq = nextC()
                                for h in hs(g):
                                    P("pe", lambda e: e.matmul(pq[rs, h - h0, :], qnT[:, h, cols], Sb[:, h, :], start=True, stop=True),
                                      r=[bq[h], bSb[h]], w=[bpq])
                                for h in hs(g):
                                    P("dve", lambda e: e.tensor_scalar(out=o2s[rs, h, :], in0=pq[rs, h - h0, :], scalar1=sc[rs, j, 2, h:h + 1],
                                                                       scalar2=None, op0=ALU.mult), r=[bpq, bsc[j]], w=[bo2[g]])
                            psd, bpsd = nextC()
                            for h in hs(g):
                                P("pe", lambda e: e.matmul(psd[:, h - h0, :], kdec[rs, h, j, :], vnew[rs, h, :], start=True, stop=True),
                                  r=[bkt[h], bvn[g]], w=[bpsd])
                            for h in hs(g):
                                P("dve", lambda e: e.scalar_tensor_tensor(out=Sm[:, h, :], in0=Sm[:, h, :], scalar=sc[:, j, 6 + half, h:h + 1],
                                                                          in1=psd[:, h - h0, :], op0=ALU.mult, op1=ALU.add),
                                  r=[bSm[h], bsc[j], bpsd], w=[bSm[h]])
                            P("act", lambda e: e.copy(out=Sb[:, h0:h0 + HG, :], in_=Sm[:, h0:h0 + HG, :]), r=[bSm[h] for h in hs(g)],
                              w=[bSb[h] for h in hs(g)])
                    if own:
                        for g in range(NG):
                            h0 = g * HG
                            po2, bpo2 = nextC()
                            for h in hs(g):
                                P("pe", lambda e: e.matmul(po2[:, h - h0, :], atT[:, h, j, :], vnew[:, h, :], start=True, stop=True),
                                  r=[bat[h][j], bvn[g]], w=[bpo2])
                            P("dve", lambda e: e.tensor_tensor(out=oo[:, h0:h0 + HG, :], in0=po2[:, 0:HG, :], in1=o2s[:, h0:h0 + HG, :], op=ALU.add),
                              r=[bpo2, bo2[g]], w=[boo])
                        P("pool", lambda e: e.tensor_tensor(out=osq[:], in0=oo[:], in1=oo[:], op=ALU.mult), r=[boo], w=[bosq])
                        P("dve", lambda e: e.tensor_reduce(out=oss[:, 0, :], in_=osq[:], axis=mybir.AxisListType.X, op=ALU.add),
                          r=[bosq], w=[boss])
                        P("dve", lambda e: e.tensor_scalar(out=oss[:, 1, :], in0=oss[:, 0, :], scalar1=1.0 / 128.0, scalar2=EPS,
                                                           op0=ALU.mult, op1=ALU.add), r=[boss], w=[boss])
                        P("act", lambda e: e.activation(out=oss[:, 2, :], in_=oss[:, 1, :], func=AF.Sqrt), r=[boss], w=[boss])
                        P("dve", lambda e: e.reciprocal(out=oss[:, 3, :], in_=oss[:, 2, :]), r=[boss], w=[boss])
                        for h in range(H):
                            P("dve", lambda e: e.tensor_scalar(out=oo[:, h, :], in0=oo[:, h, :], scalar1=oss[:, 3, h:h + 1], scalar2=None,
                                                               op0=ALU.mult), r=[boo, boss], w=[boo])
                        for g in range(NG):
                            h0 = g * HG
                            pt, bpt = nextA()
                            for h in hs(g):
                                P("pe", lambda e: e.transpose(pt[:, (h - h0) * 128:(h - h0 + 1) * 128], oo[:, h, :], ident[:]),
                                  r=[boo, b_const], w=[bpt])
                            for h in hs(g):
                                P("dve", lambda e: e.scalar_tensor_tensor(out=ybT[:, h, cs0:cs0 + 128], in0=pt[:, (h - h0) * 128:(h - h0 + 1) * 128],
                                                                          scalar=onw[:, 0:1], in1=zs[:, h, cs0:cs0 + 128],
                                                                          op0=ALU.mult, op1=ALU.mult), r=[bpt, bc, bz[h]], w=[byb[h]])
            if own:
                for h in range(H):
                    kb.dma("pool", YB[h * 128:(h + 1) * 128, t0 - TOK:t0 - TOK + SBK], ybT[:, h, :], r=[byb[h]], w=[b_YB])


_pT = {}


def kb_ps_bf16(kb, name):
    key = id(kb.es)
    if key not in _pT:
        _pT.clear()
        _pT[key] = ([kb.ps([128, 4, 128], BF16, name) for _ in range(2)], [Buf(name + str(i)) for i in range(2)], [0])
    tl, bl, ctr = _pT[key]
    ctr[0] += 1
    return tl[ctr[0] % 2], bl[ctr[0] % 2]


def tail_phases(kb, c, L):
    P = L["P"]; Gemm = L["Gemm"]; prenorm_block = L["prenorm_block"]
    ident = L["ident"]; ones = L["ones"]; b_const = L["b_const"]; b_mod = L["b_mod"]
    PROJ = L["PROJ"]; b_PROJ = L["b_PROJ"]; YA = L["YA"]; b_YA = L["b_YA"]; YB = L["YB"]; b_YB = L["b_YB"]
    X1 = L["X1"]; b_X1 = L["b_X1"]; ACTT = L["ACTT"]; b_ACTT = L["b_ACTT"]
    w_bl = L["w_bl"]; w_bg = L["w_bg"]; w_out = L["w_out"]; w_up = L["w_up"]; w_dn = L["w_dn"]
    x_own = L["x_own"]; out = L["out"]; b_out = L["b_out"]
    g1w = L["g1w"]; g2w = L["g2w"]; w2s = L["w2s"]; sh2 = L["sh2"]
    D, KC, TOK, NT, LB, H = c.D, c.KC, c.TOK, c.NT, c.LB, c.H
    nt = NT // 128
    NT1 = min(c.NTW, TOK)
    YG = kb.dram("yg", [D, max(NT, NT1)], F32)
    b_YG = Buf("YG")

    def out_gemm_and_epilogue(g, XT, bXT, KCn, wmat, gw, xres, bxres, xres_row0, dst, bdst, dst_row0, NT, nfp=1):
        nt = NT // 128
        rstd = kb.sb([128, nt, 4], F32, "rstd"); brs = Buf("rstd")
        sss = kb.sb([128, NT], F32, "sss"); bsss = Buf("sss")
        es_in = ExitStack(); old_es = kb.es; kb.es = es_in
        g = g()
        ssb = kb.ps([128, max(512, NT)], F32, "ssb"); bssb = PB("ssb")
        ysq = [kb.sb([128, NT], F32, "ysq") for _ in range(2)]; bys = [Buf("ysq%d" % i) for i in range(2)]
        ygs = [kb.sb([128, NT], F32, "ygs") for _ in range(2)]; byg = [Buf("ygs%d" % i) for i in range(2)]
        def mk_evac(f):
            def evac(pap, bpp, f=f):
                q_ = ysq[f % 2]; bq_ = bys[f % 2]; y_ = ygs[f % 2]; by_ = byg[f % 2]
                P("act", lambda e: e.activation(out=q_[:], in_=pap, func=AF.Square), r=[bpp], w=[bq_])
                P("act", lambda e: e.activation(out=y_[:], in_=pap, func=AF.Copy, scale=gw[:, f:f + 1]), r=[bpp, b_mod], w=[by_])
                for s0 in range(0, NT, 512):
                    s1 = min(NT, s0 + 512)
                    P("pe", lambda e: e.matmul(ssb[:, s0:s1], ones[:], q_[:, s0:s1], start=(f == 0), stop=(f == KC - 1)),
                      r=[bq_, b_const], w=[bssb])
                kb.dma("pool", YG[f * 128:(f + 1) * 128, 0:NT], y_[:], r=[by_], w=[b_YG])
            return evac
        g.run_jobs([dict(XT=XT, bXT=bXT, KCn=KCn, wcols=wmat[:, f * 128:(f + nfp) * 128], nf=nfp, M=128,
                         evacs=[mk_evac(f + t) for t in range(nfp)]) for f in range(0, KC, nfp)])
        P("act", lambda e: e.copy(out=sss[:], in_=ssb[:, 0:NT]), r=[bssb], w=[bsss])
        kb.barrier(); es_in.close(); kb.es = old_es
        pss = kb.ps([128, 512], F32, "pss"); bpss = PB("pss")
        for i in range(nt):
            P("pe", lambda e: e.transpose(pss[:, 0:128], sss[:, i * 128:(i + 1) * 128], ident[:]), r=[bsss, b_const], w=[bpss])
            P("dve", lambda e: e.tensor_scalar(out=rstd[:, i, 0:1], in0=pss[:, 0:1], scalar1=1.0 / D, scalar2=EPS, op0=ALU.mult, op1=ALU.add),
              r=[bpss], w=[brs])
            P("act", lambda e: e.activation(out=rstd[:, i, 1:2], in_=rstd[:, i, 0:1], func=AF.Sqrt), r=[brs], w=[brs])
            P("dve", lambda e: e.reciprocal(out=rstd[:, i, 2:3], in_=rstd[:, i, 1:2]), r=[brs], w=[brs])
        xt = [kb.sb([128, D], F32, "ext") for _ in range(2)]; bxt = [Buf("ext%d" % i) for i in range(2)]
        ygt = [kb.sb([128, KC, 128], F32, "ygt") for _ in range(2)]; bygt = [Buf("ygt%d" % i) for i in range(2)]
        tp = [kb.ps([128, 512], F32, "etp") for _ in range(2)]; btp = [PB("etp%d" % i) for i in range(2)]
        YGv = YG.rearrange("(kc p) t -> p kc t", p=128)
        ti = 0
        for i in range(nt):
            x_ = xt[i % 2]; bx_ = bxt[i % 2]; yt = ygt[i % 2]; byt = bygt[i % 2]
            kb.dma("sp", x_[:], xres[xres_row0 + i * 128:xres_row0 + (i + 1) * 128, :], r=[bxres], w=[bx_])
            kb.dma("act" if DUALQ else "sp", yt[:], YGv[:, :, i * 128:(i + 1) * 128], r=[b_YG], w=[byt])
            for g0 in range(0, KC, 4):
                pt = tp[ti % 2]; bp = btp[ti % 2]; ti += 1
                for q in range(4):
                    P("pe", lambda e: e.transpose(pt[:, q * 128:(q + 1) * 128], yt[:, g0 + q, :], ident[:]), r=[byt, b_const], w=[bp])
                P("dve", lambda e: e.scalar_tensor_tensor(out=x_[:, g0 * 128:(g0 + 4) * 128], in0=pt[:], scalar=rstd[:, i, 2:3],
                                                          in1=x_[:, g0 * 128:(g0 + 4) * 128], op0=ALU.mult, op1=ALU.add),
                  r=[bp, brs, bx_], w=[bx_])
            kb.dma("pool", dst[dst_row0 + i * 128:dst_row0 + (i + 1) * 128, :], x_[:], r=[bx_], w=[bdst])

    b_xown = Buf("x_own")
    for blk in range(TOK // NT1):
        c0 = blk * NT1
        with kb.phase():
            MT = kb.sb([128, KC, NT1], BF16, "MT"); bMT = Buf("MT")
            with kb.phase():
                XA = kb.sb([128, LB, NT1], BF16, "XA"); bXA = Buf("XA")
                XB = kb.sb([128, H, NT1], BF16, "XB"); bXB = Buf("XB")
                kb.dma("sp", XA[:], YA.rearrange("(kc p) t -> p kc t", p=128)[:, :, c0:c0 + NT1], r=[b_YA], w=[bXA])
                kb.dma("sp", XB[:], YB.rearrange("(kc p) t -> p kc t", p=128)[:, :, c0:c0 + NT1], r=[b_YB], w=[bXB])
                g = Gemm(min(16, LB), NT1)
                gl = [kb.sb([128, NT1], F32, "gl") for _ in range(2)]; bgl = [Buf("gl%d" % i) for i in range(2)]
                gg_ = [kb.sb([128, NT1], F32, "gg") for _ in range(2)]; bgg_ = [Buf("gg%d" % i) for i in range(2)]
                m1 = [kb.sb([128, NT1], F32, "m1") for _ in range(2)]; bm1 = [Buf("m1%d" % i) for i in range(2)]
                jobs = []
                def mk(f):
                    a_ = gl[f % 2]; ba_ = bgl[f % 2]; b_ = gg_[f % 2]; bb_ = bgg_[f % 2]; m_ = m1[f % 2]; bm_ = bm1[f % 2]
                    def pre():
                        kb.dma("sp", a_[:], PROJ[c.o_gl + f * 128:c.o_gl + (f + 1) * 128, TOK + c0:TOK + c0 + NT1], r=[b_PROJ], w=[ba_])
                        kb.dma("sp", b_[:], PROJ[c.o_gg + f * 128:c.o_gg + (f + 1) * 128, TOK + c0:TOK + c0 + NT1], r=[b_PROJ], w=[bb_])
                        P("act", lambda e: e.activation(out=a_[:], in_=a_[:], func=AF.Sigmoid), r=[ba_], w=[ba_])
                        P("act", lambda e: e.activation(out=b_[:], in_=b_[:], func=AF.Sigmoid), r=[bb_], w=[bb_])
                    def evA(pap, bpp):
                        P("dve", lambda e: e.tensor_tensor(out=m_[:], in0=pap, in1=a_[:], op=ALU.mult), r=[bpp, ba_], w=[bm_])
                    def evB(pap, bpp):
                        P("dve", lambda e: e.tensor_tensor(out=b_[:], in0=pap, in1=b_[:], op=ALU.mult), r=[bpp, bb_], w=[bb_])
                        P("pool", lambda e: e.tensor_tensor(out=MT[:, f, :], in0=m_[:], in1=b_[:], op=ALU.add), r=[bm_, bb_], w=[bMT])
                    jobs.append(dict(XT=XA, bXT=bXA, KCn=LB, wcols=w_bl[:, f * 128:(f + 1) * 128], nf=1, M=128, evacs=[evA], pre=pre))
                    jobs.append(dict(XT=XB, bXT=bXB, KCn=H, wcols=w_bg[:, f * 128:(f + 1) * 128], nf=1, M=128, evacs=[evB]))
                for f in range(KC):
                    mk(f)
                g.run_jobs(jobs)
            with kb.phase():
                out_gemm_and_epilogue(lambda: Gemm(min(16, KC), NT1), MT, bMT, KC, w_out, g1w, x_own, b_xown, c0, X1, b_X1, c0, NT1)

    NTU = min(c.NTW, TOK)
    for blk in range(TOK // NTU):
        c0 = blk * NTU
        with kb.phase():
            XT = kb.sb([128, KC, NTU], BF16, "XT2"); bXT = Buf("XT2")
            prenorm_block(X1, b_X1, c0, NTU, XT, bXT, w2s, sh2, "pn2")
            g = Gemm(min(32, KC), NTU)
            rl = [kb.sb([128, NTU], F32, "rl") for _ in range(2)]; brl = [Buf("rl%d" % i) for i in range(2)]
            ao = [kb.sb([128, NTU], BF16, "ao") for _ in range(3)]; bao = [Buf("ao%d" % i) for i in range(3)]
            jobs = []
            for f in range(c.DFF // 128):
                def evac(pap, bpp, f=f):
                    r_ = rl[f % 2]; br_ = brl[f % 2]; a_ = ao[f % 3]; ba_ = bao[f % 3]
                    P("act", lambda e: e.activation(out=r_[:], in_=pap, func=AF.Relu), r=[bpp], w=[br_])
                    eng = "pool" if f % 2 == 0 else "dve"
                    P(eng, lambda e: e.tensor_tensor(out=a_[:], in0=r_[:], in1=r_[:], op=ALU.mult), r=[br_], w=[ba_])
                    kb.dma("pool", ACTT[f * 128:(f + 1) * 128, c0:c0 + NTU], a_[:], r=[ba_], w=[b_ACTT])
                jobs.append(dict(XT=XT, bXT=bXT, KCn=KC, wcols=w_up[:, f * 128:(f + 1) * 128], nf=1, M=128, evacs=[evac]))
            g.run_jobs(jobs)

    KF = c.DFF // 128
    for blk in range(TOK // NT):
        c0 = blk * NT
        with kb.phase():
            XD = kb.sb([128, KF, NT], BF16, "XD"); bXD = Buf("XD")
            AV = ACTT.rearrange("(kc p) t -> p kc t", p=128)
            for k0 in range(0, KF, 16):
                kn = min(16, KF - k0)
                kb.dma("sp", XD[:, k0:k0 + kn, :], AV[:, k0:k0 + kn, c0:c0 + NT], r=[b_ACTT], w=[bXD])
            out_gemm_and_epilogue(lambda: Gemm(min(16, KF), NT, nacc=4), XD, bXD, KF, w_dn, g2w, X1, b_X1, c0, out, b_out, c0, NT, nfp=(2 if KC % 2 == 0 else 1))


def make_in_maps(inp, cfg, ncores):
    c = cfg
    f = lambda a: np.ascontiguousarray(np.asarray(a, dtype=np.float32))
    shared = {
        "w_ada": f(inp["w_ada"][0]), "b_ada": f(inp["b_ada"][0]),
        "mix_pre_norm": f(inp["mix_pre_norm"][0]), "mix_post_norm": f(inp["mix_post_norm"][0]),
        "w_in": f(inp["w_in"][0]),
        "lru_conv_w": f(inp["lru_conv_w"][0]), "lru_conv_b": f(inp["lru_conv_b"][0]),
        "lru_gate_a_w": f(inp["lru_gate_a_w"][0]), "lru_gate_a_b": f(inp["lru_gate_a_b"][0]).reshape(-1),
        "lru_gate_i_w": f(inp["lru_gate_i_w"][0]), "lru_gate_i_b": f(inp["lru_gate_i_b"][0]).reshape(-1),
        "lru_lambda": f(inp["lru_lambda"][0]),
        "gdn_conv_w": f(inp["gdn_conv_w"][0]), "gdn_a_log": f(inp["gdn_a_log"][0]).reshape(-1, 1),
        "gdn_dt_bias": f(inp["gdn_dt_bias"][0]).reshape(-1, 1), "gdn_out_norm": f(inp["gdn_out_norm"][0]),
        "w_branch_lru": f(inp["w_branch_lru"][0]), "w_branch_gdn": f(inp["w_branch_gdn"][0]), "w_out": f(inp["w_out"][0]),
        "mlp_pre_norm": f(inp["mlp_pre_norm"][0]), "mlp_post_norm": f(inp["mlp_post_norm"][0]),
        "w_mlp_up": f(inp["w_mlp_up"][0]), "w_mlp_down": f(inp["w_mlp_down"][0]),
    }
    x = np.asarray(inp["x"], dtype=np.float32); cc = np.asarray(inp["c"], dtype=np.float32)
    maps = []
    for i in range(ncores):
        b, half = i // 2, i % 2
        m = dict(shared)
        m["x_own"] = np.ascontiguousarray(x[b, half * c.TOK:(half + 1) * c.TOK])
        m["x_pre"] = np.ascontiguousarray(x[b, 0:c.TOK])
        m["c"] = np.ascontiguousarray(cc[b].reshape(c.KC, 128))
        m["flag"] = np.full((128, 1), float(half), dtype=np.float32)
        maps.append(m)
    return maps


_CACHE = {}


def kernel(**inputs):
    cfg = Cfg(D=4096, T=4096, NT=512, SBK=256, PL=1024, NTW=1024)
    if "nc" not in _CACHE:
        _CACHE["nc"] = build(cfg)
    nc = _CACHE["nc"]
    maps = make_in_maps(inputs, cfg, 8)
    res = run_bass_kernel_spmd(nc, maps, core_ids=list(range(8)))
    outp = np.zeros((4, 4096, 4096), dtype=np.float32)
    for i in range(8):
        b, half = i // 2, i % 2
        outp[b, half * cfg.TOK:(half + 1) * cfg.TOK] = res.results[i]["out"]
    return outp
```

```python
import numpy as np
from contextlib import ExitStack, contextmanager
import concourse.bass as bass
import concourse.mybir as mybir
from concourse.bass_utils import run_bass_kernel_spmd

F32 = mybir.dt.float32
BF16 = mybir.dt.bfloat16
AF = mybir.ActivationFunctionType
ALU = mybir.AluOpType
EPS = 1e-6
SEM_LIMIT = 30000
import os
GSTOP = int(os.environ.get('GSTOP', '9'))
DUALQ = int(os.environ.get('DUALQ', '0'))
G2S = int(os.environ.get('G2S', '9'))
NDSEM = 40


class Cfg:
    def __init__(s, D, T, NT, SBK, PL, debug=False, NTW=None):
        s.D = D; s.T = T; s.NT = NT; s.SBK = SBK; s.PL = PL; s.debug = debug; s.NTW = NTW or NT
        s.LW = D // 2; s.LB = s.LW // 128; s.H = (D // 2) // 128; s.KEY = s.H * 128; s.VAL = s.H * 128
        s.DFF = 4 * D; s.TOK = T // 2; s.KC = D // 128
        s.o_lx = 0; s.o_lg = s.LW; s.o_q = 2 * s.LW; s.o_k = s.o_q + s.KEY; s.o_v = s.o_k + s.KEY
        s.o_z = s.o_v + s.VAL; s.o_a = s.o_z + s.VAL; s.o_b = s.o_a + s.H; s.o_gl = s.o_b + s.H; s.o_gg = s.o_gl + D
        s.INW = s.o_gg + D


class Buf:
    __slots__ = ("name", "lw", "rd", "excl")

    def __init__(s, name, excl=False):
        s.name = name; s.lw = None; s.rd = {}; s.excl = excl


def PB(name):
    return Buf(name, True)


class KB:
    def __init__(s, cfg):
        s.cfg = cfg
        s.nc = bass.Bass("TRN2", target_bir_lowering=False)
        nc = s.nc
        s.E = {"pe": nc.tensor, "act": nc.scalar, "dve": nc.vector, "pool": nc.gpsimd, "sp": nc.sync}
        s.root = ExitStack()
        s.es = s.root
        s.semh = {}
        s.sem = {}; s.cnt = {}; s.seen = {e: {} for e in s.E}
        s.nsem = 0
        for e in ("pe", "act", "dve", "pool"):
            s.sem[e] = s._newsem(); s.cnt[e] = 0
        s.dsem = [s._newsem() for _ in range(NDSEM)]
        s.dval = [0] * NDSEM
        s.dnext = 0
        s.uid = 0
        s.block = s.root.enter_context(nc.Block())

    def _newsem(s):
        s.nsem += 1
        name = "s%d" % s.nsem
        h = s.root.enter_context(s.nc.semaphore(name))
        s.semh[name] = h
        return name

    def sb(s, shape, dt, name=None):
        s.uid += 1
        t = s.es.enter_context(s.nc.sbuf_tensor("%s_%d" % (name or "t", s.uid), list(shape), dt))
        return t

    def ps(s, shape, dt=F32, name=None):
        s.uid += 1
        return s.es.enter_context(s.nc.psum_tensor("%s_%d" % (name or "p", s.uid), list(shape), dt))

    def dram(s, name, shape, dt, kind=None):
        k = kind or ("ExternalOutput" if s.cfg.debug else "Internal")
        return s.nc.dram_tensor(name, list(shape), dt, kind=k).ap()

    @contextmanager
    def phase(s):
        s.barrier()
        old = s.es
        es = ExitStack()
        s.es = es
        try:
            yield
        finally:
            s.barrier()
            es.close()
            s.es = old

    @contextmanager
    def scope(s):
        old = s.es
        es = ExitStack()
        s.es = es
        try:
            yield
        finally:
            s.barrier()
            es.close()
            s.es = old

    def _need(s, eng, reads, writes):
        toks = []
        for b in reads:
            if b.lw:
                toks.append(b.lw)
        for b in writes:
            if b.lw and not (eng == "pe" and b.lw[2] == "pe"):
                toks.append(b.lw)
            for k, v in b.rd.items():
                toks.append((k, v, None))
        return toks

    def _wait(s, eng, toks):
        mx = {}
        for t in toks:
            if t[1] > mx.get(t[0], 0):
                mx[t[0]] = t[1]
        for k, v in mx.items():
            if s.seen[eng].get(k, 0) < v:
                s.E[eng].wait_ge(s.semh[k], v)
                s.seen[eng][k] = v

    def op(s, eng, fn, r=(), w=()):
        w = list(w)
        for b in r:
            if b.excl and b not in w:
                w.append(b)
        s._wait(eng, s._need(eng, r, w))
        if s.cnt[eng] >= SEM_LIMIT:
            s.sem[eng] = s._newsem(); s.cnt[eng] = 0
        ins = fn(s.E[eng])
        s.cnt[eng] += 1
        k = s.sem[eng]
        ins.then_inc(s.semh[k], 1)
        tok = (k, s.cnt[eng], eng)
        for b in r:
            if b.rd.get(k, 0) < tok[1]:
                b.rd[k] = tok[1]
        for b in w:
            b.lw = tok; b.rd = {}
        return ins

    def dma(s, q, out, in_, r=(), w=()):
        i = s.dnext
        s.dnext = (i + 1) % NDSEM
        k = s.dsem[i]; pv = s.dval[i]
        toks = s._need("dma", r, w)
        if pv:
            toks.append((k, pv, None))
        s._wait(q, toks)
        s.E[q].dma_start(out=out, in_=in_).then_inc(s.semh[k], 16)
        s.dval[i] = pv + 16
        tok = (k, pv + 16, "dma")
        for b in r:
            b.rd[k] = tok[1]
        for b in w:
            b.lw = tok; b.rd = {}

    def barrier(s):
        toks = [(s.sem[e], s.cnt[e], None) for e in s.cnt if s.cnt[e] > 0]
        toks += [(s.dsem[i], s.dval[i], None) for i in range(NDSEM) if s.dval[i] > 0]
        for e in s.E:
            s._wait(e, toks)


_DBG = {}


def build(cfg):
    kb = KB(cfg)
    _DBG['kb'] = kb
    nc = kb.nc
    c = cfg
    D, KC, TOK, NT, H, LB, LW = c.D, c.KC, c.TOK, c.NT, c.H, c.LB, c.LW
    TT2 = 2 * TOK
    def din(name, shape):
        return nc.dram_tensor(name, list(shape), F32, kind="ExternalInput").ap()
    x_own = din("x_own", [TOK, D]); x_pre = din("x_pre", [TOK, D]); cvec = din("c", [KC, 128]); flag = din("flag", [128, 1])
    w_ada = din("w_ada", [D, 6 * D]); b_ada = din("b_ada", [6 * D])
    n_pre1 = din("mix_pre_norm", [D]); n_post1 = din("mix_post_norm", [D])
    w_in = din("w_in", [D, c.INW])
    lru_cw = din("lru_conv_w", [4, LW]); lru_cb = din("lru_conv_b", [LW])
    lru_aw = din("lru_gate_a_w", [LB, 128, 128]); lru_ab = din("lru_gate_a_b", [LW])
    lru_iw = din("lru_gate_i_w", [LB, 128, 128]); lru_ib = din("lru_gate_i_b", [LW])
    lru_lam = din("lru_lambda", [LW])
    gdn_cw = din("gdn_conv_w", [4, 3 * c.KEY]); gdn_alog = din("gdn_a_log", [H, 1]); gdn_dtb = din("gdn_dt_bias", [H, 1])
    gdn_onw = din("gdn_out_norm", [128])
    w_bl = din("w_branch_lru", [LW, D]); w_bg = din("w_branch_gdn", [c.VAL, D]); w_out = din("w_out", [D, D])
    n_pre2 = din("mlp_pre_norm", [D]); n_post2 = din("mlp_post_norm", [D])
    w_up = din("w_mlp_up", [D, c.DFF]); w_dn = din("w_mlp_down", [c.DFF, D])
    out = nc.dram_tensor("out", [TOK, D], F32, kind="ExternalOutput").ap()
    MODV = kb.dram("modv", [6 * D], F32)
    NREC = LW + 3 * c.KEY + 2 * H
    PROJR = kb.dram("projr", [NREC, TT2], F32)
    PROJN = kb.dram("projn", [c.INW - NREC, TOK], F32)

    class _Proj:
        def __getitem__(s, key):
            rs, cs = key
            r0, r1 = rs.start, rs.stop
            if r0 < c.o_lg:
                return PROJR[r0:r1, cs]
            if c.o_q <= r0 < c.o_z:
                return PROJR[r0 - c.o_q + LW:r1 - c.o_q + LW, cs]
            if c.o_a <= r0 < c.o_gl:
                return PROJR[r0 - c.o_a + LW + 3 * c.KEY:r1 - c.o_a + LW + 3 * c.KEY, cs]
            cs2 = slice(cs.start - TOK, cs.stop - TOK)
            assert cs2.start >= 0
            if r0 < c.o_q:
                return PROJN[r0 - c.o_lg:r1 - c.o_lg, cs2]
            if r0 < c.o_a:
                return PROJN[r0 - c.o_z + LW:r1 - c.o_z + LW, cs2]
            return PROJN[r0 - c.o_gl + LW + c.VAL:r1 - c.o_gl + LW + c.VAL, cs2]
    PROJ = _Proj()
    YA = kb.dram("ya", [LW, TOK], BF16)
    YB = kb.dram("yb", [c.VAL, TOK], BF16)
    X1 = kb.dram("x1", [TOK, D], F32)
    ACTT = kb.dram("actt", [c.DFF, TOK], BF16)
    b_PROJ = Buf("PROJ"); b_YA = Buf("YA"); b_YB = Buf("YB"); b_X1 = Buf("X1"); b_ACTT = Buf("ACTT"); b_MODV = Buf("MODV")
    b_out = Buf("out")

    ident = kb.sb([128, 128], F32, "ident"); identb = kb.sb([128, 128], BF16, "identb")
    ones = kb.sb([128, 128], F32, "ones")
    UPI = kb.sb([128, 128], F32, "UPI"); LOS = kb.sb([128, 128], F32, "LOS")
    BLK = kb.sb([128, 128], F32, "BLK"); CHA = kb.sb([128, 128], F32, "CHA"); CHB = kb.sb([128, 128], F32, "CHB")
    flg = kb.sb([128, 1], F32, "flg")
    NV = 6 * KC
    modfm = kb.sb([128, NV], F32, "modfm")
    w1s = kb.sb([128, KC], F32, "w1s"); w2s = kb.sb([128, KC], F32, "w2s")
    g1w = kb.sb([128, KC], F32, "g1w"); g2w = kb.sb([128, KC], F32, "g2w")
    b_const = Buf("const"); b_mod = Buf("mod")

    def P(eng, fn, r=(), w=()):
        return kb.op(eng, fn, r, w)

    P("pool", lambda e: e.memset(ident[:], 1.0), w=[b_const])
    P("pool", lambda e: e.affine_select(out=ident[:], in_=ident[:], pattern=[[-1, 128]], compare_op=ALU.is_equal,
                                        fill=0.0, base=0, channel_multiplier=1), r=[b_const], w=[b_const])
    P("pool", lambda e: e.tensor_copy(out=identb[:], in_=ident[:]), r=[b_const], w=[b_const])
    P("pool", lambda e: e.memset(ones[:], 1.0), w=[b_const])
    P("pool", lambda e: e.memset(UPI[:], 1.0), w=[b_const])
    P("pool", lambda e: e.affine_select(out=UPI[:], in_=UPI[:], pattern=[[1, 128]], compare_op=ALU.is_ge,
                                        fill=0.0, base=0, channel_multiplier=-1), r=[b_const], w=[b_const])
    P("pool", lambda e: e.memset(UPI[0:64, 64:128], 0.0), r=[b_const], w=[b_const])
    P("pool", lambda e: e.memset(LOS[:], 1.0), w=[b_const])
    P("pool", lambda e: e.affine_select(out=LOS[:], in_=LOS[:], pattern=[[-1, 128]], compare_op=ALU.is_gt,
                                        fill=0.0, base=0, channel_multiplier=1), r=[b_const], w=[b_const])
    P("pool", lambda e: e.memset(LOS[64:128, 0:64], 0.0), r=[b_const], w=[b_const])
    P("pool", lambda e: e.memset(BLK[:], 0.0), w=[b_const])
    P("pool", lambda e: e.memset(BLK[0:64, 0:64], 1.0), r=[b_const], w=[b_const])
    P("pool", lambda e: e.memset(BLK[64:128, 64:128], 1.0), r=[b_const], w=[b_const])
    P("pool", lambda e: e.memset(CHA[:], 0.0), w=[b_const])
    P("pool", lambda e: e.memset(CHA[0:64, :], 1.0), r=[b_const], w=[b_const])
    P("pool", lambda e: e.memset(CHB[:], 0.0), w=[b_const])
    P("pool", lambda e: e.memset(CHB[64:128, :], 1.0), r=[b_const], w=[b_const])
    kb.dma("sp", flg[:], flag[:, :], w=[b_const])

    def load_vec_fm(vec_ap, n, dst_ap, rbufs=(), wbuf=None):
        v2 = vec_ap.rearrange("(n p) -> n p", p=128)
        with ExitStack() as es:
            old = kb.es; kb.es = es
            for g0 in range(0, n, 128):
                gn = min(128, n - g0)
                st = kb.sb([128, 128], F32, "lv"); pt = kb.ps([128, 128], F32, "lvp")
                bs = Buf("lv"); bp = PB("lvp")
                kb.dma("sp", st[0:gn, :], v2[g0:g0 + gn, :], r=list(rbufs), w=[bs])
                P("pe", lambda e: e.transpose(pt[:, 0:gn], st[0:gn, :], ident[0:gn, 0:gn]), r=[bs, b_const], w=[bp])
                P("dve", lambda e: e.tensor_copy(out=dst_ap[:, g0:g0 + gn], in_=pt[:, 0:gn]), r=[bp], w=[wbuf])
            kb.barrier()
            kb.es = old

    with kb.phase():
        cin = kb.sb([128, 128], F32, "cin"); cact = kb.sb([128, 128], F32, "cact"); cT = kb.sb([128, KC], F32, "cT")
        cps = kb.ps([128, 128], F32, "cps")
        b_c = Buf("c"); b_cp = PB("cp"); b_cT = Buf("cT")
        kb.dma("sp", cin[0:KC, :], cvec[:, :], w=[b_c])
        P("act", lambda e: e.activation(out=cact[0:KC, :], in_=cin[0:KC, :], func=AF.Silu), r=[b_c], w=[b_c])
        P("pe", lambda e: e.transpose(cps[:, 0:KC], cact[0:KC, :], ident[0:KC, 0:KC]), r=[b_c, b_const], w=[b_cp])
        P("dve", lambda e: e.tensor_copy(out=cT[:], in_=cps[:, 0:KC]), r=[b_cp], w=[b_cT])
        KP = min(8, KC)
        NPC = KC // KP
        NAW = 8
        wst = [kb.sb([128, KP, 512], F32, "adaw") for _ in range(NAW)]
        bw = [Buf("adaw%d" % i) for i in range(NAW)]
        mps = [kb.ps([128, 512], F32, "mps") for _ in range(2)]
        bmp = [PB("mps%d" % i) for i in range(2)]
        mst = [kb.sb([1, 512], F32, "mst") for _ in range(2)]
        bms = [Buf("mst%d" % i) for i in range(2)]
        wv = w_ada.rearrange("(kc p) n -> p kc n", p=128)
        li = 0
        for nt in range(6 * D // 512):
            pp = mps[nt % 2]; bpp = bmp[nt % 2]
            for pc in range(NPC):
                wt = wst[li % NAW]; bwt = bw[li % NAW]; li += 1
                kb.dma("sp", wt[:], wv[:, pc * KP:(pc + 1) * KP, nt * 512:(nt + 1) * 512], w=[bwt])
                for j in range(KP):
                    kc = pc * KP + j
                    P("pe", lambda e, kc=kc, j=j, wt=wt, pp=pp: e.matmul(pp[0:1, :], cT[:, kc:kc + 1], wt[:, j, :],
                                                                        start=(kc == 0), stop=(kc == KC - 1)),
                      r=[b_cT, bwt], w=[bpp])
            ms = mst[nt % 2]; bm = bms[nt % 2]
            P("act", lambda e, ms=ms, pp=pp: e.copy(out=ms[:], in_=pp[0:1, :]), r=[bpp], w=[bm])
            kb.dma("pool", MODV[nt * 512:(nt + 1) * 512].rearrange("(o n) -> o n", o=1), ms[:], r=[bm], w=[b_MODV])
        load_vec_fm(MODV, NV, modfm, rbufs=[b_MODV], wbuf=b_mod)
        tmpv = kb.sb([128, NV], F32, "tmpv"); b_tmp = Buf("tmpv")
        load_vec_fm(b_ada, NV, tmpv, wbuf=b_tmp)
        P("dve", lambda e: e.tensor_tensor(out=modfm[:], in0=modfm[:], in1=tmpv[:], op=ALU.add), r=[b_mod, b_tmp], w=[b_mod])
        nv = kb.sb([128, 4, KC], F32, "nv"); b_nv = Buf("nv")
        for i, v in enumerate([n_pre1, n_post1, n_pre2, n_post2]):
            load_vec_fm(v, KC, nv[:, i, :], wbuf=b_nv)
        P("dve", lambda e: e.scalar_tensor_tensor(out=w1s[:], in0=modfm[:, KC:2 * KC], scalar=1.0, in1=nv[:, 0, :],
                                                  op0=ALU.add, op1=ALU.mult), r=[b_mod, b_nv], w=[b_mod])
        P("dve", lambda e: e.scalar_tensor_tensor(out=w2s[:], in0=modfm[:, 4 * KC:5 * KC], scalar=1.0, in1=nv[:, 2, :],
                                                  op0=ALU.add, op1=ALU.mult), r=[b_mod, b_nv], w=[b_mod])
        P("dve", lambda e: e.tensor_tensor(out=g1w[:], in0=modfm[:, 2 * KC:3 * KC], in1=nv[:, 1, :], op=ALU.mult),
          r=[b_mod, b_nv], w=[b_mod])
        P("dve", lambda e: e.tensor_tensor(out=g2w[:], in0=modfm[:, 5 * KC:6 * KC], in1=nv[:, 3, :], op=ALU.mult),
          r=[b_mod, b_nv], w=[b_mod])
    sh1 = modfm[:, 0:KC]; sh2 = modfm[:, 3 * KC:4 * KC]

    cast_rr = [0]

    def prenorm_block(xsrc, bsrc, t0, ntok, XT, bXT, ws, sh, tag):
        with ExitStack() as es:
            old = kb.es; kb.es = es
            xt = [kb.sb([128, D], F32, "xt") for _ in range(2)]; bx = [Buf("xt%d" % i) for i in range(2)]
            junk = kb.sb([128, D], BF16, "junk"); bj = Buf("junk")
            st = [kb.sb([128, 4], F32, "st") for _ in range(2)]; bst = [Buf("st%d" % i) for i in range(2)]
            tp = [kb.ps([128, 512], F32, "tp") for _ in range(2)]; btp = [PB("tp%d" % i) for i in range(2)]
            for i in range(ntok // 128):
                x_ = xt[i % 2]; b_ = bx[i % 2]; s_ = st[i % 2]; bs_ = bst[i % 2]
                kb.dma("sp", x_[:], xsrc[t0 + i * 128:t0 + (i + 1) * 128, :], r=[bsrc], w=[b_])
                P("act", lambda e: e.activation(out=junk[:], in_=x_[:], func=AF.Square, accum_out=s_[:, 0:1]),
                  r=[b_], w=[bj, bs_])
                P("dve", lambda e: e.tensor_scalar(out=s_[:, 1:2], in0=s_[:, 0:1], scalar1=1.0 / D, scalar2=EPS,
                                                   op0=ALU.mult, op1=ALU.add), r=[bs_], w=[bs_])
                P("act", lambda e: e.activation(out=s_[:, 2:3], in_=s_[:, 1:2], func=AF.Sqrt), r=[bs_], w=[bs_])
                P("dve", lambda e: e.reciprocal(out=s_[:, 3:4], in_=s_[:, 2:3]), r=[bs_], w=[bs_])
                P("act", lambda e: e.activation(out=x_[:], in_=x_[:], func=AF.Identity, scale=s_[:, 3:4]),
                  r=[b_, bs_], w=[b_])
                for g in range(KC // 4 if KC >= 4 else 1):
                    pt = tp[g % 2]; bp = btp[g % 2]
                    nq = min(4, KC)
                    for q in range(nq):
                        kc = g * 4 + q
                        P("pe", lambda e, kc=kc, q=q, pt=pt: e.transpose(pt[:, q * 128:(q + 1) * 128],
                                                                         x_[:, kc * 128:(kc + 1) * 128], ident[:]),
                          r=[b_, b_const], w=[bp])
                    for q in range(nq):
                        kc = g * 4 + q
                        eng = "dve" if q % 2 == 0 else "pool"
                        eng = "dve"
                        P(eng, lambda e, kc=kc, q=q, pt=pt: e.tensor_scalar(
                            out=XT[:, kc, i * 128:(i + 1) * 128], in0=pt[:, q * 128:(q + 1) * 128],
                            scalar1=ws[:, kc:kc + 1], scalar2=sh[:, kc:kc + 1], op0=ALU.mult, op1=ALU.add),
                          r=[bp, b_mod], w=[bXT])
            kb.barrier()
            kb.es = old

    class Gemm:
        def __init__(g, kpc, ntok, nacc=2):
            g.kpc = kpc; g.ntok = ntok
            g.nws = 4
            g.wst = [kb.sb([128, kpc, 128], F32, "wst") for _ in range(g.nws)]; g.bws = [Buf("wst%d" % i) for i in range(g.nws)]
            g.wbf = [kb.sb([128, kpc, 128], BF16, "wbf") for _ in range(3)]; g.bwb = [Buf("wbf%d" % i) for i in range(3)]
            g.acc = [kb.ps([128, max(512, ntok)], F32, "acc") for _ in range(nacc)]; g.bacc = [PB("acc%d" % i) for i in range(nacc)]
            g.nacc = nacc
            g.li = 0; g.ai = 0

        def run_multi(g, XT, bXT, KCn, wcols, nf, evacs):
            ntok = g.ntok
            accs = []
            for t in range(nf):
                accs.append((g.acc[g.ai % g.nacc], g.bacc[g.ai % g.nacc])); g.ai += 1
            wv = wcols.rearrange("(kc p) m -> p kc m", p=128)
            kp = g.kpc // nf
            W = nf * 128
            for p0 in range(0, KCn, kp):
                pn = min(kp, KCn - p0)
                ws = g.wst[g.li % g.nws]; bws = g.bws[g.li % g.nws]
                wb = g.wbf[g.li % 3]; bwb = g.bwb[g.li % 3]; g.li += 1
                wsv = ws[:].rearrange("p a b -> p (a b)").rearrange("p (a b) -> p a b", b=W)
                wbv = wb[:].rearrange("p a b -> p (a b)").rearrange("p (a b) -> p a b", b=W)
                kb.dma("sp", wsv[:, 0:pn, :], wv[:, p0:p0 + pn, :], w=[bws])
                ce = ("dve", "act")[cast_rr[0] % 2]; cast_rr[0] += 1
                if ce == "act":
                    P("act", lambda e: e.copy(out=wbv[:, 0:pn, :], in_=wsv[:, 0:pn, :]), r=[bws], w=[bwb])
                else:
                    P(ce, lambda e: e.tensor_copy(out=wbv[:, 0:pn, :], in_=wsv[:, 0:pn, :]), r=[bws], w=[bwb])
                for j in range(pn):
                    kc = p0 + j
                    for t in range(nf):
                        pp, bpp = accs[t]
                        for s0 in range(0, ntok, 512):
                            s1 = min(ntok, s0 + 512)
                            P("pe", lambda e: e.matmul(pp[:, s0:s1], wbv[:, j, t * 128:(t + 1) * 128], XT[:, kc, s0:s1],
                                                      start=(kc == 0), stop=(kc == KCn - 1)), r=[bwb, bXT], w=[bpp])
            for t in range(nf):
                evacs[t](accs[t][0][:, 0:ntok], accs[t][1])

        def run_jobs(g, jobs, LA_D=int(os.environ.get("LAD", "3")), LA_C=int(os.environ.get("LAC", "1"))):
            ntok = g.ntok
            pieces = []
            for jb in jobs:
                kp = g.kpc // jb["nf"]
                for p0 in range(0, jb["KCn"], kp):
                    pieces.append((jb, p0, min(kp, jb["KCn"] - p0)))
            n = len(pieces)
            base = g.li
            g.li += n
            st = {"d": 0, "c": 0}
            views = {}

            def bufs(idx):
                gi = base + idx
                jb, p0, pn = pieces[idx]
                nf = jb["nf"]
                ws = g.wst[gi % g.nws]; bws = g.bws[gi % g.nws]
                wb = g.wbf[gi % 3]; bwb = g.bwb[gi % 3]
                if nf > 1:
                    W = nf * 128
                    wsv = ws[:].rearrange("p a b -> p (a b)").rearrange("p (a b) -> p a b", b=W)[:, 0:pn, :]
                    wbv = wb[:].rearrange("p a b -> p (a b)").rearrange("p (a b) -> p a b", b=W)
                else:
                    M = jb["M"]
                    wsv = ws[:, 0:pn, 0:M]
                    wbv = wb
                return wsv, bws, wbv, bwb

            def emit_dma(idx):
                jb, p0, pn = pieces[idx]
                wsv, bws, wbv, bwb = bufs(idx)
                wv = jb["wcols"].rearrange("(kc p) m -> p kc m", p=128)
                kb.dma("sp", wsv, wv[:, p0:p0 + pn, :], w=[bws])

            def emit_cast(idx):
                jb, p0, pn = pieces[idx]
                if p0 == 0 and jb.get("pre"):
                    jb["pre"]()
                wsv, bws, wbv, bwb = bufs(idx)
                dst = wbv[:, 0:pn, :] if jb["nf"] > 1 else wbv[:, 0:pn, 0:jb["M"]]
                ce = ("dve", "act")[cast_rr[0] % 2]; cast_rr[0] += 1
                if ce == "act":
                    P("act", lambda e: e.copy(out=dst, in_=wsv), r=[bws], w=[bwb])
                else:
                    P(ce, lambda e: e.tensor_copy(out=dst, in_=wsv), r=[bws], w=[bwb])

            accs = None
            for idx in range(n):
                while st["d"] <= min(idx + LA_D, n - 1):
                    emit_dma(st["d"]); st["d"] += 1
                while st["c"] <= min(idx + LA_C, n - 1):
                    emit_cast(st["c"]); st["c"] += 1
                jb, p0, pn = pieces[idx]
                nf = jb["nf"]; KCn = jb["KCn"]; M = jb["M"]; XT = jb["XT"]; bXT = jb["bXT"]
                if p0 == 0:
                    accs = []
                    for t in range(nf):
                        accs.append((g.acc[g.ai % g.nacc], g.bacc[g.ai % g.nacc])); g.ai += 1
                wsv, bws, wbv, bwb = bufs(idx)
                for j in range(pn):
                    kc = p0 + j
                    for t in range(nf):
                        pp, bpp = accs[t]
                        lhs = wbv[:, j, t * 128:(t + 1) * 128] if nf > 1 else wbv[:, j, 0:M]
                        for s0 in range(0, ntok, 512):
                            s1 = min(ntok, s0 + 512)
                            P("pe", lambda e: e.matmul(pp[0:M, s0:s1], lhs, XT[:, kc, s0:s1],
                                                      start=(kc == 0), stop=(kc == KCn - 1)), r=[bwb, bXT], w=[bpp])
                if p0 + pn >= KCn:
                    for t in range(nf):
                        jb["evacs"][t](accs[t][0][0:M, 0:ntok], accs[t][1])

        def run(g, XT, bXT, KCn, wcols, M, evac, start_acc=True):
            ntok = g.ntok
            pp = g.acc[g.ai % 2]; bpp = g.bacc[g.ai % 2]; g.ai += 1
            wv = wcols.rearrange("(kc p) m -> p kc m", p=128)
            for p0 in range(0, KCn, g.kpc):
                pn = min(g.kpc, KCn - p0)
                ws = g.wst[g.li % g.nws]; bws = g.bws[g.li % g.nws]
                wb = g.wbf[g.li % 3]; bwb = g.bwb[g.li % 3]; g.li += 1
                kb.dma(("sp", "act")[g.li % 2] if DUALQ else "sp", ws[:, 0:pn, 0:M], wv[:, p0:p0 + pn, :], w=[bws])
                ce = ("dve", "act")[cast_rr[0] % 2]; cast_rr[0] += 1
                if ce == "act":
                    P("act", lambda e: e.copy(out=wb[:, 0:pn, 0:M], in_=ws[:, 0:pn, 0:M]), r=[bws], w=[bwb])
                else:
                    P(ce, lambda e: e.tensor_copy(out=wb[:, 0:pn, 0:M], in_=ws[:, 0:pn, 0:M]), r=[bws], w=[bwb])
                for j in range(pn):
                    kc = p0 + j
                    for s0 in range(0, ntok, 512):
                        s1 = min(ntok, s0 + 512)
                        P("pe", lambda e, j=j, kc=kc: e.matmul(pp[0:M, s0:s1], wb[:, j, 0:M], XT[:, kc, s0:s1],
                                                              start=(kc == 0), stop=(kc == KCn - 1)),
                          r=[bwb, bXT], w=[bpp])
            evac(pp[0:M, 0:ntok], bpp)

    segs_rec = [(c.o_lx, LW), (c.o_q, 3 * c.KEY), (c.o_a, H), (c.o_b, H)]
    segs_non = [(c.o_lg, LW), (c.o_z, c.VAL), (c.o_gl, D), (c.o_gg, D)]

    def win_pass(xsrc, tcol0, segs):
        NT = min(c.NTW, TOK)
        for blk in range(TOK // NT):
            with kb.phase():
                XT = kb.sb([128, KC, NT], BF16, "XT"); bXT = Buf("XT")
                prenorm_block(xsrc, Buf("xin"), blk * NT, NT, XT, bXT, w1s, sh1, "pn1")
                g = Gemm(min(KC, 32), NT)
                ost = [kb.sb([128, NT], F32, "ost") for _ in range(3)]; bo = [Buf("ost%d" % i) for i in range(3)]
                oi = [0]
                jobs = []
                for (o0, wd) in segs:
                    for f0 in range(0, wd, 128):
                        M = min(128, wd - f0)
                        def evac(pap, bpp, o0=o0, f0=f0, M=M):
                            o_ = ost[oi[0] % 3]; b_ = bo[oi[0] % 3]; oi[0] += 1
                            P("act", lambda e: e.copy(out=o_[0:M, :], in_=pap), r=[bpp], w=[b_])
                            kb.dma("pool", PROJ[o0 + f0:o0 + f0 + M, tcol0 + blk * NT:tcol0 + (blk + 1) * NT], o_[0:M, :],
                                   r=[b_], w=[b_PROJ])
                        jobs.append(dict(XT=XT, bXT=bXT, KCn=KC, wcols=w_in[:, o0 + f0:o0 + f0 + M], nf=1, M=M, evacs=[evac]))
                g.run_jobs(jobs)

    win_pass(x_pre, 0, segs_rec)
    win_pass(x_own, TOK, segs_rec + segs_non)

    PL = c.PL
    with kb.phase():
        cw = kb.sb([128, 4, LB], F32, "lcw"); cb = kb.sb([128, LB], F32, "lcb")
        ab = kb.sb([128, LB], F32, "lab"); ib = kb.sb([128, LB], F32, "lib"); nsp = kb.sb([128, LB], F32, "nsp")
        b_lc = Buf("lruconst")
        for j in range(4):
            load_vec_fm(lru_cw[j, :], LB, cw[:, j, :], wbuf=b_lc)
        load_vec_fm(lru_cb, LB, cb, wbuf=b_lc)
        load_vec_fm(lru_ab, LB, ab, wbuf=b_lc)
        load_vec_fm(lru_ib, LB, ib, wbuf=b_lc)
        load_vec_fm(lru_lam, LB, nsp, wbuf=b_lc)
        P("act", lambda e: e.activation(out=nsp[:], in_=nsp[:], func=AF.Exp, scale=-1.0), r=[b_lc], w=[b_lc])
        P("act", lambda e: e.activation(out=nsp[:], in_=nsp[:], func=AF.Ln, bias=1.0), r=[b_lc], w=[b_lc])
        P("dve", lambda e: e.tensor_scalar(out=nsp[:], in0=nsp[:], scalar1=-8.0, scalar2=None, op0=ALU.mult),
          r=[b_lc], w=[b_lc])
        gw32 = kb.sb([128, 2, 128], F32, "gw32"); b_gw32 = Buf("gw32")
        gwb = [kb.sb([128, 2, 128], BF16, "gwb") for _ in range(2)]; b_gwb = [Buf("gwb%d" % i) for i in range(2)]
        NB = 3
        def tiles(n, shape, dt):
            return [kb.sb(shape, dt, n) for _ in range(NB)], [Buf(n + str(i)) for i in range(NB)]
        xin, bxin = tiles("xin", [128, PL + 3], F32)
        xa, bxa = tiles("xa", [128, PL], F32)
        xab, bxab = tiles("xab", [128, PL], BF16)
        rr, brr = tiles("rr", [128, PL], F32)
        ii, bii = tiles("ii", [128, PL], F32)
        aa, baa = tiles("aa", [128, PL], F32)
        mm, bmm = tiles("mm", [128, PL], F32)
        hh, bhh = tiles("hh", [128, PL], F32)
        gg, bgg = tiles("gg", [128, PL], F32)
        g2, bg2 = tiles("g2", [128, PL], F32)
        yo, byo = tiles("yo", [128, PL], BF16)
        state = kb.sb([128, 1], F32, "lstate"); b_state = Buf("lstate")
        gps = [kb.ps([128, 512], F32, "gps") for _ in range(4)]; bgps = [PB("gps%d" % i) for i in range(4)]
        gi = 0; it = 0
        for ct in range(LB):
            wb_ = gwb[ct % 2]; bwb_ = b_gwb[ct % 2]
            kb.dma("sp", gw32[:, 0, :], lru_aw[ct, :, :], w=[b_gw32])
            kb.dma("sp", gw32[:, 1, :], lru_iw[ct, :, :], w=[b_gw32])
            P("pool", lambda e: e.tensor_copy(out=wb_[:], in_=gw32[:]), r=[b_gw32], w=[bwb_])
            for pc in range(TT2 // PL):
                t0 = pc * PL
                own = t0 >= TOK
                k = it % NB; it += 1
                xi = xin[k]; bxi = bxin[k]
                kb.dma("sp", xi[:, 3:PL + 3], PROJ[c.o_lx + ct * 128:c.o_lx + (ct + 1) * 128, t0:t0 + PL], r=[b_PROJ], w=[bxi])
                if t0 == 0:
                    P("pool", lambda e: e.memset(xi[:, 0:3], 0.0), w=[bxi])
                    P("pool", lambda e: e.memset(state[:], 0.0), w=[b_state])
                else:
                    xp = xin[(k - 1) % NB]; bxp = bxin[(k - 1) % NB]
                    if t0 == TOK:
                        P("dve", lambda e: e.tensor_scalar(out=xi[:, 0:3], in0=xp[:, PL:PL + 3], scalar1=flg[:, 0:1],
                                                           scalar2=None, op0=ALU.mult), r=[bxp, b_const], w=[bxi])
                        P("dve", lambda e: e.tensor_scalar(out=state[:], in0=state[:], scalar1=flg[:, 0:1],
                                                           scalar2=None, op0=ALU.mult), r=[b_state, b_const], w=[b_state])
                    else:
                        P("dve", lambda e: e.tensor_copy(out=xi[:, 0:3], in_=xp[:, PL:PL + 3]), r=[bxp], w=[bxi])
                xa_ = xa[k]; bxa_ = bxa[k]
                P("dve", lambda e: e.tensor_scalar(out=xa_[:], in0=xi[:, 0:PL], scalar1=cw[:, 0, ct:ct + 1],
                                                   scalar2=cb[:, ct:ct + 1], op0=ALU.mult, op1=ALU.add),
                  r=[bxi, b_lc], w=[bxa_])
                for j in range(1, 4):
                    P("dve", lambda e, j=j: e.scalar_tensor_tensor(out=xa_[:], in0=xi[:, j:j + PL], scalar=cw[:, j, ct:ct + 1],
                                                                   in1=xa_[:], op0=ALU.mult, op1=ALU.add),
                      r=[bxi, b_lc, bxa_], w=[bxa_])
                xb = xab[k]; bxb = bxab[k]
                P("pool", lambda e: e.tensor_copy(out=xb[:], in_=xa_[:]), r=[bxa_], w=[bxb])
                r_ = rr[k]; br_ = brr[k]; i_ = ii[k]; bi_ = bii[k]
                for sub in range(0, PL, 512):
                    sn = min(512, PL - sub)
                    for gsel, (dst, bdst, bias) in enumerate([(r_, br_, ab), (i_, bi_, ib)]):
                        pp = gps[gi % 4]; bpp = bgps[gi % 4]; gi += 1
                        P("pe", lambda e, pp=pp, gsel=gsel: e.matmul(pp[:, 0:sn], wb_[:, gsel, :], xb[:, sub:sub + sn],
                                                                     start=True, stop=True), r=[bwb_, bxb], w=[bpp])
                        P("act", lambda e, pp=pp, dst=dst, bias=bias: e.activation(
                            out=dst[:, sub:sub + sn], in_=pp[:, 0:sn], func=AF.Sigmoid, bias=bias[:, ct:ct + 1]),
                          r=[bpp, b_lc], w=[bdst])
                a_ = aa[k]; ba_ = baa[k]; m_ = mm[k]; bm_ = bmm[k]; h_ = hh[k]; bh_ = bhh[k]
                P("act", lambda e: e.activation(out=a_[:], in_=r_[:], func=AF.Exp, scale=nsp[:, ct:ct + 1]),
                  r=[br_, b_lc], w=[ba_])
                P("pool", lambda e: e.tensor_tensor(out=m_[:], in0=a_[:], in1=a_[:], op=ALU.mult), r=[ba_], w=[bm_])
                P("act", lambda e: e.activation(out=m_[:], in_=m_[:], func=AF.Sqrt, scale=-1.0, bias=1.0), r=[bm_], w=[bm_])
                P("pool", lambda e: e.tensor_tensor(out=i_[:], in0=i_[:], in1=xa_[:], op=ALU.mult), r=[bi_, bxa_], w=[bi_])
                P("pool", lambda e: e.tensor_tensor(out=m_[:], in0=m_[:], in1=i_[:], op=ALU.mult), r=[bm_, bi_], w=[bm_])
                P("dve", lambda e: e.tensor_tensor_scan(out=h_[:], data0=a_[:], data1=m_[:], initial=state[:, 0:1],
                                                        op0=ALU.mult, op1=ALU.add), r=[ba_, bm_, b_state], w=[bh_])
                P("dve", lambda e: e.tensor_copy(out=state[:], in_=h_[:, PL - 1:PL]), r=[bh_], w=[b_state])
                if own:
                    g_ = gg[k]; bg_ = bgg[k]; q_ = g2[k]; bq_ = bg2[k]; y_ = yo[k]; by_ = byo[k]
                    kb.dma("sp", g_[:], PROJ[c.o_lg + ct * 128:c.o_lg + (ct + 1) * 128, t0:t0 + PL], r=[b_PROJ], w=[bg_])
                    P("pool", lambda e: e.tensor_tensor(out=q_[:], in0=g_[:], in1=g_[:], op=ALU.mult), r=[bg_], w=[bq_])
                    P("pool", lambda e: e.tensor_scalar(out=q_[:], in0=q_[:], scalar1=0.044715, scalar2=1.0,
                                                        op0=ALU.mult, op1=ALU.add), r=[bq_], w=[bq_])
                    P("pool", lambda e: e.tensor_tensor(out=q_[:], in0=q_[:], in1=g_[:], op=ALU.mult), r=[bq_, bg_], w=[bq_])
                    P("act", lambda e: e.activation(out=q_[:], in_=q_[:], func=AF.Sigmoid, scale=2.0 * 0.7978845608028654),
                      r=[bq_], w=[bq_])
                    P("dve", lambda e: e.tensor_tensor(out=q_[:], in0=q_[:], in1=g_[:], op=ALU.mult), r=[bq_, bg_], w=[bq_])
                    P("dve", lambda e: e.tensor_tensor(out=y_[:], in0=q_[:], in1=h_[:], op=ALU.mult), r=[bq_, bh_], w=[by_])
                    kb.dma("pool", YA[ct * 128:(ct + 1) * 128, t0 - TOK:t0 - TOK + PL], y_[:], r=[by_], w=[b_YA])

    gdn_phase(kb, c, locals())

    tail_phases(kb, c, locals())
    kb.barrier()
    kb.root.close()
    return nc


def gdn_phase(kb, c, L):
    P = L["P"]; load_vec_fm = L["load_vec_fm"]
    ident = L["ident"]; identb = L["identb"]; ones = L["ones"]; UPI = L["UPI"]; LOS = L["LOS"]; BLK = L["BLK"]
    CHA = L["CHA"]; CHB = L["CHB"]; flg = L["flg"]; b_const = L["b_const"]
    PROJ = L["PROJ"]; b_PROJ = L["b_PROJ"]; YB = L["YB"]; b_YB = L["b_YB"]
    gdn_cw = L["gdn_cw"]; gdn_alog = L["gdn_alog"]; gdn_dtb = L["gdn_dtb"]; gdn_onw = L["gdn_onw"]
    H, TOK, SBK, KEY = c.H, c.TOK, c.SBK, c.KEY
    TT2 = 2 * TOK
    nt = SBK // 128
    HG = min(4, H)
    NG = H // HG
    with kb.phase():
        bc = Buf("gconst")
        gcw = kb.sb([128, 4, 3 * H], F32, "gcw")
        for j in range(4):
            load_vec_fm(gdn_cw[j, :], 3 * H, gcw[:, j, :], wbuf=bc)
        onw = kb.sb([128, 1], F32, "onw")
        load_vec_fm(gdn_onw, 1, onw, wbuf=bc)
        dtb = kb.sb([128, 1], F32, "dtb"); negA = kb.sb([128, 1], F32, "negA")
        kb.dma("sp", dtb[0:H, :], gdn_dtb[:, :], w=[bc])
        kb.dma("sp", negA[0:H, :], gdn_alog[:, :], w=[bc])
        P("act", lambda e: e.activation(out=negA[0:H, :], in_=negA[0:H, :], func=AF.Exp), r=[bc], w=[bc])
        P("dve", lambda e: e.tensor_scalar(out=negA[0:H, :], in0=negA[0:H, :], scalar1=-1.0, scalar2=None, op0=ALU.mult),
          r=[bc], w=[bc])
        MASK2 = kb.sb([128, 2, 128], F32, "MASK2")
        P("pool", lambda e: e.tensor_copy(out=MASK2[:, 0, :], in_=LOS[:]), r=[b_const], w=[bc])
        P("pool", lambda e: e.tensor_copy(out=MASK2[:, 1, :], in_=UPI[:]), r=[b_const], w=[bc])
        Sm = kb.sb([128, H, 128], F32, "Sm"); Sb = kb.sb([128, H, 128], BF16, "Sb")
        bSm = [Buf("Sm%d" % h) for h in range(H)]; bSb = [Buf("Sb%d" % h) for h in range(H)]
        qnT = kb.sb([128, H, SBK], BF16, "qnT"); knT = kb.sb([128, H, SBK], BF16, "knT")
        bq = [Buf("qnT%d" % h) for h in range(H)]; bk = [Buf("knT%d" % h) for h in range(H)]
        kbg = kb.sb([128, H, nt, 128], BF16, "kbg"); kdec = kb.sb([128, H, nt, 128], BF16, "kdec")
        vb = kb.sb([128, H, nt, 128], BF16, "vb")
        bkt = [Buf("ktok%d" % h) for h in range(H)]
        zs = kb.sb([128, H, SBK], F32, "zs"); bz = [Buf("zs%d" % h) for h in range(H)]
        ybT = kb.sb([128, H, SBK], BF16, "ybT"); byb = [Buf("ybT%d" % h) for h in range(H)]
        uu = kb.sb([128, H, nt, 128], F32, "uu"); nwT = kb.sb([128, H, nt, 128], BF16, "nwT")
        atT = kb.sb([128, H, nt, 128], BF16, "atT")
        bu = [[Buf("u") for _ in range(nt)] for _ in range(H)]
        bnw = [[Buf("nw") for _ in range(nt)] for _ in range(H)]
        bat = [[Buf("at") for _ in range(nt)] for _ in range(H)]
        sc = kb.sb([128, nt, 8, H], F32, "sc"); bsc = [Buf("sc%d" % j) for j in range(nt)]
        afm = kb.sb([128, 2, SBK], F32, "afm"); bafm = Buf("afm")
        pA = [kb.ps([128, 512], F32, "pA") for _ in range(1)]; bpA = [PB("pA%d" % i) for i in range(1)]
        pT = [kb.ps([128, 4, 128], BF16, "pT") for _ in range(1)]; bpT = [PB("pT%d" % i) for i in range(1)]
        itp = [0]
        def nextT():
            return pT[0], bpT[0]
        pB = [kb.ps([128, 4, 2, 128], F32, "pB") for _ in range(2)]; bpB = [PB("pB%d" % i) for i in range(2)]
        pC = [kb.ps([128, 4, 128], F32, "pC") for _ in range(2)]; bpC = [PB("pC%d" % i) for i in range(2)]
        ia = [0]; ib = [0]; ic = [0]
        def nextA():
            ia[0] += 1; return pA[0], bpA[0]
        def nextB():
            ib[0] += 1; return pB[ib[0] % 2], bpB[ib[0] % 2]
        def nextC():
            ic[0] += 1; return pC[ic[0] % 2], bpC[ic[0] % 2]
        it = [0]

        for sbi in range(TT2 // SBK):
            t0 = sbi * SBK
            own = t0 >= TOK
            kb.dma("sp", afm[0:H, 0, :], PROJ[c.o_a:c.o_a + H, t0:t0 + SBK], r=[b_PROJ], w=[bafm])
            kb.dma("sp", afm[0:H, 1, :], PROJ[c.o_b:c.o_b + H, t0:t0 + SBK], r=[b_PROJ], w=[bafm])
            P("act", lambda e: e.activation(out=afm[0:H, 0, :], in_=afm[0:H, 0, :], func=AF.Exp, bias=dtb[0:H, 0:1]),
              r=[bafm, bc], w=[bafm])
            P("act", lambda e: e.activation(out=afm[0:H, 0, :], in_=afm[0:H, 0, :], func=AF.Ln, bias=1.0), r=[bafm], w=[bafm])
            P("dve", lambda e: e.tensor_scalar(out=afm[0:H, 0, :], in0=afm[0:H, 0, :], scalar1=negA[0:H, 0:1], scalar2=None,
                                               op0=ALU.mult), r=[bafm, bc], w=[bafm])
            P("act", lambda e: e.activation(out=afm[0:H, 1, :], in_=afm[0:H, 1, :], func=AF.Sigmoid), r=[bafm], w=[bafm])
            for j in range(nt):
                pt, bpt = nextA()
                for q in range(2):
                    P("pe", lambda e: e.transpose(pt[:, q * H:(q + 1) * H], afm[0:H, q, j * 128:(j + 1) * 128], ident[0:H, 0:H]),
                      r=[bafm, b_const], w=[bpt])
                P("dve", lambda e: e.tensor_copy(out=sc[:, j, 0:2, :], in_=pt[:, 0:2 * H].rearrange("p (a h) -> p a h", a=2)),
                  r=[bpt], w=[bsc[j]])
                pt2, bpt2 = nextA()
                for q, mt in enumerate([UPI, BLK, CHA, CHB]):
                    P("pe", lambda e: e.matmul(pt2[:, q * H:(q + 1) * H], mt[:], sc[:, j, 0, :], start=True, stop=True),
                      r=[bsc[j], b_const], w=[bpt2])
                P("act", lambda e: e.activation(out=sc[:, j, 2, :], in_=pt2[:, 0:H], func=AF.Exp), r=[bpt2], w=[bsc[j]])
                P("dve", lambda e: e.tensor_copy(out=sc[:, j, 4, :], in_=pt2[:, 0:H]), r=[bpt2], w=[bsc[j]])
                P("dve", lambda e: e.tensor_tensor(out=sc[:, j, 3, :], in0=pt2[:, H:2 * H], in1=sc[:, j, 4, :], op=ALU.subtract),
                  r=[bpt2, bsc[j]], w=[bsc[j]])
                P("act", lambda e: e.activation(out=sc[:, j, 3, :], in_=sc[:, j, 3, :], func=AF.Exp), r=[bsc[j]], w=[bsc[j]])
                P("act", lambda e: e.activation(out=sc[:, j, 6:8, :], in_=pt2[:, 2 * H:4 * H].rearrange("p (a h) -> p a h", a=2),
                                                func=AF.Exp), r=[bpt2], w=[bsc[j]])
                P("dve", lambda e: e.tensor_tensor(out=sc[:, j, 4, :], in0=sc[:, j, 1, :], in1=sc[:, j, 2, :], op=ALU.mult),
                  r=[bsc[j]], w=[bsc[j]])
                P("dve", lambda e: e.tensor_scalar(out=sc[:, j, 5, :], in0=sc[:, j, 1, :], scalar1=-1.0, scalar2=None,
                                                   op0=ALU.mult), r=[bsc[j]], w=[bsc[j]])
            with kb.scope():
                xin_g = [kb.sb([128, 3, HG, SBK + 3], F32, "xin_g") for _ in range(2)]; bxg = [Buf("xin_g%d" % i) for i in range(2)]
                cv_g = [kb.sb([128, 3, HG, SBK], F32, "cv_g") for _ in range(2)]; bcg = [Buf("cv_g%d" % i) for i in range(2)]
                sq_g = kb.sb([128, 2, HG, SBK], F32, "sq_g"); bsg = Buf("sq_g")
                rn_g = kb.sb([128, 2, HG, SBK], F32, "rn_g"); brg = Buf("rn_g")
                for g in range(NG):
                    h0 = g * HG
                    xg = xin_g[g % 2]; bx = bxg[g % 2]; cg = cv_g[g % 2]; bcv_ = bcg[g % 2]
                    segs = (0, 1, 2) if own else (1, 2)
                    s_lo = segs[0]
                    for seg in segs:
                        row0 = c.o_q + seg * KEY + h0 * 128
                        if t0 == 0:
                            kb.dma("sp", xg[:, seg, :, 3:SBK + 3],
                                   PROJ[row0:row0 + HG * 128, t0:t0 + SBK].rearrange("(hh p) t -> p hh t", p=128), r=[b_PROJ], w=[bx])
                            P("pool", lambda e: e.memset(xg[:, seg, :, 0:3], 0.0), w=[bx])
                        else:
                            kb.dma("sp", xg[:, seg, :, 0:SBK + 3],
                                   PROJ[row0:row0 + HG * 128, t0 - 3:t0 + SBK].rearrange("(hh p) t -> p hh t", p=128), r=[b_PROJ], w=[bx])
                            if t0 == TOK:
                                P("dve", lambda e: e.tensor_scalar(out=xg[:, seg, :, 0:3], in0=xg[:, seg, :, 0:3], scalar1=flg[:, 0:1],
                                                                   scalar2=None, op0=ALU.mult), r=[bx, b_const], w=[bx])
                    for j in range(4):
                        for seg in segs:
                            for hh in range(HG):
                                idx = seg * H + h0 + hh
                                if j == 0:
                                    P("dve", lambda e: e.tensor_scalar(out=cg[:, seg, hh, :], in0=xg[:, seg, hh, 0:SBK], scalar1=gcw[:, 0, idx:idx + 1],
                                                                       scalar2=None, op0=ALU.mult), r=[bx, bc], w=[bcv_])
                                else:
                                    P("dve", lambda e: e.scalar_tensor_tensor(out=cg[:, seg, hh, :], in0=xg[:, seg, hh, j:j + SBK],
                                                                              scalar=gcw[:, j, idx:idx + 1], in1=cg[:, seg, hh, :],
                                                                              op0=ALU.mult, op1=ALU.add), r=[bx, bc, bcv_], w=[bcv_])
                    P("act", lambda e: e.activation(out=cg[:, s_lo:3, :, :], in_=cg[:, s_lo:3, :, :], func=AF.Silu), r=[bcv_], w=[bcv_])
                    if own:
                        zr = c.o_z + h0 * 128
                        kb.dma("sp", zs[:, h0:h0 + HG, :], PROJ[zr:zr + HG * 128, t0:t0 + SBK].rearrange("(hh p) t -> p hh t", p=128),
                               r=[b_PROJ], w=[bz[h] for h in range(h0, h0 + HG)])
                        P("act", lambda e: e.activation(out=zs[:, h0:h0 + HG, :], in_=zs[:, h0:h0 + HG, :], func=AF.Silu),
                          r=[bz[h] for h in range(h0, h0 + HG)], w=[bz[h] for h in range(h0, h0 + HG)])
                    nqk = 2 - s_lo
                    P("pool", lambda e: e.tensor_tensor(out=sq_g[:, 0:nqk, :, :], in0=cg[:, s_lo:2, :, :], in1=cg[:, s_lo:2, :, :], op=ALU.mult),
                      r=[bcv_], w=[bsg])
                    sqf = sq_g[:].rearrange("p a b c -> p (a b c)")
                    rnf = rn_g[:].rearrange("p a b c -> p (a b c)")
                    ncol = nqk * HG * SBK
                    for c0 in range(0, ncol, 1024):
                        cn = min(1024, ncol - c0)
                        pb, bpb = nextB()
                        pbf = pb[:].rearrange("p a b c -> p (a b c)")
                        for s0 in range(0, cn, 512):
                            sn = min(512, cn - s0)
                            P("pe", lambda e: e.matmul(pbf[:, s0:s0 + sn], ones[:], sqf[:, c0 + s0:c0 + s0 + sn], start=True, stop=True),
                              r=[bsg, b_const], w=[bpb])
                        P("act", lambda e: e.activation(out=rnf[:, c0:c0 + cn], in_=pbf[:, 0:cn], func=AF.Ln, bias=EPS), r=[bpb], w=[brg])
                    P("act", lambda e: e.activation(out=rn_g[:, 0:nqk, :, :], in_=rn_g[:, 0:nqk, :, :], func=AF.Exp, scale=-0.5), r=[brg], w=[brg])
                    if own:
                        P("dve", lambda e: e.scalar_tensor_tensor(out=qnT[:, h0:h0 + HG, :], in0=cg[:, 0, :, :], scalar=float(128.0 ** -0.5),
                                                                  in1=rn_g[:, 0, :, :], op0=ALU.mult, op1=ALU.mult),
                          r=[bcv_, brg], w=[bq[h] for h in range(h0, h0 + HG)])
                    P("dve", lambda e: e.tensor_tensor(out=cg[:, 1, :, :], in0=cg[:, 1, :, :], in1=rn_g[:, nqk - 1, :, :], op=ALU.mult),
                      r=[bcv_, brg], w=[bcv_])
                    P("pool", lambda e: e.tensor_copy(out=knT[:, h0:h0 + HG, :], in_=cg[:, 1, :, :]), r=[bcv_],
                      w=[bk[h] for h in range(h0, h0 + HG)])
                    for j in range(nt):
                        for seg in (1, 2):
                            pt, bpt = nextC()
                            for hh in range(HG):
                                P("pe", lambda e: e.transpose(pt[:, hh, :], cg[:, seg, hh, j * 128:(j + 1) * 128], ident[:]),
                                  r=[bcv_, b_const], w=[bpt])
                            def bcs(q):
                                return sc[:, j, q, h0:h0 + HG].unsqueeze(2).to_broadcast([128, HG, 128])
                            wk = [bkt[h] for h in range(h0, h0 + HG)]
                            if seg == 1:
                                P("dve", lambda e: e.tensor_tensor(out=kbg[:, h0:h0 + HG, j, :], in0=pt[:, 0:HG, :], in1=bcs(4), op=ALU.mult),
                                  r=[bpt, bsc[j]], w=wk)
                                P("dve", lambda e: e.tensor_tensor(out=kdec[:, h0:h0 + HG, j, :], in0=pt[:, 0:HG, :], in1=bcs(3), op=ALU.mult),
                                  r=[bpt, bsc[j]], w=wk)
                            else:
                                P("dve", lambda e: e.tensor_tensor(out=vb[:, h0:h0 + HG, j, :], in0=pt[:, 0:HG, :], in1=bcs(1), op=ALU.mult),
                                  r=[bpt, bsc[j]], w=wk)
            with kb.scope():
                lhsD = kb.sb([128, H, 128], F32, "lhsD"); blD = [Buf("lhsD") for _ in range(NG)]
                E2 = kb.sb([128, H, 2, 128], F32, "E2"); bE2 = [Buf("E2") for _ in range(NG)]
                Nc = [kb.sb([128, H, 2, 128], BF16, "Nc") for _ in range(2)]
                bNc = [[Buf("Nc") for _ in range(NG)] for _ in range(2)]
                Pm = [kb.sb([128, H, 128], BF16, "Pm") for _ in range(2)]; bPm = [[Buf("Pm") for _ in range(NG)] for _ in range(2)]
                vnew = kb.sb([128, H, 128], BF16, "vnew"); bvn = [Buf("vnew") for _ in range(NG)]
                o2s = kb.sb([128, H, 128], F32, "o2s"); bo2 = [Buf("o2s") for _ in range(NG)]
                oo = kb.sb([128, H, 128], F32, "oo"); boo = Buf("oo")
                osq = kb.sb([128, H, 128], F32, "osq"); bosq = Buf("osq")
                oss = kb.sb([128, 4, H], F32, "oss"); boss = Buf("oss")
                for j in range(nt):
                    def hs(g):
                        return range(g * HG, (g + 1) * HG)
                    cs = slice(j * 128, (j + 1) * 128)
                    for g in range(NG):
                        h0 = g * HG
                        for h in hs(g):
                            P("pool", lambda e: e.tensor_scalar(out=lhsD[:, h, :], in0=UPI[:], scalar1=sc[:, j, 0, h:h + 1], scalar2=None,
                                                                op0=ALU.mult), r=[b_const, bsc[j]], w=[blD[g]])
                        pb, bpb = nextB()
                        for h in hs(g):
                            P("pe", lambda e: e.matmul(pb[:, h - h0, 0, :], lhsD[:, h, :], LOS[:], start=True, stop=True),
                              r=[blD[g], b_const], w=[bpb])
                            P("pe", lambda e: e.matmul(pb[:, h - h0, 1, :], LOS[:], lhsD[:, h, :], start=True, stop=True),
                              r=[blD[g], b_const], w=[bpb])
                        P("act", lambda e: e.activation(out=E2[:, h0:h0 + HG, :, :], in_=pb[:, 0:HG, :, :], func=AF.Exp), r=[bpb], w=[bE2[g]])
                        for h in hs(g):
                            P("pool", lambda e: e.tensor_tensor(out=E2[:, h, :, :], in0=E2[:, h, :, :], in1=MASK2[:], op=ALU.mult),
                              r=[bE2[g], bc], w=[bE2[g]])
                        if G2S < 1:
                            continue
                        pb, bpb = nextB()
                        for h in hs(g):
                            P("pe", lambda e: e.matmul(pb[:, h - h0, 0, :], knT[:, h, cs], knT[:, h, cs], start=True, stop=True),
                              r=[bk[h]], w=[bpb])
                            if own:
                                P("pe", lambda e: e.matmul(pb[:, h - h0, 1, :], knT[:, h, cs], qnT[:, h, cs], start=True, stop=True),
                                  r=[bk[h], bq[h]], w=[bpb])
                        for h in hs(g):
                            P("dve", lambda e: e.scalar_tensor_tensor(out=Nc[0][:, h, 0, :], in0=pb[:, h - h0, 0, :], scalar=sc[:, j, 5, h:h + 1],
                                                                      in1=E2[:, h, 0, :], op0=ALU.mult, op1=ALU.mult),
                              r=[bpb, bsc[j], bE2[g]], w=[bNc[0][g]])
                        if own:
                            P("dve", lambda e: e.tensor_tensor(out=atT[:, h0:h0 + HG, j, :], in0=pb[:, 0:HG, 1, :], in1=E2[:, h0:h0 + HG, 1, :],
                                                               op=ALU.mult), r=[bpb, bE2[g]], w=[bat[h][j] for h in hs(g)])
                    if G2S < 2:
                        continue
                    for g in range(NG):
                        h0 = g * HG
                        pc_ = nextT()
                        for h in hs(g):
                            P("pe", lambda e: e.transpose(pc_[0][:, h - h0, :], Nc[0][:, h, 0, :], identb[:]), r=[bNc[0][g], b_const], w=[pc_[1]])
                        P("act", lambda e: e.copy(out=Nc[0][:, h0:h0 + HG, 1, :], in_=pc_[0][:, 0:HG, :]), r=[pc_[1]], w=[bNc[0][g]])
                        for h in hs(g):
                            P("pool", lambda e: e.tensor_tensor(out=Pm[0][:, h, :], in0=Nc[0][:, h, 1, :], in1=identb[:], op=ALU.add),
                              r=[bNc[0][g], b_const], w=[bPm[0][g]])
                    cur = 0
                    if G2S < 3:
                        continue
                    for lvl in range(1, 6):
                        nx = 1 - cur
                        for g in range(NG):
                            h0 = g * HG
                            pb, bpb = nextB()
                            for h in hs(g):
                                P("pe", lambda e: e.matmul(pb[:, h - h0, 0, :], Nc[cur][:, h, 1, :], Nc[cur][:, h, 0, :], start=True, stop=True),
                                  r=[bNc[cur][g]], w=[bpb])
                                if lvl < 5:
                                    P("pe", lambda e: e.matmul(pb[:, h - h0, 1, :], Nc[cur][:, h, 0, :], Nc[cur][:, h, 1, :], start=True, stop=True),
                                      r=[bNc[cur][g]], w=[bpb])
                            if lvl < 5:
                                P("act", lambda e: e.copy(out=Nc[nx][:, h0:h0 + HG, :, :], in_=pb[:, 0:HG, :, :]), r=[bpb], w=[bNc[nx][g]])
                            else:
                                P("act", lambda e: e.copy(out=Nc[nx][:, h0:h0 + HG, 0, :], in_=pb[:, 0:HG, 0, :]), r=[bpb], w=[bNc[nx][g]])
                            pc2, bpc2 = nextC()
                            for h in hs(g):
                                P("pe", lambda e: e.matmul(pc2[:, h - h0, :], Nc[nx][:, h, 0, :], Pm[cur][:, h, :], start=True, stop=True),
                                  r=[bNc[nx][g], bPm[cur][g]], w=[bpc2])
                            P("dve", lambda e: e.tensor_tensor(out=Pm[nx][:, h0:h0 + HG, :], in0=pc2[:, 0:HG, :], in1=Pm[cur][:, h0:h0 + HG, :],
                                                               op=ALU.add), r=[bpc2, bPm[cur][g]], w=[bPm[nx][g]])
                        cur = nx
                    if G2S < 4:
                        continue
                    for g in range(NG):
                        h0 = g * HG
                        pb, bpb = nextB()
                        for h in hs(g):
                            P("pe", lambda e: e.matmul(pb[:, h - h0, 0, :], Pm[cur][:, h, :], vb[:, h, j, :], start=True, stop=True),
                              r=[bPm[cur][g], bkt[h]], w=[bpb])
                            if G2S >= 5:
                                P("pe", lambda e: e.matmul(pb[:, h - h0, 1, :], kbg[:, h, j, :], Pm[cur][:, h, :], start=True, stop=True),
                                  r=[bPm[cur][g], bkt[h]], w=[bpb])
                        if G2S >= 6:
                            P("act", lambda e: e.copy(out=uu[:, h0:h0 + HG, j, :], in_=pb[:, 0:HG, 0, :]), r=[bpb], w=[bu[h][j] for h in hs(g)])
                        if G2S >= 7:
                            P("dve", lambda e: e.tensor_scalar(out=nwT[:, h0:h0 + HG, j, :], in0=pb[:, 0:HG, 1, :], scalar1=-1.0, scalar2=None,
                                                           op0=ALU.mult), r=[bpb], w=[bnw[h][j] for h in hs(g)])
                if t0 == 0:
                    P("pool", lambda e: e.memset(Sm[:], 0.0), w=bSm)
                    P("pool", lambda e: e.memset(Sb[:], 0.0), w=bSb)
                elif t0 == TOK:
                    P("dve", lambda e: e.tensor_scalar(out=Sm[:], in0=Sm[:], scalar1=flg[:, 0:1], scalar2=None, op0=ALU.mult),
                      r=bSm + [b_const], w=bSm)
                    P("act", lambda e: e.copy(out=Sb[:], in_=Sm[:]), r=bSm, w=bSb)
                for j in range(nt):
                    cs0 = j * 128
                    po1 = [None] * NG
                    for half in range(2):
                        rs = slice(half * 64, half * 64 + 64)
                        cols = slice(cs0 + half * 64, cs0 + half * 64 + 64)
                        for g in range(NG):
                            h0 = g * HG
                            pw, bpw = nextC()
                            for h in hs(g):
                                P("pe", lambda e: e.matmul(pw[rs, h - h0, :], nwT[:, h, j, rs], Sb[:, h, :], start=True, stop=True),
                                  r=[bnw[h][j], bSb[h]], w=[bpw])
                            P("dve", lambda e: e.tensor_tensor(out=vnew[rs, h0:h0 + HG, :], in0=pw[rs, 0:HG, :], in1=uu[rs, h0:h0 + HG, j, :],
                                                               op=ALU.add), r=[bpw] + [bu[h][j] for h in hs(g)], w=[bvn[g]])
                            if own:
                                pq, bpq = nextC()
                                for h in hs(g):
                                    P("pe", lambda e: e.matmul(pq[rs, h - h0, :], qnT[:, h, cols], Sb[:, h, :], start=True, stop=True),
                                      r=[bq[h], bSb[h]], w=[bpq])
                                for h in hs(g):
                                    P("dve", lambda e: e.tensor_scalar(out=o2s[rs, h, :], in0=pq[rs, h - h0, :], scalar1=sc[rs, j, 2, h:h + 1],
                                                                       scalar2=None, op0=ALU.mult), r=[bpq, bsc[j]], w=[bo2[g]])
                            psd, bpsd = nextC()
                            for h in hs(g):
                                P("pe", lambda e: e.matmul(psd[:, h - h0, :], kdec[rs, h, j, :], vnew[rs, h, :], start=True, stop=True),
                                  r=[bkt[h], bvn[g]], w=[bpsd])
                            for h in hs(g):
                                P("dve", lambda e: e.scalar_tensor_tensor(out=Sm[:, h, :], in0=Sm[:, h, :], scalar=sc[:, j, 6 + half, h:h + 1],
                                                                          in1=psd[:, h - h0, :], op0=ALU.mult, op1=ALU.add),
                                  r=[bSm[h], bsc[j], bpsd], w=[bSm[h]])
                            P("act", lambda e: e.copy(out=Sb[:, h0:h0 + HG, :], in_=Sm[:, h0:h0 + HG, :]), r=[bSm[h] for h in hs(g)],
                              w=[bSb[h] for h in hs(g)])
                    if own:
                        for g in range(NG):
                            h0 = g * HG
                            po2, bpo2 = nextC()
                            for h in hs(g):
                                P("pe", lambda e: e.matmul(po2[:, h - h0, :], atT[:, h, j, :], vnew[:, h, :], start=True, stop=True),
                                  r=[bat[h][j], bvn[g]], w=[bpo2])
                            P("dve", lambda e: e.tensor_tensor(out=oo[:, h0:h0 + HG, :], in0=po2[:, 0:HG, :], in1=o2s[:, h0:h0 + HG, :], op=ALU.add),
                              r=[bpo2, bo2[g]], w=[boo])
                        P("pool", lambda e: e.tensor_tensor(out=osq[:], in0=oo[:], in1=oo[:], op=ALU.mult), r=[boo], w=[bosq])
                        P("dve", lambda e: e.tensor_reduce(out=oss[:, 0, :], in_=osq[:], axis=mybir.AxisListType.X, op=ALU.add),
                          r=[bosq], w=[boss])
                        P("dve", lambda e: e.tensor_scalar(out=oss[:, 1, :], in0=oss[:, 0, :], scalar1=1.0 / 128.0, scalar2=EPS,
                                                           op0=ALU.mult, op1=ALU.add), r=[boss], w=[boss])
                        P("act", lambda e: e.activation(out=oss[:, 2, :], in_=oss[:, 1, :], func=AF.Sqrt), r=[boss], w=[boss])
                        P("dve", lambda e: e.reciprocal(out=oss[:, 3, :], in_=oss[:, 2, :]), r=[boss], w=[boss])
                        for h in range(H):
                            P("dve", lambda e: e.tensor_scalar(out=oo[:, h, :], in0=oo[:, h, :], scalar1=oss[:, 3, h:h + 1], scalar2=None,
                                                               op0=ALU.mult), r=[boo, boss], w=[boo])
                        for g in range(NG):
                            h0 = g * HG
                            pt, bpt = nextA()
                            for h in hs(g):
                                P("pe", lambda e: e.transpose(pt[:, (h - h0) * 128:(h - h0 + 1) * 128], oo[:, h, :], ident[:]),
                                  r=[boo, b_const], w=[bpt])
                            for h in hs(g):
                                P("dve", lambda e: e.scalar_tensor_tensor(out=ybT[:, h, cs0:cs0 + 128], in0=pt[:, (h - h0) * 128:(h - h0 + 1) * 128],
                                                                          scalar=onw[:, 0:1], in1=zs[:, h, cs0:cs0 + 128],
                                                                          op0=ALU.mult, op1=ALU.mult), r=[bpt, bc, bz[h]], w=[byb[h]])
            if own:
                for h in range(H):
                    kb.dma("pool", YB[h * 128:(h + 1) * 128, t0 - TOK:t0 - TOK + SBK], ybT[:, h, :], r=[byb[h]], w=[b_YB])


_pT = {}


def kb_ps_bf16(kb, name):
    key = id(kb.es)
    if key not in _pT:
        _pT.clear()
        _pT[key] = ([kb.ps([128, 4, 128], BF16, name) for _ in range(2)], [Buf(name + str(i)) for i in range(2)], [0])
    tl, bl, ctr = _pT[key]
    ctr[0] += 1
    return tl[ctr[0] % 2], bl[ctr[0] % 2]


def tail_phases(kb, c, L):
    P = L["P"]; Gemm = L["Gemm"]; prenorm_block = L["prenorm_block"]
    ident = L["ident"]; ones = L["ones"]; b_const = L["b_const"]; b_mod = L["b_mod"]
    PROJ = L["PROJ"]; b_PROJ = L["b_PROJ"]; YA = L["YA"]; b_YA = L["b_YA"]; YB = L["YB"]; b_YB = L["b_YB"]
    X1 = L["X1"]; b_X1 = L["b_X1"]; ACTT = L["ACTT"]; b_ACTT = L["b_ACTT"]
    w_bl = L["w_bl"]; w_bg = L["w_bg"]; w_out = L["w_out"]; w_up = L["w_up"]; w_dn = L["w_dn"]
    x_own = L["x_own"]; out = L["out"]; b_out = L["b_out"]
    g1w = L["g1w"]; g2w = L["g2w"]; w2s = L["w2s"]; sh2 = L["sh2"]
    D, KC, TOK, NT, LB, H = c.D, c.KC, c.TOK, c.NT, c.LB, c.H
    nt = NT // 128
    NT1 = min(c.NTW, TOK)
    YG = kb.dram("yg", [D, max(NT, NT1)], F32)
    b_YG = Buf("YG")

    def out_gemm_and_epilogue(g, XT, bXT, KCn, wmat, gw, xres, bxres, xres_row0, dst, bdst, dst_row0, NT, nfp=1):
        nt = NT // 128
        rstd = kb.sb([128, nt, 4], F32, "rstd"); brs = Buf("rstd")
        sss = kb.sb([128, NT], F32, "sss"); bsss = Buf("sss")
        es_in = ExitStack(); old_es = kb.es; kb.es = es_in
        g = g()
        ssb = kb.ps([128, max(512, NT)], F32, "ssb"); bssb = PB("ssb")
        ysq = [kb.sb([128, NT], F32, "ysq") for _ in range(2)]; bys = [Buf("ysq%d" % i) for i in range(2)]
        ygs = [kb.sb([128, NT], F32, "ygs") for _ in range(2)]; byg = [Buf("ygs%d" % i) for i in range(2)]
        def mk_evac(f):
            def evac(pap, bpp, f=f):
                q_ = ysq[f % 2]; bq_ = bys[f % 2]; y_ = ygs[f % 2]; by_ = byg[f % 2]
                P("act", lambda e: e.activation(out=q_[:], in_=pap, func=AF.Square), r=[bpp], w=[bq_])
                P("act", lambda e: e.activation(out=y_[:], in_=pap, func=AF.Copy, scale=gw[:, f:f + 1]), r=[bpp, b_mod], w=[by_])
                for s0 in range(0, NT, 512):
                    s1 = min(NT, s0 + 512)
                    P("pe", lambda e: e.matmul(ssb[:, s0:s1], ones[:], q_[:, s0:s1], start=(f == 0), stop=(f == KC - 1)),
                      r=[bq_, b_const], w=[bssb])
                kb.dma("pool", YG[f * 128:(f + 1) * 128, 0:NT], y_[:], r=[by_], w=[b_YG])
            return evac
        g.run_jobs([dict(XT=XT, bXT=bXT, KCn=KCn, wcols=wmat[:, f * 128:(f + nfp) * 128], nf=nfp, M=128,
                         evacs=[mk_evac(f + t) for t in range(nfp)]) for f in range(0, KC, nfp)])
        P("act", lambda e: e.copy(out=sss[:], in_=ssb[:, 0:NT]), r=[bssb], w=[bsss])
        kb.barrier(); es_in.close(); kb.es = old_es
        pss = kb.ps([128, 512], F32, "pss"); bpss = PB("pss")
        for i in range(nt):
            P("pe", lambda e: e.transpose(pss[:, 0:128], sss[:, i * 128:(i + 1) * 128], ident[:]), r=[bsss, b_const], w=[bpss])
            P("dve", lambda e: e.tensor_scalar(out=rstd[:, i, 0:1], in0=pss[:, 0:1], scalar1=1.0 / D, scalar2=EPS, op0=ALU.mult, op1=ALU.add),
              r=[bpss], w=[brs])
            P("act", lambda e: e.activation(out=rstd[:, i, 1:2], in_=rstd[:, i, 0:1], func=AF.Sqrt), r=[brs], w=[brs])
            P("dve", lambda e: e.reciprocal(out=rstd[:, i, 2:3], in_=rstd[:, i, 1:2]), r=[brs], w=[brs])
        xt = [kb.sb([128, D], F32, "ext") for _ in range(2)]; bxt = [Buf("ext%d" % i) for i in range(2)]
        ygt = [kb.sb([128, KC, 128], F32, "ygt") for _ in range(2)]; bygt = [Buf("ygt%d" % i) for i in range(2)]
        tp = [kb.ps([128, 512], F32, "etp") for _ in range(2)]; btp = [PB("etp%d" % i) for i in range(2)]
        YGv = YG.rearrange("(kc p) t -> p kc t", p=128)
        ti = 0
        for i in range(nt):
            x_ = xt[i % 2]; bx_ = bxt[i % 2]; yt = ygt[i % 2]; byt = bygt[i % 2]
            kb.dma("sp", x_[:], xres[xres_row0 + i * 128:xres_row0 + (i + 1) * 128, :], r=[bxres], w=[bx_])
            kb.dma("act" if DUALQ else "sp", yt[:], YGv[:, :, i * 128:(i + 1) * 128], r=[b_YG], w=[byt])
            for g0 in range(0, KC, 4):
                pt = tp[ti % 2]; bp = btp[ti % 2]; ti += 1
                for q in range(4):
                    P("pe", lambda e: e.transpose(pt[:, q * 128:(q + 1) * 128], yt[:, g0 + q, :], ident[:]), r=[byt, b_const], w=[bp])
                P("dve", lambda e: e.scalar_tensor_tensor(out=x_[:, g0 * 128:(g0 + 4) * 128], in0=pt[:], scalar=rstd[:, i, 2:3],
                                                          in1=x_[:, g0 * 128:(g0 + 4) * 128], op0=ALU.mult, op1=ALU.add),
                  r=[bp, brs, bx_], w=[bx_])
            kb.dma("pool", dst[dst_row0 + i * 128:dst_row0 + (i + 1) * 128, :], x_[:], r=[bx_], w=[bdst])

    b_xown = Buf("x_own")
    for blk in range(TOK // NT1):
        c0 = blk * NT1
        with kb.phase():
            MT = kb.sb([128, KC, NT1], BF16, "MT"); bMT = Buf("MT")
            with kb.phase():
                XA = kb.sb([128, LB, NT1], BF16, "XA"); bXA = Buf("XA")
                XB = kb.sb([128, H, NT1], BF16, "XB"); bXB = Buf("XB")
                kb.dma("sp", XA[:], YA.rearrange("(kc p) t -> p kc t", p=128)[:, :, c0:c0 + NT1], r=[b_YA], w=[bXA])
                kb.dma("sp", XB[:], YB.rearrange("(kc p) t -> p kc t", p=128)[:, :, c0:c0 + NT1], r=[b_YB], w=[bXB])
                g = Gemm(min(16, LB), NT1)
                gl = [kb.sb([128, NT1], F32, "gl") for _ in range(2)]; bgl = [Buf("gl%d" % i) for i in range(2)]
                gg_ = [kb.sb([128, NT1], F32, "gg") for _ in range(2)]; bgg_ = [Buf("gg%d" % i) for i in range(2)]
                m1 = [kb.sb([128, NT1], F32, "m1") for _ in range(2)]; bm1 = [Buf("m1%d" % i) for i in range(2)]
                jobs = []
                def mk(f):
                    a_ = gl[f % 2]; ba_ = bgl[f % 2]; b_ = gg_[f % 2]; bb_ = bgg_[f % 2]; m_ = m1[f % 2]; bm_ = bm1[f % 2]
                    def pre():
                        kb.dma("sp", a_[:], PROJ[c.o_gl + f * 128:c.o_gl + (f + 1) * 128, TOK + c0:TOK + c0 + NT1], r=[b_PROJ], w=[ba_])
                        kb.dma("sp", b_[:], PROJ[c.o_gg + f * 128:c.o_gg + (f + 1) * 128, TOK + c0:TOK + c0 + NT1], r=[b_PROJ], w=[bb_])
                        P("act", lambda e: e.activation(out=a_[:], in_=a_[:], func=AF.Sigmoid), r=[ba_], w=[ba_])
                        P("act", lambda e: e.activation(out=b_[:], in_=b_[:], func=AF.Sigmoid), r=[bb_], w=[bb_])
                    def evA(pap, bpp):
                        P("dve", lambda e: e.tensor_tensor(out=m_[:], in0=pap, in1=a_[:], op=ALU.mult), r=[bpp, ba_], w=[bm_])
                    def evB(pap, bpp):
                        P("dve", lambda e: e.tensor_tensor(out=b_[:], in0=pap, in1=b_[:], op=ALU.mult), r=[bpp, bb_], w=[bb_])
                        P("pool", lambda e: e.tensor_tensor(out=MT[:, f, :], in0=m_[:], in1=b_[:], op=ALU.add), r=[bm_, bb_], w=[bMT])
                    jobs.append(dict(XT=XA, bXT=bXA, KCn=LB, wcols=w_bl[:, f * 128:(f + 1) * 128], nf=1, M=128, evacs=[evA], pre=pre))
                    jobs.append(dict(XT=XB, bXT=bXB, KCn=H, wcols=w_bg[:, f * 128:(f + 1) * 128], nf=1, M=128, evacs=[evB]))
                for f in range(KC):
                    mk(f)
                g.run_jobs(jobs)
            with kb.phase():
                out_gemm_and_epilogue(lambda: Gemm(min(16, KC), NT1), MT, bMT, KC, w_out, g1w, x_own, b_xown, c0, X1, b_X1, c0, NT1)

    NTU = min(c.NTW, TOK)
    for blk in range(TOK // NTU):
        c0 = blk * NTU
        with kb.phase():
            XT = kb.sb([128, KC, NTU], BF16, "XT2"); bXT = Buf("XT2")
            prenorm_block(X1, b_X1, c0, NTU, XT, bXT, w2s, sh2, "pn2")
            g = Gemm(min(32, KC), NTU, nacc=4)
            rl = [kb.sb([128, NTU], F32, "rl") for _ in range(2)]; brl = [Buf("rl%d" % i) for i in range(2)]
            ao = [kb.sb([128, NTU], BF16, "ao") for _ in range(3)]; bao = [Buf("ao%d" % i) for i in range(3)]
            jobs = []
            evs = []
            for f in range(c.DFF // 128):
                def evac(pap, bpp, f=f):
                    r_ = rl[f % 2]; br_ = brl[f % 2]; a_ = ao[f % 3]; ba_ = bao[f % 3]
                    P("act", lambda e: e.activation(out=r_[:], in_=pap, func=AF.Relu), r=[bpp], w=[br_])
                    eng = "pool" if f % 2 == 0 else "dve"
                    P(eng, lambda e: e.tensor_tensor(out=a_[:], in0=r_[:], in1=r_[:], op=ALU.mult), r=[br_], w=[ba_])
                    kb.dma("pool", ACTT[f * 128:(f + 1) * 128, c0:c0 + NTU], a_[:], r=[ba_], w=[b_ACTT])
                evs.append(evac)
                if len(evs) == 2:
                    jobs.append(dict(XT=XT, bXT=bXT, KCn=KC, wcols=w_up[:, (f - 1) * 128:(f + 1) * 128], nf=2, M=128, evacs=evs))
                    evs = []
            g.run_jobs(jobs)

    KF = c.DFF // 128
    for blk in range(TOK // NT):
        c0 = blk * NT
        with kb.phase():
            XD = kb.sb([128, KF, NT], BF16, "XD"); bXD = Buf("XD")
            AV = ACTT.rearrange("(kc p) t -> p kc t", p=128)
            for k0 in range(0, KF, 16):
                kn = min(16, KF - k0)
                kb.dma("sp", XD[:, k0:k0 + kn, :], AV[:, k0:k0 + kn, c0:c0 + NT], r=[b_ACTT], w=[bXD])
            out_gemm_and_epilogue(lambda: Gemm(min(16, KF), NT, nacc=4), XD, bXD, KF, w_dn, g2w, X1, b_X1, c0, out, b_out, c0, NT, nfp=(2 if KC % 2 == 0 else 1))


def make_in_maps(inp, cfg, ncores):
    c = cfg
    f = lambda a: np.ascontiguousarray(np.asarray(a, dtype=np.float32))
    shared = {
        "w_ada": f(inp["w_ada"][0]), "b_ada": f(inp["b_ada"][0]),
        "mix_pre_norm": f(inp["mix_pre_norm"][0]), "mix_post_norm": f(inp["mix_post_norm"][0]),
        "w_in": f(inp["w_in"][0]),
        "lru_conv_w": f(inp["lru_conv_w"][0]), "lru_conv_b": f(inp["lru_conv_b"][0]),
        "lru_gate_a_w": f(inp["lru_gate_a_w"][0]), "lru_gate_a_b": f(inp["lru_gate_a_b"][0]).reshape(-1),
        "lru_gate_i_w": f(inp["lru_gate_i_w"][0]), "lru_gate_i_b": f(inp["lru_gate_i_b"][0]).reshape(-1),
        "lru_lambda": f(inp["lru_lambda"][0]),
        "gdn_conv_w": f(inp["gdn_conv_w"][0]), "gdn_a_log": f(inp["gdn_a_log"][0]).reshape(-1, 1),
        "gdn_dt_bias": f(inp["gdn_dt_bias"][0]).reshape(-1, 1), "gdn_out_norm": f(inp["gdn_out_norm"][0]),
        "w_branch_lru": f(inp["w_branch_lru"][0]), "w_branch_gdn": f(inp["w_branch_gdn"][0]), "w_out": f(inp["w_out"][0]),
        "mlp_pre_norm": f(inp["mlp_pre_norm"][0]), "mlp_post_norm": f(inp["mlp_post_norm"][0]),
        "w_mlp_up": f(inp["w_mlp_up"][0]), "w_mlp_down": f(inp["w_mlp_down"][0]),
    }
    x = np.asarray(inp["x"], dtype=np.float32); cc = np.asarray(inp["c"], dtype=np.float32)
    maps = []
    for i in range(ncores):
        b, half = i // 2, i % 2
        m = dict(shared)
        m["x_own"] = np.ascontiguousarray(x[b, half * c.TOK:(half + 1) * c.TOK])
        m["x_pre"] = np.ascontiguousarray(x[b, 0:c.TOK])
        m["c"] = np.ascontiguousarray(cc[b].reshape(c.KC, 128))
        m["flag"] = np.full((128, 1), float(half), dtype=np.float32)
        maps.append(m)
    return maps


_CACHE = {}


def kernel(**inputs):
    cfg = Cfg(D=4096, T=4096, NT=512, SBK=256, PL=1024, NTW=1024)
    if "nc" not in _CACHE:
        _CACHE["nc"] = build(cfg)
    nc = _CACHE["nc"]
    maps = make_in_maps(inputs, cfg, 8)
    res = run_bass_kernel_spmd(nc, maps, core_ids=list(range(8)))
    outp = np.zeros((4, 4096, 4096), dtype=np.float32)
    for i in range(8):
        b, half = i // 2, i % 2
        outp[b, half * cfg.TOK:(half + 1) * cfg.TOK] = res.results[i]["out"]
    return outp
```
